# Optimizing a Trainium2 kernel written in Bass

```python
import math
import jax, jax.numpy as jnp
from jax import lax
import numpy as np

D_MODEL = 2048
BATCH = 8
SEQ = 4096
DEPTH = 4
DEC_BATCH = 4
DEC_SEQ = 4096
PAST_LEN = 128

N_MIXERS = 2
N_A_LAYERS = (DEPTH + 1) // 2
N_B_LAYERS = DEPTH // 2
EPS = 1e-6
ROPE_THETA = 10000.0
BLOCK = 128
A_HEADS = 16
A_KV_HEADS = 4
A_GROUP = A_HEADS // A_KV_HEADS
A_HEAD_DIM = D_MODEL // A_HEADS
WINDOW = 128
B_HEADS = 16
Q_LORA = 512
KV_LORA = 512
NOPE_DIM = 128
ROPE_DIM = 64
V_DIM = 128
D_FF = 5632
CONV_W = 3

kernel_name = "hybrid_swa_sink_mla_convffn_encoder"


def rmsnorm(x, g):
    xf = x.astype(jnp.float32)
    y = xf * lax.rsqrt(jnp.mean(xf * xf, axis=-1, keepdims=True) + EPS) * g.astype(jnp.float32)
    return y.astype(x.dtype)


def rope_tables(seq, dim):
    pos = jnp.arange(seq, dtype=jnp.float32)
    inv = 1.0 / (ROPE_THETA ** (jnp.arange(0, dim, 2, dtype=jnp.float32) / dim))
    ang = pos[:, None] * inv[None, :]
    return jnp.cos(ang), jnp.sin(ang)


def apply_rope(x, cos, sin):
    shp = (cos.shape[0],) + (1,) * (x.ndim - 3) + (cos.shape[1],)
    c = cos.reshape(shp)
    s = sin.reshape(shp)
    xf = x.astype(jnp.float32)
    x1, x2 = jnp.split(xf, 2, axis=-1)
    out = jnp.concatenate([x1 * c - x2 * s, x2 * c + x1 * s], axis=-1)
    return out.astype(x.dtype)


def window_gqa_sink(x, w_qkv, w_o, sink, cos, sin):
    B, S, _ = x.shape
    nb = S // BLOCK
    qkv = x @ w_qkv
    qd = A_HEADS * A_HEAD_DIM
    kd = A_KV_HEADS * A_HEAD_DIM
    q = qkv[..., :qd].reshape(B, S, A_KV_HEADS, A_GROUP, A_HEAD_DIM)
    k = qkv[..., qd:qd + kd].reshape(B, S, A_KV_HEADS, A_HEAD_DIM)
    v = qkv[..., qd + kd:].reshape(B, S, A_KV_HEADS, A_HEAD_DIM)
    q = apply_rope(q, cos, sin)
    k = apply_rope(k, cos, sin)
    pad = ((0, 0), (WINDOW, WINDOW), (0, 0), (0, 0))
    kp = jnp.pad(k, pad)
    vp = jnp.pad(v, pad)
    qb = jnp.moveaxis(q.reshape(B, nb, BLOCK, A_KV_HEADS, A_GROUP, A_HEAD_DIM), 1, 0)
    span = BLOCK + 2 * WINDOW
    scale = A_HEAD_DIM ** -0.5
    sink_b = sink.astype(jnp.float32).reshape(A_KV_HEADS, A_GROUP)[None, :, :, None, None]
    a_idx = jnp.arange(BLOCK)[:, None]
    c_idx = jnp.arange(span)[None, :]
    rel = c_idx - a_idx

    def block_fn(args):
        n, qn = args
        start = n * BLOCK
        kn = lax.dynamic_slice_in_dim(kp, start, span, axis=1)
        vn = lax.dynamic_slice_in_dim(vp, start, span, axis=1)
        s = jnp.einsum('bqkgd,bckd->bkgqc', qn, kn).astype(jnp.float32) * scale
        j = start + c_idx - WINDOW
        valid = (rel >= 0) & (rel <= 2 * WINDOW) & (j >= 0) & (j < S)
        s = jnp.where(valid[None, None, None], s, -1e30)
        m = jnp.maximum(jnp.max(s, axis=-1, keepdims=True), sink_b)
        p = jnp.exp(s - m)
        denom = jnp.sum(p, axis=-1, keepdims=True) + jnp.exp(sink_b - m)
        return jnp.einsum('bkgqc,bckd->bqkgd', (p / denom).astype(vn.dtype), vn)

    o = lax.map(block_fn, (jnp.arange(nb), qb))
    o = jnp.moveaxis(o, 0, 1).reshape(B, S, A_HEADS * A_HEAD_DIM)
    return o @ w_o


def mla(x, w_in, q_norm, w_q_up, kv_norm, w_kv_up, w_o, cos, sin):
    B, S, _ = x.shape
    nb = S // BLOCK
    h = x @ w_in
    cq = rmsnorm(h[..., :Q_LORA], q_norm)
    ckv = rmsnorm(h[..., Q_LORA:Q_LORA + KV_LORA], kv_norm)
    k_rope = apply_rope(h[..., Q_LORA + KV_LORA:], cos, sin)
    q = (cq @ w_q_up).reshape(B, S, B_HEADS, NOPE_DIM + ROPE_DIM)
    q_nope = q[..., :NOPE_DIM]
    q_rope = apply_rope(q[..., NOPE_DIM:], cos, sin)
    kv = (ckv @ w_kv_up).reshape(B, S, B_HEADS, NOPE_DIM + V_DIM)
    k_nope = kv[..., :NOPE_DIM]
    v = kv[..., NOPE_DIM:]
    scale = (NOPE_DIM + ROPE_DIM) ** -0.5
    qn_b = jnp.moveaxis(q_nope.reshape(B, nb, BLOCK, B_HEADS, NOPE_DIM), 1, 0)
    qr_b = jnp.moveaxis(q_rope.reshape(B, nb, BLOCK, B_HEADS, ROPE_DIM), 1, 0)

    def block_fn(args):
        qn, qr = args
        s = (jnp.einsum('bqhd,bkhd->bhqk', qn, k_nope).astype(jnp.float32)
             + jnp.einsum('bqhr,bkr->bhqk', qr, k_rope).astype(jnp.float32)) * scale
        p = jax.nn.softmax(s, axis=-1)
        return jnp.einsum('bhqk,bkhd->bqhd', p.astype(v.dtype), v)

    o = lax.map(block_fn, (qn_b, qr_b))
    o = jnp.moveaxis(o, 0, 1).reshape(B, S, B_HEADS * V_DIM)
    return o @ w_o


def conv_ffn(x, w_in, conv_w, conv_b, w_out):
    h = x @ w_in
    hp = jnp.pad(h, ((0, 0), (1, 1), (0, 0)))
    h = hp[:, :-2] * conv_w[0] + hp[:, 1:-1] * conv_w[1] + hp[:, 2:] * conv_w[2] + conv_b
    g = h[..., :D_FF]
    u = h[..., D_FF:]
    return (jax.nn.silu(g) * u) @ w_out


def trunk(x, norm_mix, norm_ffn, norm_final, a_w_qkv, a_w_o, a_sink,
          b_w_in, b_q_norm, b_w_q_up, b_kv_norm, b_w_kv_up, b_w_o,
          f_w_in, f_conv_w, f_conv_b, f_w_out):
    S = x.shape[1]
    cos_a, sin_a = rope_tables(S, A_HEAD_DIM)
    cos_b, sin_b = rope_tables(S, ROPE_DIM)
    for i in range(DEPTH):
        h = rmsnorm(x, norm_mix[i])
        j = i // N_MIXERS
        if i % N_MIXERS == 0:
            x = x + window_gqa_sink(h, a_w_qkv[j], a_w_o[j], a_sink[j], cos_a, sin_a)
        else:
            x = x + mla(h, b_w_in[j], b_q_norm[j], b_w_q_up[j], b_kv_norm[j],
                        b_w_kv_up[j], b_w_o[j], cos_b, sin_b)
        x = x + conv_ffn(rmsnorm(x, norm_ffn[i]), f_w_in[i], f_conv_w[i], f_conv_b[i], f_w_out[i])
    return rmsnorm(x, norm_final)


def setup_inputs(seed: int = 0) -> dict:
    key = jax.random.key(seed)
    ks = jax.random.split(key, 20)
    f32 = jnp.float32

    def w(k, shape, fan_in):
        return jax.random.normal(k, shape, f32) * (fan_in ** -0.5)

    def gain(k, shape):
        return 1.0 + 0.05 * jax.random.normal(k, shape, f32)

    qkv_out = (A_HEADS + 2 * A_KV_HEADS) * A_HEAD_DIM
    return {
        "x_prompt": jax.random.normal(ks[0], (BATCH, SEQ, D_MODEL), f32),
        "x_sample": jax.random.normal(ks[1], (DEC_BATCH, DEC_SEQ, D_MODEL), f32),
        "norm_mix": gain(ks[2], (DEPTH, D_MODEL)),
        "norm_ffn": gain(ks[3], (DEPTH, D_MODEL)),
        "norm_final": gain(ks[4], (D_MODEL,)),
        "a_w_qkv": w(ks[5], (N_A_LAYERS, D_MODEL, qkv_out), D_MODEL),
        "a_w_o": w(ks[6], (N_A_LAYERS, A_HEADS * A_HEAD_DIM, D_MODEL), A_HEADS * A_HEAD_DIM),
        "a_sink": 0.5 * jax.random.normal(ks[7], (N_A_LAYERS, A_HEADS), f32),
        "b_w_in": w(ks[8], (N_B_LAYERS, D_MODEL, Q_LORA + KV_LORA + ROPE_DIM), D_MODEL),
        "b_q_norm": gain(ks[9], (N_B_LAYERS, Q_LORA)),
        "b_w_q_up": w(ks[10], (N_B_LAYERS, Q_LORA, B_HEADS * (NOPE_DIM + ROPE_DIM)), Q_LORA),
        "b_kv_norm": gain(ks[11], (N_B_LAYERS, KV_LORA)),
        "b_w_kv_up": w(ks[12], (N_B_LAYERS, KV_LORA, B_HEADS * (NOPE_DIM + V_DIM)), KV_LORA),
        "b_w_o": w(ks[13], (N_B_LAYERS, B_HEADS * V_DIM, D_MODEL), B_HEADS * V_DIM),
        "f_w_in": w(ks[14], (DEPTH, D_MODEL, 2 * D_FF), D_MODEL),
        "f_conv_w": w(ks[15], (DEPTH, CONV_W, 2 * D_FF), CONV_W),
        "f_conv_b": 0.01 * jax.random.normal(ks[16], (DEPTH, 2 * D_FF), f32),
        "f_w_out": w(ks[17], (DEPTH, D_FF, D_MODEL), D_FF),
    }


def reference(x_prompt, x_sample, norm_mix, norm_ffn, norm_final, a_w_qkv, a_w_o, a_sink,
              b_w_in, b_q_norm, b_w_q_up, b_kv_norm, b_w_kv_up, b_w_o,
              f_w_in, f_conv_w, f_conv_b, f_w_out):
    y_prompt = trunk(x_prompt, norm_mix, norm_ffn, norm_final, a_w_qkv, a_w_o, a_sink,
                     b_w_in, b_q_norm, b_w_q_up, b_kv_norm, b_w_kv_up, b_w_o,
                     f_w_in, f_conv_w, f_conv_b, f_w_out)
    y_sample = trunk(x_sample, norm_mix, norm_ffn, norm_final, a_w_qkv, a_w_o, a_sink,
                     b_w_in, b_q_norm, b_w_q_up, b_kv_norm, b_w_kv_up, b_w_o,
                     f_w_in, f_conv_w, f_conv_b, f_w_out)
    return (y_prompt, y_sample)
```

```python
from contextlib import ExitStack
import numpy as np
import ml_dtypes
import concourse.bass as bass
import concourse.mybir as mybir
from concourse.bass_utils import run_bass_kernel_spmd

F32 = mybir.dt.float32
BF16 = mybir.dt.bfloat16
AF = mybir.ActivationFunctionType
ALU = mybir.AluOpType
DBG = set()
PAD = 128
TB = 512
EPS = 1e-6
THETA = 10000.0


class Cfg:
    def __init__(self, D=2048, S=4096, NSEQ=2, DEPTH=4, A_KV=4, B_HEADS=16, QL=512, KVL=512, DFF=5632):
        self.D, self.S, self.NSEQ, self.DEPTH = D, S, NSEQ, DEPTH
        self.A_HEADS = D // 128
        self.A_KV = A_KV
        self.G = self.A_HEADS // A_KV
        self.B_HEADS, self.QL, self.KVL, self.DFF = B_HEADS, QL, KVL, DFF
        self.NDC = D // 128
        self.NFC = DFF // 128
        self.LA = (DEPTH + 1) // 2
        self.LB = DEPTH // 2
        self.SP = S + 2 * PAD
        self.NB = S // TB


def _tiles(W, mt):
    din, dout = W.shape
    kc, nm = din // 128, dout // mt
    return np.ascontiguousarray(W.reshape(kc, 128, nm, mt).transpose(2, 1, 0, 3))


class WLayout:
    def __init__(self, cfg):
        c = cfg
        self.off = {}
        self.NG = max(1, c.DEPTH)
        self.total = [0] * self.NG
        kd, kq, kv, kf = c.NDC, c.QL // 128, c.KVL // 128, c.NFC
        for j in range(c.LA):
            g = 2 * j
            self._add(g, ("a_qk", j), c.A_HEADS + c.A_KV, kd, 128)
            self._add(g, ("a_v", j), 1, kd, c.A_KV * 128)
            self._add(g, ("a_o", j), c.NDC, c.A_HEADS, 128)
        for j in range(c.LB):
            g = 2 * j + 1
            self._add(g, ("b_in", j), (c.QL + c.KVL) // 128, kd, 128)
            self._add(g, ("b_inr", j), 1, kd, 64)
            self._add(g, ("b_qn", j), c.B_HEADS, kq, 128)
            self._add(g, ("b_qr", j), c.B_HEADS, kq, 64)
            self._add(g, ("b_kn", j), c.B_HEADS, kv, 128)
            self._add(g, ("b_v", j), c.B_HEADS, kv, 128)
            self._add(g, ("b_o", j), c.NDC, c.B_HEADS, 128)
        for l in range(c.DEPTH):
            self._add(l, ("f_in", l), 2 * c.NFC, kd, 128)
            self._add(l, ("f_out", l), c.NDC, kf, 128)
        self.CW = 2048
        self.rows = [max(8, -(-(-(-t // self.CW)) // 8) * 8) for t in self.total]

    def _add(self, g, key, nt, kc, mt):
        self.off[key] = (g, self.total[g], nt, kc, mt)
        self.total[g] += nt * 128 * kc * mt

    def tile(self, key, t):
        g, off, nt, kc, mt = self.off[key]
        return g, off + t * 128 * kc * mt, kc, mt


def pack_weights(cfg, wl, inp):
    c = cfg
    flats = [np.zeros(r * wl.CW, np.float32) for r in wl.rows]

    def put(key, arr):
        g, off, nt, kc, mt = wl.off[key]
        assert arr.shape == (nt, 128, kc, mt), (key, arr.shape, (nt, 128, kc, mt))
        flats[g][off:off + arr.size] = arr.reshape(-1)

    qd, kd = c.A_HEADS * 128, c.A_KV * 128
    for j in range(c.LA):
        w = np.asarray(inp["a_w_qkv"][j])
        put(("a_qk", j), _tiles(w[:, :qd + kd], 128))
        put(("a_v", j), _tiles(w[:, qd + kd:], kd))
        put(("a_o", j), _tiles(np.asarray(inp["a_w_o"][j]), 128))
    for j in range(c.LB):
        w = np.asarray(inp["b_w_in"][j])
        put(("b_in", j), _tiles(w[:, :c.QL + c.KVL], 128))
        put(("b_inr", j), _tiles(w[:, c.QL + c.KVL:], 64))
        w = np.asarray(inp["b_w_q_up"][j]).reshape(c.QL, c.B_HEADS, 192)
        put(("b_qn", j), _tiles(np.ascontiguousarray(w[:, :, :128]).reshape(c.QL, -1), 128))
        put(("b_qr", j), _tiles(np.ascontiguousarray(w[:, :, 128:]).reshape(c.QL, -1), 64))
        w = np.asarray(inp["b_w_kv_up"][j]).reshape(c.KVL, c.B_HEADS, 256)
        put(("b_kn", j), _tiles(np.ascontiguousarray(w[:, :, :128]).reshape(c.KVL, -1), 128))
        put(("b_v", j), _tiles(np.ascontiguousarray(w[:, :, 128:]).reshape(c.KVL, -1), 128))
        put(("b_o", j), _tiles(np.asarray(inp["b_w_o"][j]), 128))
    for l in range(c.DEPTH):
        w = np.asarray(inp["f_w_in"][l])
        t = _tiles(w, 128)
        inter = np.empty_like(t)
        inter[0::2] = t[:c.NFC]
        inter[1::2] = t[c.NFC:]
        put(("f_in", l), inter)
        put(("f_out", l), _tiles(np.asarray(inp["f_w_out"][l]), 128))
    return [f.reshape(r, wl.CW) for f, r in zip(flats, wl.rows)]


class CLayout:
    def __init__(self, cfg):
        c = cfg
        self.off = {}
        n = 0
        for key, w in ([(("gmix", l), c.NDC) for l in range(c.DEPTH)] + [(("gffn", l), c.NDC) for l in range(c.DEPTH)]
                       + [(("gfin", 0), c.NDC)]
                       + [(("bqn", j), c.QL // 128) for j in range(c.LB)] + [(("bkvn", j), c.KVL // 128) for j in range(c.LB)]
                       + [(("cw", l, k), 2 * c.NFC) for l in range(c.DEPTH) for k in range(3)]
                       + [(("cb", l), 2 * c.NFC) for l in range(c.DEPTH)]
                       + [(("sink", j), c.A_HEADS) for j in range(c.LA)]):
            self.off[key] = n
            n += w
        self.n = n


def pack_consts(cfg, cl, inp):
    c = cfg
    out = np.zeros((128, cl.n), np.float32)

    def colmajor(v):
        v = np.asarray(v, np.float32)
        return v.reshape(-1, 128).T

    def inter(v):
        t = colmajor(v)
        o = np.empty_like(t)
        o[:, 0::2] = t[:, :c.NFC]
        o[:, 1::2] = t[:, c.NFC:]
        return o

    for l in range(c.DEPTH):
        out[:, cl.off[("gmix", l)]:][:, :c.NDC] = colmajor(inp["norm_mix"][l])
        out[:, cl.off[("gffn", l)]:][:, :c.NDC] = colmajor(inp["norm_ffn"][l])
        for k in range(3):
            out[:, cl.off[("cw", l, k)]:][:, :2 * c.NFC] = inter(inp["f_conv_w"][l][k])
        out[:, cl.off[("cb", l)]:][:, :2 * c.NFC] = inter(inp["f_conv_b"][l])
    out[:, cl.off[("gfin", 0)]:][:, :c.NDC] = colmajor(inp["norm_final"])
    for j in range(c.LB):
        out[:, cl.off[("bqn", j)]:][:, :c.QL // 128] = colmajor(inp["b_q_norm"][j])
        out[:, cl.off[("bkvn", j)]:][:, :c.KVL // 128] = colmajor(inp["b_kv_norm"][j])
    for j in range(c.LA):
        out[:, cl.off[("sink", j)]:][:, :c.A_HEADS] = np.broadcast_to(np.asarray(inp["a_sink"][j], np.float32)[None, :], (128, c.A_HEADS))
    return out


def make_tables(cfg):
    c = cfg
    f32 = np.float32

    def tab(dim, pos):
        inv = (f32(1.0) / (f32(THETA) ** (np.arange(0, dim, 2, dtype=f32) / f32(dim)))).astype(f32)
        ang = (pos.astype(f32)[None, :] * inv[:, None]).astype(f32)
        cs, sn = np.cos(ang).astype(f32), np.sin(ang).astype(f32)
        return np.stack([np.concatenate([cs, cs], 0), np.concatenate([sn, -sn], 0)], 1)

    posA = np.arange(-PAD, c.S + PAD)
    tabA = np.ascontiguousarray(tab(128, posA))
    tabB = np.ascontiguousarray(tab(64, np.arange(c.S)))
    G = c.G
    cbf = np.zeros((128, 128 + 64 + 2 * G * 128 + 128), np.float32)
    cbf[:, 192 + 2 * G * 128:] = np.eye(128, dtype=np.float32)
    for m in range(128):
        cbf[(m + 64) % 128, m] = 1.0
    for m in range(64):
        cbf[(m + 32) % 64, 128 + m] = 1.0
    b = np.arange(128)[:, None]
    a = np.arange(128)[None, :]
    mL = (a <= b).astype(np.float32)
    mR = (b <= a).astype(np.float32)
    cbf[:, 192:192 + G * 128] = np.tile(mL, (1, G))
    cbf[:, 192 + G * 128:192 + 2 * G * 128] = np.tile(mR, (1, G))
    return tabA, tabB, cbf.astype(ml_dtypes.bfloat16), np.eye(128, dtype=np.float32)


class Sched:
    CE = ("pe", "act", "dve", "pool")

    def __init__(self, nc, stack):
        self.nc = nc
        self.stack = stack
        self.streams = {e: [] for e in self.CE + ("sp",)}
        self.sem = {e: stack.enter_context(nc.semaphore("tl_" + e)) for e in self.CE}
        self.cnt = {e: 0 for e in self.CE}
        self.dsem = {}
        self.waited = {e: {} for e in self.streams}
        self.lastw = {}
        self.readers = {}
        self.out_tokens = []

    def _wait(self, o, d):
        if d is None or d.get("tok") is None:
            return
        if d["eng"] == "pe" and o["eng"] == "pe" and not d.get("dma") and not o.get("dma"):
            return
        name, sem, v = d["tok"]
        w = self.waited[o["eng"]]
        if w.get(name, 0) >= v:
            return
        w[name] = v
        o["waits"].append((sem, v))

    def _track(self, o, reads, writes):
        deps = []
        for r in reads:
            if r in self.lastw:
                deps.append(self.lastw[r])
        for r in writes:
            if r in self.lastw:
                deps.append(self.lastw[r])
            deps.extend(self.readers.get(r, ()))
        for d in deps:
            if d is not o:
                self._wait(o, d)
        for r in reads:
            self.readers.setdefault(r, []).append(o)
        for r in writes:
            self.lastw[r] = o
            self.readers[r] = []

    def op(self, eng, fn, reads=(), writes=(), signal=True):
        o = {"eng": eng, "fn": fn, "waits": [], "tok": None, "inc": None}
        self._track(o, reads, writes)
        if signal:
            self.cnt[eng] += 1
            o["tok"] = ("tl_" + eng, self.sem[eng], self.cnt[eng])
            o["inc"] = (self.sem[eng], 1)
        else:
            assert eng == "pe"
            o["tok"] = ("tl_" + eng, self.sem[eng], self.cnt[eng] + 1)
        self.streams[eng].append(o)
        return o

    def dma(self, q, key, fn, reads=(), writes=(), is_out=False):
        if key not in self.dsem:
            self.dsem[key] = [self.stack.enter_context(self.nc.semaphore("d_" + key)), 0]
        ent = self.dsem[key]
        o = {"eng": q, "fn": fn, "waits": [], "dma": True}
        self._track(o, reads, writes)
        ent[1] += 16
        o["tok"] = ("d_" + key, ent[0], ent[1])
        o["inc"] = (ent[0], 16)
        self.streams[q].append(o)
        return o

    def barrier(self):
        toks = [("tl_" + e, self.sem[e], self.cnt[e]) for e in self.CE if self.cnt[e] > 0]
        toks += [("d_" + k, v[0], v[1]) for k, v in self.dsem.items() if v[1] > 0 and not k.startswith("cast")]
        for e in self.streams:
            o = {"eng": e, "fn": None, "waits": [], "tok": None, "inc": None}
            w = self.waited[e]
            for name, sem, v in toks:
                if name == "tl_" + e:
                    continue
                if w.get(name, 0) < v:
                    w[name] = v
                    o["waits"].append((sem, v))
            self.streams[e].append(o)
        self.lastw = {k: v for k, v in self.lastw.items() if isinstance(k, tuple) and k[0] == "wbf"}
        self.readers = {}

    def simulate(self):
        val = {}
        ptr = {e: 0 for e in self.streams}
        prog = True
        while prog:
            prog = False
            for e, ops in self.streams.items():
                while ptr[e] < len(ops):
                    o = ops[ptr[e]]
                    if any(val.get(id(sem), 0) < v for sem, v in o["waits"]):
                        break
                    if o.get("inc") is not None and o["fn"] is not None:
                        val[id(o["inc"][0])] = val.get(id(o["inc"][0]), 0) + o["inc"][1]
                    ptr[e] += 1
                    prog = True
        stuck = {e: (ptr[e], len(ops)) for e, ops in self.streams.items() if ptr[e] < len(ops)}
        return stuck

    def emit(self, block):
        nc = self.nc
        engmap = {"pe": "tensor", "act": "scalar", "dve": "vector", "pool": "gpsimd", "sp": "sync"}
        for e, ops in self.streams.items():
            def body(eng, ops=ops):
                for o in ops:
                    for sem, v in o["waits"]:
                        eng.wait_ge(sem, v)
                    if o["fn"] is None:
                        continue
                    ins = o["fn"](eng)
                    if o["inc"] is not None:
                        ins.then_inc(o["inc"][0], o["inc"][1])
            getattr(block, engmap[e])(body)


def build(cfg):
    c = cfg
    wl, cl = WLayout(c), CLayout(c)
    D, S, NSEQ, NDC, SPW, NB, G = c.D, c.S, c.NSEQ, c.NDC, c.SP, c.NB, c.G
    nc = bass.Bass("TRN2", target_bir_lowering=False)
    x_in = nc.dram_tensor("x_in", [NSEQ * S, D], F32, kind="ExternalInput").ap()
    wall = [nc.dram_tensor("wall%d" % g, [wl.rows[g], wl.CW], F32, kind="ExternalInput").ap() for g in range(wl.NG)]
    cst_d = nc.dram_tensor("consts", [128, cl.n], F32, kind="ExternalInput").ap()
    tabA_d = nc.dram_tensor("tabA", [128, 2, SPW], F32, kind="ExternalInput").ap()
    tabB_d = nc.dram_tensor("tabB", [64, 2, S], F32, kind="ExternalInput").ap()
    NCB = 192 + 2 * G * 128 + 128
    cbf_d = nc.dram_tensor("cbf", [128, NCB], BF16, kind="ExternalInput").ap()
    id_d = nc.dram_tensor("ident", [128, 128], F32, kind="ExternalInput").ap()
    y_out = nc.dram_tensor("y", [NSEQ * S, D], F32, kind="ExternalOutput").ap()
    wbf = [nc.dram_tensor("wbf%d" % g, [wl.rows[g], wl.CW], BF16, kind="Internal").ap() for g in range(wl.NG)]
    xs = [nc.dram_tensor("xs%d" % i, [NSEQ, D, SPW], F32, kind="Internal").ap() for i in range(2)]
    DO = c.B_HEADS * 128
    osc = nc.dram_tensor("osc", [NSEQ, DO, S], BF16, kind="Internal").ap()
    wflat = [w_.rearrange("r c -> (r c)") for w_ in wbf]

    def xview(buf, seq, c0, w):
        return xs[buf][seq].rearrange("(k p) c -> p k c", p=128)[:, :, c0:c0 + w]

    uid = [0]

    def SB(stack, name, shape, dt):
        uid[0] += 1
        return stack.enter_context(nc.sbuf_tensor("%s_%d" % (name, uid[0]), shape, dt))

    with ExitStack() as top:
        sch = Sched(nc, top)
        cst = top.enter_context(nc.sbuf_tensor("cst", [128, cl.n], F32))
        cbf = top.enter_context(nc.sbuf_tensor("cbf_s", [128, NCB], BF16))
        ones = top.enter_context(nc.sbuf_tensor("ones", [128, 128], BF16))
        epsb = top.enter_context(nc.sbuf_tensor("epsb", [128, 1], F32))
        sinkexp = top.enter_context(nc.sbuf_tensor("sinkexp", [128, max(1, c.LA) * c.A_HEADS], F32))
        pp = [top.enter_context(nc.psum_tensor("pp%d" % i, [128, 2, 512], F32)) for i in range(4)]
        RA = cbf[:, 0:128]
        RB = cbf[0:64, 128:192]
        maskL = cbf[:, 192:192 + G * 128]
        maskR = cbf[:, 192 + G * 128:192 + 2 * G * 128]
        identb = cbf[:, 192 + 2 * G * 128:192 + 2 * G * 128 + 128]

        def bank(k):
            return pp[k // 2][:, k % 2, :]

        def cc(key, i=0, n=1):
            o = cl.off[key]
            return cst[:, o + i:o + i + n]

        sch.dma("sp", "cst", lambda e: e.dma_start(out=cst[:], in_=cst_d[:, :]), writes=["cst"])
        sch.dma("sp", "cbf", lambda e: e.dma_start(out=cbf[:], in_=cbf_d[:, :]), writes=["cbf"])
        sch.op("dve", lambda e: e.memset(ones[:], 1.0), writes=["ones"])
        sch.op("dve", lambda e: e.memset(epsb[:], EPS), writes=["epsb"])
        if c.LA > 0:
            o0 = cl.off[("sink", 0)]
            sch.op("act", lambda e: e.activation(out=sinkexp[:], in_=cst[:, o0:o0 + c.LA * c.A_HEADS], func=AF.Exp),
                   reads=["cst"], writes=["sinkexp"])

        class WPool:
            def __init__(self, st, name, n, kc, mt):
                self.name, self.n, self.i = name, n, 0
                self.tiles = [SB(st, "%s%d" % (name, i), [128, kc, mt], BF16) for i in range(n)]

            def load(self, key, t):
                g_, off, kc, mt = wl.tile(key, t)
                i = self.i % self.n
                self.i += 1
                tl = self.tiles[i]
                src = wflat[g_][off:off + 128 * kc * mt].rearrange("(p k m) -> p k m", p=128, k=kc)
                res = (self.name, i)
                sch.dma("sp", "%s%d" % (self.name, i), lambda e: e.dma_start(out=tl[:, 0:kc, 0:mt], in_=src), reads=[("wbf", g_)], writes=[res])
                return tl, res

        def rr(n):
            st = {"i": -1}

            def nxt():
                st["i"] = (st["i"] + 1) % n
                return st["i"]
            return nxt

        def norm_block(st_tiles, buf, seq, c0, W, gkey, hT, hres, stat_banks):
            xt, sq, rstd = st_tiles
            nsub = 2
            w = W // nsub
            assert w * nsub == W and w <= 384
            for sc in range(nsub):
                src = xview(buf, seq, c0 + sc * w, w)
                blocks = sorted(set([min(max((c0 + sc * w - PAD) // TB, 0), NB - 1), min(max((c0 + sc * w + w - 1 - PAD) // TB, 0), NB - 1)]))
                sch.dma("sp", "xt", lambda e, src=src: e.dma_start(out=xt[:, :, 0:w], in_=src),
                        reads=[("x", buf, seq, b) for b in blocks], writes=["xt"])
                sb = stat_banks[sc % len(stat_banks)]
                for dc in range(NDC):
                    sqt = sq[dc % 2]
                    sch.op("act", lambda e, dc=dc, sqt=sqt: e.activation(out=sqt[:, 0:w], in_=xt[:, dc, 0:w], func=AF.Square),
                           reads=["xt"], writes=[("sq", dc % 2)])
                    sch.op("pe", lambda e, dc=dc, sqt=sqt, sb=sb: e.matmul(bank(sb)[:, 0:w], lhsT=ones[:], rhs=sqt[:, 0:w], start=(dc == 0), stop=(dc == NDC - 1)),
                           reads=[("sq", dc % 2), "ones"], writes=[("ps", sb)], signal=True)
                sch.op("act", lambda e, sb=sb: e.activation(out=rstd[:, 0:w], in_=bank(sb)[:, 0:w], func=AF.Sqrt, bias=epsb[:, 0:1], scale=1.0 / D),
                       reads=[("ps", sb), "epsb"], writes=["rstd"])
                sch.op("dve", lambda e: e.reciprocal(out=rstd[:, 0:w], in_=rstd[:, 0:w]),
                       reads=["rstd"], writes=["rstd"])
                for dc in range(NDC):
                    sch.op("dve", lambda e, dc=dc, sc=sc: e.scalar_tensor_tensor(out=hT[:, dc, sc * w:(sc + 1) * w], in0=xt[:, dc, 0:w], scalar=cc(gkey, dc),
                                                                                 in1=rstd[:, 0:w], op0=ALU.mult, op1=ALU.mult),
                           reads=["xt", "rstd", "cst"], writes=[hres])

        def rope(ps_k, rot_k, w, R, npart, tab, tcol, qb, t1, t2, out_ap, out_res, rc):
            i = rc()
            qbt, t1t, t2t = qb[i], t1[i], t2[i]
            P = slice(0, npart)
            sch.op("act", lambda e: e.activation(out=qbt[P, 0:w], in_=bank(ps_k)[P, 0:w], func=AF.Copy),
                   reads=[("ps", ps_k)], writes=[("qb", i)])
            sch.op("pe", lambda e: e.matmul(bank(rot_k)[P, 0:w], lhsT=R, rhs=qbt[P, 0:w], start=True, stop=True),
                   reads=[("qb", i), "cbf"], writes=[("ps", rot_k)])
            sch.op("dve", lambda e: e.tensor_tensor(out=t1t[P, 0:w], in0=bank(ps_k)[P, 0:w], in1=tab[P, 0, tcol:tcol + w], op=ALU.mult),
                   reads=[("ps", ps_k), "tab"], writes=[("t1", i)])
            sch.op("dve", lambda e: e.tensor_tensor(out=t2t[P, 0:w], in0=bank(rot_k)[P, 0:w], in1=tab[P, 1, tcol:tcol + w], op=ALU.mult),
                   reads=[("ps", rot_k), "tab"], writes=[("t2", i)])
            sch.op("pool", lambda e: e.tensor_tensor(out=out_ap, in0=t1t[P, 0:w], in1=t2t[P, 0:w], op=ALU.add),
                   reads=[("t1", i), ("t2", i)], writes=[out_res])

        def out_proj(wp, wkey, nkc, act_tile, act_res, xin_buf, xout_buf, seq, blk, xres, xn, bankrr):
            c0 = PAD + blk * TB
            for m in range(NDC):
                wt, wres = wp.load(wkey, m)
                k = bankrr()
                i = m % 2
                srcx = xs[xin_buf][seq][m * 128:(m + 1) * 128, c0:c0 + TB]
                sch.dma("sp", "xres%d" % i, lambda e, i=i, srcx=srcx: e.dma_start(out=xres[i][:], in_=srcx),
                        reads=[("x", xin_buf, seq, blk)], writes=[("xres", i)])
                for kc in range(nkc):
                    sch.op("pe", lambda e, kc=kc, wt=wt, k=k: e.matmul(bank(k), lhsT=wt[:, kc, 0:128], rhs=act_tile[:, kc, :], start=(kc == 0), stop=(kc == nkc - 1)),
                           reads=[wres, act_res], writes=[("ps", k)], signal=(kc == nkc - 1))
                sch.op("dve", lambda e, i=i, k=k: e.tensor_tensor(out=xn[i][:], in0=bank(k), in1=xres[i][:], op=ALU.add),
                       reads=[("ps", k), ("xres", i)], writes=[("xn", i)])
                dst = xs[xout_buf][seq][m * 128:(m + 1) * 128, c0:c0 + TB]
                sch.dma("pool", "xn%d" % i, lambda e, i=i, dst=dst: e.dma_start(out=dst, in_=xn[i][:]),
                        reads=[("xn", i)], writes=[("x", xout_buf, seq, blk)])

        with ExitStack() as ph:
            xtok = SB(ph, "xtok", [128, 4, D], F32)
            xst = ph.enter_context(nc.sbuf_tensor("xst", [128, NDC, TB], F32))
            zt = ph.enter_context(nc.sbuf_tensor("zt", [128, NDC, PAD], F32))
            idt = ph.enter_context(nc.sbuf_tensor("idt", [128, 128], F32))
            sch.dma("sp", "idt", lambda e: e.dma_start(out=idt[:], in_=id_d[:, :]), writes=["idt"])
            def cast_group(g_):
                r0 = 0
                while r0 < wl.rows[g_]:
                    r1 = min(wl.rows[g_], r0 + 1024)
                    last = r1 >= wl.rows[g_]
                    sch.dma("pool", "cast%d" % g_, lambda e, r0=r0, r1=r1, g_=g_: e.dma_start(out=wbf[g_][r0:r1, :], in_=wall[g_][r0:r1, :]),
                            writes=([("wbf", g_)] if last else []))
                    r0 = r1
            cast_group(0)
            sch.op("dve", lambda e: e.memset(zt[:], 0.0), writes=["zt"])
            for b in range(2):
                for seq in range(NSEQ):
                    for side in range(2):
                        dst = xview(b, seq, 0 if side == 0 else PAD + S, PAD)
                        sch.dma("pool", "zpad", lambda e, dst=dst: e.dma_start(out=dst, in_=zt[:]), reads=["zt"])
            brr = rr(8)
            for seq in range(NSEQ):
                for blk in range(NB):
                    rbase = seq * S + blk * TB
                    src = x_in[rbase:rbase + TB, :].rearrange("(t p) d -> p t d", p=128)
                    sch.dma("sp", "xtok", lambda e, src=src: e.dma_start(out=xtok[:], in_=src), writes=["xtok"])
                    for dc in range(NDC):
                        k = brr()
                        for tt in range(4):
                            sch.op("pe", lambda e, k=k, tt=tt, dc=dc: e.transpose(bank(k)[:, tt * 128:(tt + 1) * 128], xtok[:, tt, dc * 128:(dc + 1) * 128], idt[:]),
                                   reads=["xtok", "idt"], writes=[("ps", k)], signal=(tt == 3))
                        if dc % 2 == 0:
                            sch.op("act", lambda e, k=k, dc=dc: e.activation(out=xst[:, dc, :], in_=bank(k), func=AF.Copy),
                                   reads=[("ps", k)], writes=[("xst", dc)])
                        else:
                            sch.op("dve", lambda e, k=k, dc=dc: e.tensor_copy(out=xst[:, dc, :], in_=bank(k)),
                                   reads=[("ps", k)], writes=[("xst", dc)])
                    dst = xview(0, seq, PAD + blk * TB, TB)
                    sch.dma("pool", "xst", lambda e, dst=dst: e.dma_start(out=dst, in_=xst[:]),
                            reads=[("xst", dc) for dc in range(NDC)], writes=[("x", 0, seq, blk)])
            for g_ in range(1, wl.NG):
                cast_group(g_)
            sch.barrier()

        def phase_A(l, ja, seq, bin_, bout):
            W = TB + 2 * PAD
            scale = 128.0 ** -0.5
            KV = c.A_KV
            GW = G * 128
            with ExitStack() as ph:
                def T(name, shape, dt):
                    return SB(ph, name, shape, dt)
                xt = T("a_xt", [128, NDC, W // 2], F32)
                sq = [T("a_sq%d" % i, [128, W // 2], BF16) for i in range(2)]
                rstd = T("a_rstd", [128, W // 2], F32)
                hT = T("a_hT", [128, NDC, W], BF16)
                tab = T("a_tab", [128, 2, W], F32)
                qT = T("a_qT", [128, c.A_HEADS, TB], BF16)
                kT = T("a_kT", [128, KV, W], BF16)
                V = T("a_V", [128, W // 128, KV * 128], BF16)
                qb = [T("a_qb%d" % i, [128, 384], BF16) for i in range(4)]
                qb2 = [T("a_qc%d" % i, [128, 384], BF16) for i in range(4)]
                t1 = t2 = None
                PT = [T("a_PT%d" % i, [128, TB], BF16) for i in range(6)]
                den = [T("a_den%d" % i, [128, TB], F32) for i in range(2)]
                oT = T("a_oT", [128, c.A_HEADS, TB], BF16)
                xres = [T("a_xres%d" % i, [128, TB], F32) for i in range(2)]
                xn = [T("a_xn%d" % i, [128, TB], F32) for i in range(2)]
                wp = WPool(ph, "a_w", 3, NDC, 128)
                wv = T("a_wv", [128, NDC, KV * 128], BF16)
                vg, voff, vkc, vmt = wl.tile(("a_v", ja), 0)
                sch.dma("sp", "a_wv", lambda e: e.dma_start(out=wv[:], in_=wflat[vg][voff:voff + 128 * vkc * vmt].rearrange("(p k m) -> p k m", p=128, k=vkc)),
                        reads=[("wbf", vg)], writes=["wv"])
                brr = rr(8)
                rc = rr(4)
                ptr = rr(6)
                dr = rr(2)
                for blk in range(NB):
                    c0 = blk * TB
                    sch.dma("sp", "a_tab", lambda e, c0=c0: e.dma_start(out=tab[:], in_=tabA_d[:, :, c0:c0 + W]), writes=["tab"])
                    norm_block((xt, sq, rstd), bin_, seq, c0, W, ("gmix", l), hT, "hT", [brr(), brr()])
                    pend_rope = []
                    for t in range(0 if "A_noqk" not in DBG else 0, (c.A_HEADS + KV) if "A_noqk" not in DBG else 0):
                        wt, wres = wp.load(("a_qk", ja), t)
                        isq = t < c.A_HEADS
                        subs = [(PAD, TB)] if isq else [(0, W // 2), (W // 2, W // 2)]
                        for (s0, sw) in subs:
                            k = brr()
                            for kc in range(NDC):
                                sch.op("pe", lambda e, kc=kc, wt=wt, k=k, s0=s0, sw=sw: e.matmul(bank(k)[:, 0:sw], lhsT=wt[:, kc, :], rhs=hT[:, kc, s0:s0 + sw], start=(kc == 0), stop=(kc == NDC - 1)),
                                       reads=[wres, "hT"], writes=[("ps", k)], signal=(kc == NDC - 1))
                            for f in pend_rope:
                                f(brr())
                            pend_rope = []
                            if isq:
                                for hh in range(2):
                                    o_ap = qT[:, t, hh * 256:(hh + 1) * 256]
                                    pend_rope.append(rope_a(k, hh * 256, 256, RA, 128, tab, s0 + hh * 256, qb, qb2, o_ap, "qT", rc))
                            else:
                                o_ap = kT[:, t - c.A_HEADS, s0:s0 + sw]
                                pend_rope.append(rope_a(k, 0, sw, RA, 128, tab, s0, qb, qb2, o_ap, "kT", rc))
                    for tt in range(W // 128 if "A_nov" not in DBG else 0):
                        k = brr()
                        if tt == 1:
                            for f in pend_rope:
                                f(brr())
                            pend_rope = []
                        for kc in range(NDC):
                            sch.op("pe", lambda e, kc=kc, k=k, tt=tt: e.matmul(bank(k)[:, 0:KV * 128], lhsT=hT[:, kc, tt * 128:(tt + 1) * 128], rhs=wv[:, kc, :], start=(kc == 0), stop=(kc == NDC - 1)),
                                   reads=["hT", "wv"], writes=[("ps", k)], signal=(kc == NDC - 1))
                        sch.op("act", lambda e, k=k, tt=tt: e.activation(out=V[:, tt, :], in_=bank(k)[:, 0:KV * 128], func=AF.Copy),
                               reads=[("ps", k)], writes=["V"])
                    def att_scores(kh, n, blk=blk):
                        nbg = blk * 4 + n
                        js = [j for j in (0, 1, 2) if 0 <= nbg - 1 + j < S // 128]
                        pts = []
                        for j in js:
                            kt = n + j
                            k = brr()
                            sch.op("pe", lambda e, k=k, kt=kt: e.matmul(bank(k)[:, 0:GW].rearrange("p (g q) -> p g q", g=G), lhsT=kT[:, kh, kt * 128:(kt + 1) * 128],
                                                                     rhs=qT[:, kh * G:(kh + 1) * G, n * 128:(n + 1) * 128], start=True, stop=True),
                                   reads=["kT", "qT"], writes=[("ps", k)])
                            pi = ptr()
                            sch.op("act", lambda e, k=k, pi=pi: e.activation(out=PT[pi][:, 0:GW], in_=bank(k)[:, 0:GW], func=AF.Exp, scale=scale),
                                   reads=[("ps", k)], writes=[("PT", pi)])
                            if j != 1:
                                msk = maskL if j == 0 else maskR
                                sch.op("pool", lambda e, pi=pi, msk=msk: e.tensor_tensor(out=PT[pi][:, 0:GW], in0=PT[pi][:, 0:GW], in1=msk, op=ALU.mult),
                                       reads=[("PT", pi), "cbf"], writes=[("PT", pi)])
                            pts.append((pi, kt))
                        return (kh, n, pts)

                    def att_pv(st):
                        kh, n, pts = st
                        ko, kd = brr(), brr()
                        L = len(pts)
                        for idx, (pi, kt) in enumerate(pts):
                            sch.op("pe", lambda e, pi=pi, kt=kt, idx=idx: e.matmul(bank(ko)[:, 0:GW], lhsT=V[:, kt, kh * 128:(kh + 1) * 128], rhs=PT[pi][:, 0:GW], start=(idx == 0), stop=(idx == L - 1)),
                                   reads=["V", ("PT", pi)], writes=[("ps", ko)], signal=(idx == L - 1))
                        for idx, (pi, kt) in enumerate(pts):
                            sch.op("pe", lambda e, pi=pi, idx=idx: e.matmul(bank(kd)[:, 0:GW], lhsT=ones[:], rhs=PT[pi][:, 0:GW], start=(idx == 0), stop=(idx == L - 1)),
                                   reads=["ones", ("PT", pi)], writes=[("ps", kd)], signal=(idx == L - 1))
                        di = dr()
                        for g in range(G):
                            h = kh * G + g
                            sch.op("dve", lambda e, g=g, h=h: e.tensor_scalar(out=den[di][:, g * 128:(g + 1) * 128], in0=bank(kd)[:, g * 128:(g + 1) * 128],
                                                                         scalar1=sinkexp[:, ja * c.A_HEADS + h:ja * c.A_HEADS + h + 1], scalar2=None, op0=ALU.add),
                                   reads=[("ps", kd), "sinkexp"], writes=[("den", di)])
                        sch.op("dve", lambda e: e.reciprocal(out=den[di][:, 0:GW], in_=den[di][:, 0:GW]), reads=[("den", di)], writes=[("den", di)])
                        sch.op("dve", lambda e: e.tensor_tensor(out=oT[:, kh * G:(kh + 1) * G, n * 128:(n + 1) * 128], in0=bank(ko)[:, 0:GW].rearrange("p (g q) -> p g q", g=G),
                                                                in1=den[di][:, 0:GW].rearrange("p (g q) -> p g q", g=G), op=ALU.mult),
                               reads=[("ps", ko), ("den", di)], writes=["oT"])

                    prev = None
                    for kh in range(KV if "A_noattn" not in DBG else 0):
                        for n in range(4):
                            cur = att_scores(kh, n)
                            if prev is not None:
                                att_pv(prev)
                            prev = cur
                    if prev is not None:
                        att_pv(prev)
                    out_proj(wp, ("a_o", ja), c.A_HEADS, oT, "oT", bin_, bout, seq, blk, xres, xn, brr)
                sch.barrier()

        def rope_a(k, pc0, w, R, npart, tab, tcol, qb, qb2, out_ap, out_res, rc):
            i = rc()
            y1, y2 = qb[i], qb2[i]
            P = slice(0, npart)
            sch.op("dve", lambda e: e.tensor_tensor(out=y1[P, 0:w], in0=bank(k)[P, pc0:pc0 + w], in1=tab[P, 0, tcol:tcol + w], op=ALU.mult),
                   reads=[("ps", k), "tab"], writes=[("qb", i)])
            sch.op("dve", lambda e: e.tensor_tensor(out=y2[P, 0:w], in0=bank(k)[P, pc0:pc0 + w], in1=tab[P, 1, tcol:tcol + w], op=ALU.mult),
                   reads=[("ps", k), "tab"], writes=[("qb2", i)])

            def fin(k2):
                sch.op("pe", lambda e: e.matmul(bank(k2)[P, 0:w], lhsT=identb[P, P], rhs=y1[P, 0:w], start=True, stop=False),
                       reads=[("qb", i), "cbf"], writes=[("ps", k2)], signal=False)
                sch.op("pe", lambda e: e.matmul(bank(k2)[P, 0:w], lhsT=R, rhs=y2[P, 0:w], start=False, stop=True),
                       reads=[("qb2", i), "cbf"], writes=[("ps", k2)])
                sch.op("act", lambda e: e.activation(out=out_ap, in_=bank(k2)[P, 0:w], func=AF.Copy),
                       reads=[("ps", k2)], writes=[out_res])
            return fin

        def rope_cols(k, k2, pc0, w, R, npart, tab, tcol, qb, t1, t2, out_ap, out_res, rc, qb2=None):
            if "oldrope" not in DBG and "norope" not in DBG:
                fin = rope_a(k, pc0, w, R, npart, tab, tcol, qb, qb2, out_ap, out_res, rc)
                fin(k2)
                return
            if "norope" in DBG:
                sch.op("act", lambda e: e.activation(out=out_ap, in_=bank(k)[0:npart, pc0:pc0 + w], func=AF.Copy),
                       reads=[("ps", k)], writes=[out_res])
                return
            i = rc()
            qbt, t1t, t2t = qb[i], t1[i], t2[i]
            P = slice(0, npart)
            sch.op("act", lambda e: e.activation(out=qbt[P, 0:w], in_=bank(k)[P, pc0:pc0 + w], func=AF.Copy),
                   reads=[("ps", k)], writes=[("qb", i)])
            Ruse = R if "ropeones" not in DBG else ones[0:npart, 0:npart]
            if "rope_nomm" in DBG:
                k2 = k
            else:
                sch.op("pe", lambda e: e.matmul(bank(k2)[P, 0:w], lhsT=Ruse, rhs=qbt[P, 0:w], start=True, stop=True),
                       reads=[("qb", i), "cbf"], writes=[("ps", k2)])
            if "rope_nodve" in DBG:
                sch.op("act", lambda e: e.activation(out=out_ap, in_=bank(k2)[P, 0:w], func=AF.Copy),
                       reads=[("ps", k2)], writes=[out_res])
                return
            if "rope_nomul" in DBG:
                sch.op("dve", lambda e: e.memset(t1t[P, 0:w], 1.0), reads=[], writes=[("t1", i)])
                sch.op("dve", lambda e: e.memset(t2t[P, 0:w], 1.0), reads=[], writes=[("t2", i)])
                sch.op("pool", lambda e: e.tensor_tensor(out=out_ap, in0=t1t[P, 0:w], in1=t2t[P, 0:w], op=ALU.add),
                       reads=[("t1", i), ("t2", i), ("ps", k2)], writes=[out_res])
                return
            sch.op("dve", lambda e: e.tensor_tensor(out=t1t[P, 0:w], in0=bank(k)[P, pc0:pc0 + w], in1=tab[P, 0, tcol:tcol + w], op=ALU.mult),
                   reads=[("ps", k), "tab", ("qb", i)], writes=[("t1", i)])
            sch.op("dve", lambda e: e.tensor_tensor(out=t2t[P, 0:w], in0=bank(k2)[P, 0:w], in1=tab[P, 1, tcol:tcol + w], op=ALU.mult),
                   reads=[("ps", k2), "tab"], writes=[("t2", i)])
            if "rope_noadd" in DBG:
                sch.op("act", lambda e: e.activation(out=out_ap, in_=t1t[P, 0:w], func=AF.Copy),
                       reads=[("t1", i), ("t2", i)], writes=[out_res])
                return
            sch.op("pool" if "ropedve" not in DBG else "dve", lambda e: e.tensor_tensor(out=out_ap, in0=t1t[P, 0:w], in1=t2t[P, 0:w], op=ALU.add),
                   reads=[("t1", i), ("t2", i)], writes=[out_res])

        def phase_F(l, seq, bin_, bout):
            W = TB + 2
            NFC = c.NFC
            with ExitStack() as ph:
                def T(name, shape, dt):
                    return SB(ph, name, shape, dt)
                xt = T("f_xt", [128, NDC, W // 2], F32)
                sq = [T("f_sq%d" % i, [128, W // 2], BF16) for i in range(2)]
                rstd = T("f_rstd", [128, W // 2], F32)
                hT = T("f_hT", [128, NDC, W], BF16)
                aT = T("f_aT", [128, NFC, TB], BF16)
                tmp = [T("f_tmp%d" % i, [128, 2, 256], F32) for i in range(3)]
                sg = [T("f_sg%d" % i, [128, 2, 256], F32) for i in range(2)]
                xres = [T("f_xres%d" % i, [128, TB], F32) for i in range(2)]
                xn = [T("f_xn%d" % i, [128, TB], F32) for i in range(2)]
                wpi = WPool(ph, "f_wi", 4, NDC, 128)
                wpo = WPool(ph, "f_wo", 2, NFC, 128)
                prr = rr(3)
                trr = rr(3)
                srr = rr(2)
                orr_state = rr(2)

                def obank():
                    return 6 + orr_state()
                for blk in range(NB):
                    c0 = PAD + blk * TB - 1
                    norm_block((xt, sq, rstd), bin_, seq, c0, W, ("gffn", l), hT, "hT", [6, 7])
                    for i in range(NFC):
                        sgi = None
                        for which in range(2):
                            t = 2 * i + which
                            wt, wres = wpi.load(("f_in", l), t)
                            pi = prr()
                            for sc in range(2):
                                for kc in range(NDC):
                                    sch.op("pe", lambda e, kc=kc, wt=wt, pi=pi, sc=sc: e.matmul(pp[pi][:, sc, 0:258], lhsT=wt[:, kc, :], rhs=hT[:, kc, sc * 256:sc * 256 + 258], start=(kc == 0), stop=(kc == NDC - 1)),
                                           reads=[wres, "hT"], writes=[("pp", pi)], signal=(kc == NDC - 1 and sc == 1))
                            ti = trr()
                            w0, w1, w2, cb = cc(("cw", l, 0), t), cc(("cw", l, 1), t), cc(("cw", l, 2), t), cc(("cb", l), t)
                            sch.op("act", lambda e, ti=ti, pi=pi, w1=w1, cb=cb: e.activation(out=tmp[ti][:], in_=pp[pi][:, :, 1:257], func=AF.Identity, bias=cb, scale=w1),
                                   reads=[("pp", pi), "cst"], writes=[("tmp", ti)])
                            sch.op("dve", lambda e, ti=ti, pi=pi, w0=w0: e.scalar_tensor_tensor(out=tmp[ti][:], in0=pp[pi][:, :, 0:256], scalar=w0, in1=tmp[ti][:], op0=ALU.mult, op1=ALU.add),
                                   reads=[("pp", pi), ("tmp", ti), "cst"], writes=[("tmp", ti)])
                            sch.op("dve", lambda e, ti=ti, pi=pi, w2=w2: e.scalar_tensor_tensor(out=tmp[ti][:], in0=pp[pi][:, :, 2:258], scalar=w2, in1=tmp[ti][:], op0=ALU.mult, op1=ALU.add),
                                   reads=[("pp", pi), ("tmp", ti), "cst"], writes=[("tmp", ti)])
                            if which == 0:
                                sgi = srr()
                                sch.op("act", lambda e, ti=ti, sgi=sgi: e.activation(out=sg[sgi][:], in_=tmp[ti][:], func=AF.Silu),
                                       reads=[("tmp", ti)], writes=[("sg", sgi)])
                            else:
                                sch.op("pool", lambda e, ti=ti, sgi=sgi, i=i: e.tensor_tensor(out=aT[:, i, :].rearrange("p (s q) -> p s q", s=2), in0=tmp[ti][:], in1=sg[sgi][:], op=ALU.mult),
                                       reads=[("tmp", ti), ("sg", sgi)], writes=["aT"])
                    c1 = PAD + blk * TB
                    for m in range(NDC):
                        wt, wres = wpo.load(("f_out", l), m)
                        k = obank()
                        ii = m % 2
                        srcx = xs[bin_][seq][m * 128:(m + 1) * 128, c1:c1 + TB]
                        sch.dma("sp", "f_xres%d" % ii, lambda e, ii=ii, srcx=srcx: e.dma_start(out=xres[ii][:], in_=srcx),
                                reads=[("x", bin_, seq, blk)], writes=[("xres", ii)])
                        for kc in range(NFC):
                            sch.op("pe", lambda e, kc=kc, wt=wt, k=k: e.matmul(bank(k), lhsT=wt[:, kc, :], rhs=aT[:, kc, :], start=(kc == 0), stop=(kc == NFC - 1)),
                                   reads=[wres, "aT"], writes=[("ps", k)], signal=(kc == NFC - 1))
                        sch.op("dve", lambda e, ii=ii, k=k: e.tensor_tensor(out=xn[ii][:], in0=bank(k), in1=xres[ii][:], op=ALU.add),
                               reads=[("ps", k), ("xres", ii)], writes=[("xn", ii)])
                        dst = xs[bout][seq][m * 128:(m + 1) * 128, c1:c1 + TB]
                        sch.dma("pool", "f_xn%d" % ii, lambda e, ii=ii, dst=dst: e.dma_start(out=dst, in_=xn[ii][:]),
                                reads=[("xn", ii)], writes=[("x", bout, seq, blk)])
                sch.barrier()

        def phase_B(l, jb, seq, bin_, bout):
            scale = 192.0 ** -0.5
            KQ, KK = c.QL // 128, c.KVL // 128
            NH = c.B_HEADS
            NT = S // 128
            NQC = S // TB
            with ExitStack() as outer:
                def TO(name, shape, dt):
                    return SB(outer, name, shape, dt)
                cqn = TO("b_cqn", [128, KQ, S], BF16)
                ckvn = TO("b_ckvn", [128, KK, S], BF16)
                krT = TO("b_krT", [64, S], BF16)
                tab = TO("b_tab", [64, 2, S], F32)
                qb = [TO("b_qb%d" % i, [128, 256], BF16) for i in range(2)]
                qb2 = [TO("b_qc%d" % i, [128, 256], BF16) for i in range(2)]
                t1 = [TO("b_t1%d" % i, [128, 256], F32) for i in range(2)]
                t2 = [TO("b_t2%d" % i, [128, 256], F32) for i in range(2)]
                sch.dma("sp", "b_tab", lambda e: e.dma_start(out=tab[:], in_=tabB_d[:, :, :]), writes=["tab"])
                rc = rr(2)
                with ExitStack() as ph:
                    def T(name, shape, dt):
                        return SB(ph, name, shape, dt)
                    xt = T("b1_xt", [128, NDC, TB // 2], F32)
                    sq = [T("b1_sq%d" % i, [128, TB], BF16) for i in range(2)]
                    rstd = T("b1_rstd", [128, TB], F32)
                    hT = T("b1_hT", [128, NDC, TB], BF16)
                    cf = T("b1_cf", [128, max(KQ, KK), TB], F32)
                    wp = WPool(ph, "b1_w", 3, NDC, 128)
                    brr = rr(6)
                    for blk in range(NB):
                        c0 = PAD + blk * TB
                        norm_block((xt, sq, rstd), bin_, seq, c0, TB, ("gmix", l), hT, "hT", [6, 7])
                        for grp, (nk, dst, gk) in enumerate(((KQ, cqn, ("bqn", jb)), (KK, ckvn, ("bkvn", jb)))):
                            for j in range(nk):
                                t = grp * KQ + j
                                wt, wres = wp.load(("b_in", jb), t)
                                k = brr()
                                for kc in range(NDC):
                                    sch.op("pe", lambda e, kc=kc, wt=wt, k=k: e.matmul(bank(k), lhsT=wt[:, kc, :], rhs=hT[:, kc, :], start=(kc == 0), stop=(kc == NDC - 1)),
                                           reads=[wres, "hT"], writes=[("ps", k)], signal=(kc == NDC - 1))
                                sch.op("act", lambda e, k=k, j=j: e.activation(out=cf[:, j, :], in_=bank(k), func=AF.Copy),
                                       reads=[("ps", k)], writes=[("cf", j)])
                                sqt = sq[j % 2]
                                sch.op("act", lambda e, k=k, sqt=sqt: e.activation(out=sqt[:], in_=bank(k), func=AF.Square),
                                       reads=[("ps", k)], writes=[("sq", j % 2)])
                                sch.op("pe", lambda e, j=j, sqt=sqt, nk=nk: e.matmul(bank(7), lhsT=ones[:], rhs=sqt[:], start=(j == 0), stop=(j == nk - 1)),
                                       reads=[("sq", j % 2), "ones"], writes=[("ps", 7)], signal=True)
                            sch.op("act", lambda e, nk=nk: e.activation(out=rstd[:], in_=bank(7), func=AF.Sqrt, bias=epsb[:, 0:1], scale=1.0 / (nk * 128)),
                                   reads=[("ps", 7), "epsb"], writes=["rstd"])
                            sch.op("dve", lambda e: e.reciprocal(out=rstd[:], in_=rstd[:]),
                                   reads=["rstd"], writes=["rstd"])
                            for j in range(nk):
                                sch.op("dve", lambda e, j=j, dst=dst, gk=gk, blk=blk: e.scalar_tensor_tensor(out=dst[:, j, blk * TB:(blk + 1) * TB], in0=cf[:, j, :], scalar=cc(gk, j), in1=rstd[:],
                                                                                                  op0=ALU.mult, op1=ALU.mult),
                                       reads=[("cf", j), "rstd", "cst"], writes=[("lat", grp)])
                        wt, wres = wp.load(("b_inr", jb), 0)
                        k, k2 = brr(), brr()
                        for kc in range(NDC):
                            sch.op("pe", lambda e, kc=kc, wt=wt, k=k: e.matmul(bank(k)[0:64, :], lhsT=wt[:, kc, 0:64], rhs=hT[:, kc, :], start=(kc == 0), stop=(kc == NDC - 1)),
                                   reads=[wres, "hT"], writes=[("ps", k)], signal=(kc == NDC - 1))
                        for hh in range(2):
                            rope_cols(k, k2, hh * 256, 256, RB, 64, tab, blk * TB + hh * 256, qb, t1, t2, krT[:, blk * TB + hh * 256:blk * TB + (hh + 1) * 256], "krT", rc, qb2)
                    sch.barrier()
                with ExitStack() as ph:
                    def T(name, shape, dt):
                        return SB(ph, name, shape, dt)
                    qn = T("b2_qn", [128, S], BF16)
                    qr = T("b2_qr", [64, S], BF16)
                    kn = T("b2_kn", [128, S], BF16)
                    Vh = T("b2_V", [128, NT, 128], BF16)
                    PT = [T("b2_PT%d" % i, [128, TB], BF16) for i in range(4)]
                    rden = [T("b2_rd%d" % i, [128, TB], F32) for i in range(2)]
                    acc = [[T("b2_acc%d%d" % (i, j), [128, TB], F32) for j in range(2)] for i in range(2)]
                    hi = [T("b2_hi%d" % i, [128, TB], BF16) for i in range(2)]
                    lo = [T("b2_lo%d" % i, [128, TB], BF16) for i in range(2)]
                    oh = [T("b2_oh%d" % i, [128, S], BF16) for i in range(2)]
                    wq = WPool(ph, "b2_wqn", 2, KQ, 128)
                    wqr = WPool(ph, "b2_wqr", 2, KQ, 64)
                    wk = WPool(ph, "b2_wkn", 2, KK, 128)
                    wvp = WPool(ph, "b2_wv", 2, KK, 128)
                    srr = rr(4)
                    prr = rr(4)
                    arr = rr(2)
                    for h in range(NH):
                        wqt, wqres = wq.load(("b_qn", jb), h)
                        wqrt, wqrres = wqr.load(("b_qr", jb), h)
                        wkt, wkres = wk.load(("b_kn", jb), h)
                        wvt, wvres = wvp.load(("b_v", jb), h)
                        for qc in range(NQC):
                            cs = slice(qc * TB, (qc + 1) * TB)
                            k = srr()
                            for kc in range(KQ):
                                sch.op("pe", lambda e, kc=kc, k=k, cs=cs, wqt=wqt: e.matmul(bank(k), lhsT=wqt[:, kc, :], rhs=cqn[:, kc, cs], start=(kc == 0), stop=(kc == KQ - 1)),
                                       reads=[wqres, ("lat", 0)], writes=[("ps", k)], signal=(kc == KQ - 1))
                            sch.op("act", lambda e, k=k, cs=cs: e.activation(out=qn[:, cs], in_=bank(k), func=AF.Copy), reads=[("ps", k)], writes=["qn"])
                            k = srr()
                            for kc in range(KK):
                                sch.op("pe", lambda e, kc=kc, k=k, cs=cs, wkt=wkt: e.matmul(bank(k), lhsT=wkt[:, kc, :], rhs=ckvn[:, kc, cs], start=(kc == 0), stop=(kc == KK - 1)),
                                       reads=[wkres, ("lat", 1)], writes=[("ps", k)], signal=(kc == KK - 1))
                            sch.op("dve", lambda e, k=k, cs=cs: e.tensor_copy(out=kn[:, cs], in_=bank(k)), reads=[("ps", k)], writes=["kn"])
                            k, k2 = srr(), srr()
                            for kc in range(KQ):
                                sch.op("pe", lambda e, kc=kc, k=k, cs=cs, wqrt=wqrt: e.matmul(bank(k)[0:64, :], lhsT=wqrt[:, kc, :], rhs=cqn[:, kc, cs], start=(kc == 0), stop=(kc == KQ - 1)),
                                       reads=[wqrres, ("lat", 0)], writes=[("ps", k)], signal=(kc == KQ - 1))
                            for hh in range(2):
                                rope_cols(k, k2, hh * 256, 256, RB, 64, tab, qc * TB + hh * 256, qb, t1, t2, qr[:, qc * TB + hh * 256:qc * TB + (hh + 1) * 256], "qr", rc, qb2)
                        for t4 in range(NT // 4):
                            k = srr()
                            for tt in range(4):
                                tk = t4 * 4 + tt
                                for kc in range(KK):
                                    sch.op("pe", lambda e, kc=kc, k=k, tt=tt, tk=tk, wvt=wvt: e.matmul(bank(k)[:, tt * 128:(tt + 1) * 128], lhsT=ckvn[:, kc, tk * 128:(tk + 1) * 128], rhs=wvt[:, kc, :],
                                                                                                 start=(kc == 0), stop=(kc == KK - 1)),
                                           reads=[wvres, ("lat", 1)], writes=[("ps", k)], signal=(kc == KK - 1 and tt == 3))
                            sch.op("act", lambda e, k=k, t4=t4: e.activation(out=Vh[:, t4 * 4:(t4 + 1) * 4, :], in_=bank(k).rearrange("p (t d) -> p t d", t=4), func=AF.Copy),
                                   reads=[("ps", k)], writes=["Vh"])
                        oi = h % 2
                        for qc in range(NQC):
                            cs = slice(qc * TB, (qc + 1) * TB)
                            a = arr()
                            ko, kd = 4 + 2 * a, 5 + 2 * a
                            pend = []

                            def scores(m, cs=cs):
                                k = srr()
                                sch.op("pe", lambda e, k=k, m=m: e.matmul(bank(k), lhsT=kn[:, m * 128:(m + 1) * 128], rhs=qn[:, cs], start=True, stop=False),
                                       reads=["kn", "qn"], writes=[("ps", k)], signal=False)
                                sch.op("pe", lambda e, k=k, m=m: e.matmul(bank(k), lhsT=krT[:, m * 128:(m + 1) * 128], rhs=qr[:, cs], start=False, stop=True),
                                       reads=["krT", "qr"], writes=[("ps", k)])
                                pi = prr()
                                sch.op("act", lambda e, k=k, pi=pi: e.activation(out=PT[pi][:], in_=bank(k), func=AF.Exp, scale=scale),
                                       reads=[("ps", k)], writes=[("PT", pi)])
                                return pi

                            di = qc % 2

                            def pv(m, pi, ko=ko, kd=kd, di=di):
                                sch.op("pe", lambda e: e.matmul(bank(ko), lhsT=Vh[:, m, :], rhs=PT[pi][:], start=(m == 0), stop=(m == NT - 1)),
                                       reads=["Vh", ("PT", pi)], writes=[("ps", ko)], signal=(m == NT - 1))
                                eng = "pool" if m % 2 == 0 else "dve"
                                ac = acc[di][m % 2]
                                if m < 2:
                                    sch.op(eng, lambda e: e.tensor_copy(out=ac[:], in_=PT[pi][:]), reads=[("PT", pi)], writes=[("acc", di, m % 2)])
                                else:
                                    sch.op(eng, lambda e: e.tensor_tensor(out=ac[:], in0=ac[:], in1=PT[pi][:], op=ALU.add),
                                           reads=[("PT", pi), ("acc", di, m % 2)], writes=[("acc", di, m % 2)])
                            LOOK = 2
                            for m in range(NT + LOOK):
                                if m < NT:
                                    pend.append((m, scores(m)))
                                if m >= LOOK:
                                    mm, pi = pend.pop(0)
                                    pv(mm, pi)
                            a0, a1 = acc[di][0], acc[di][1]
                            sch.op("dve", lambda e, a0=a0, a1=a1: e.tensor_tensor(out=a0[:], in0=a0[:], in1=a1[:], op=ALU.add),
                                   reads=[("acc", di, 0), ("acc", di, 1)], writes=[("acc", di, 0)])
                            sch.op("dve", lambda e, a0=a0, di=di: e.tensor_copy(out=hi[di][:], in_=a0[:]), reads=[("acc", di, 0)], writes=[("hi", di)])
                            sch.op("dve", lambda e, a0=a0, di=di: e.tensor_tensor(out=lo[di][:], in0=a0[:], in1=hi[di][:], op=ALU.subtract),
                                   reads=[("acc", di, 0), ("hi", di)], writes=[("lo", di)])
                            sch.op("pe", lambda e, di=di, kd=kd: e.matmul(bank(kd), lhsT=ones[:], rhs=hi[di][:], start=True, stop=False),
                                   reads=["ones", ("hi", di)], writes=[("ps", kd)], signal=False)
                            sch.op("pe", lambda e, di=di, kd=kd: e.matmul(bank(kd), lhsT=ones[:], rhs=lo[di][:], start=False, stop=True),
                                   reads=["ones", ("lo", di)], writes=[("ps", kd)])
                            sch.op("dve", lambda e, di=di, kd=kd: e.reciprocal(out=rden[di][:], in_=bank(kd)), reads=[("ps", kd)], writes=[("rden", di)])
                            sch.op("dve", lambda e, di=di, ko=ko, oi=oi, cs=cs: e.tensor_tensor(out=oh[oi][:, cs], in0=bank(ko), in1=rden[di][:], op=ALU.mult),
                                   reads=[("ps", ko), ("rden", di)], writes=[("oh", oi)])
                        dst = osc[seq][h * 128:(h + 1) * 128, :]
                        sch.dma("pool", "b2_oh%d" % oi, lambda e, oi=oi, dst=dst: e.dma_start(out=dst, in_=oh[oi][:]),
                                reads=[("oh", oi)], writes=[("osc", seq)])
                    sch.barrier()
            with ExitStack() as ph:
                def T(name, shape, dt):
                    return SB(ph, name, shape, dt)
                ot = [T("b3_ot%d" % i, [128, NH, TB], BF16) for i in range(2)]
                xres = [T("b3_xres%d" % i, [128, TB], F32) for i in range(2)]
                xn = [T("b3_xn%d" % i, [128, TB], F32) for i in range(2)]
                wp = WPool(ph, "b3_w", 3, NH, 128)
                brr = rr(8)
                for blk in range(NB):
                    oi = blk % 2
                    src = osc[seq].rearrange("(k p) c -> p k c", p=128)[:, :, blk * TB:(blk + 1) * TB]
                    sch.dma("sp", "b3_ot%d" % oi, lambda e, oi=oi, src=src: e.dma_start(out=ot[oi][:], in_=src),
                            reads=[("osc", seq)], writes=[("ot", oi)])
                    out_proj(wp, ("b_o", jb), NH, ot[oi], ("ot", oi), bin_, bout, seq, blk, xres, xn, brr)
                sch.barrier()

        def phase_Y(seq, bin_):
            with ExitStack() as ph:
                def T(name, shape, dt):
                    return SB(ph, name, shape, dt)
                xt = T("y_xt", [128, NDC, TB // 2], F32)
                sq = [T("y_sq%d" % i, [128, TB // 2], BF16) for i in range(2)]
                rstd = T("y_rstd", [128, TB // 2], F32)
                yT = T("y_yT", [128, NDC, TB], F32)
                ytok = [T("y_tok%d" % i, [128, D], F32) for i in range(2)]
                idt = T("y_idt", [128, 128], F32)
                sch.dma("sp", "y_idt", lambda e: e.dma_start(out=idt[:], in_=id_d[:, :]), writes=["idt"])
                brr = rr(6)
                for blk in range(NB):
                    norm_block((xt, sq, rstd), bin_, seq, PAD + blk * TB, TB, ("gfin", 0), yT, "yT", [6, 7])
                    for tt in range(4):
                        yi = tt % 2
                        for d4 in range(NDC // 4 if NDC >= 4 else 1):
                            nd = min(4, NDC)
                            k = brr()
                            for dd in range(nd):
                                dc = d4 * 4 + dd
                                sch.op("pe", lambda e, k=k, dd=dd, dc=dc, tt=tt: e.transpose(bank(k)[:, dd * 128:(dd + 1) * 128], yT[:, dc, tt * 128:(tt + 1) * 128], idt[:]),
                                       reads=["yT", "idt"], writes=[("ps", k)], signal=(dd == nd - 1))
                            eng = "act" if d4 % 2 == 0 else "dve"
                            if eng == "act":
                                sch.op("act", lambda e, k=k, d4=d4, yi=yi, nd=nd: e.activation(out=ytok[yi][:, d4 * 512:d4 * 512 + nd * 128], in_=bank(k)[:, 0:nd * 128], func=AF.Copy),
                                       reads=[("ps", k)], writes=[("ytok", yi, d4)])
                            else:
                                sch.op("dve", lambda e, k=k, d4=d4, yi=yi, nd=nd: e.tensor_copy(out=ytok[yi][:, d4 * 512:d4 * 512 + nd * 128], in_=bank(k)[:, 0:nd * 128]),
                                       reads=[("ps", k)], writes=[("ytok", yi, d4)])
                        r0 = seq * S + blk * TB + tt * 128
                        o = sch.dma("pool", "y_tok%d" % yi, lambda e, yi=yi, r0=r0: e.dma_start(out=y_out[r0:r0 + 128, :], in_=ytok[yi][:]),
                                    reads=[("ytok", yi, d4) for d4 in range(max(1, NDC // 4))], writes=[])
                        sch.out_tokens.append(o)
                sch.barrier()

        hl = 0
        for l in range(c.DEPTH):
            for seq in range(NSEQ):
                if l % 2 == 0:
                    if "noA" not in DBG:
                        phase_A(l, l // 2, seq, hl % 2, (hl + 1) % 2)
                else:
                    phase_B(l, l // 2, seq, hl % 2, (hl + 1) % 2)
            hl += 1
            for seq in range(NSEQ):
                if "noF" not in DBG:
                    phase_F(l, seq, hl % 2, (hl + 1) % 2)
            hl += 1
        for seq in range(NSEQ):
            phase_Y(seq, hl % 2)
        stuck = sch.simulate()
        assert not stuck, stuck
        print('streams', {e: len(o) for e, o in sch.streams.items()}, 'nsem', len(sch.dsem) + 4, flush=True)
        with nc.Block() as block:
            sch.emit(block)
    return nc


def _prep_common(cfg, inp):
    wl, cl = WLayout(cfg), CLayout(cfg)
    wall = pack_weights(cfg, wl, inp)
    consts = pack_consts(cfg, cl, inp)
    tabA, tabB, cbf, ident = make_tables(cfg)
    d = {"consts": consts, "tabA": tabA, "tabB": tabB, "cbf": cbf, "ident": ident}
    for g, w_ in enumerate(wall):
        d["wall%d" % g] = w_
    return d


def run(cfg, seqs, inp, trace=False):
    nc = build(cfg)
    common = _prep_common(cfg, inp)
    in_maps = []
    for s in seqs:
        m = dict(common)
        m["x_in"] = np.ascontiguousarray(s.reshape(cfg.NSEQ * cfg.S, cfg.D))
        in_maps.append(m)
    res = run_bass_kernel_spmd(nc, in_maps, core_ids=list(range(len(seqs))), trace=trace)
    outs = [np.asarray(r["y"]).reshape(cfg.NSEQ, cfg.S, cfg.D) for r in res.results]
    return outs, res


def kernel(x_prompt, x_sample, **w):
    cfg = Cfg()
    xp = np.asarray(x_prompt, np.float32)
    xsm = np.asarray(x_sample, np.float32)
    zero = np.zeros_like(xsm[0])
    seqs = [np.stack([xp[2 * i], xp[2 * i + 1]]) for i in range(4)] + [np.stack([xsm[i], zero]) for i in range(4)]
    outs, _ = run(cfg, seqs, w)
    y_prompt = np.concatenate([outs[i] for i in range(4)], axis=0).astype(np.float32)
    y_sample = np.stack([outs[4 + i][0] for i in range(4)], axis=0).astype(np.float32)
    return (y_prompt, y_sample)
```

```python
from contextlib import ExitStack
import numpy as np
import ml_dtypes
import concourse.bass as bass
import concourse.mybir as mybir
from concourse.bass_utils import run_bass_kernel_spmd

F32 = mybir.dt.float32
BF16 = mybir.dt.bfloat16
AF = mybir.ActivationFunctionType
ALU = mybir.AluOpType
DBG = set()
PAD = 128
TB = 512
EPS = 1e-6
THETA = 10000.0


class Cfg:
    def __init__(self, D=2048, S=4096, NSEQ=2, DEPTH=4, A_KV=4, B_HEADS=16, QL=512, KVL=512, DFF=5632):
        self.D, self.S, self.NSEQ, self.DEPTH = D, S, NSEQ, DEPTH
        self.A_HEADS = D // 128
        self.A_KV = A_KV
        self.G = self.A_HEADS // A_KV
        self.B_HEADS, self.QL, self.KVL, self.DFF = B_HEADS, QL, KVL, DFF
        self.NDC = D // 128
        self.NFC = DFF // 128
        self.LA = (DEPTH + 1) // 2
        self.LB = DEPTH // 2
        self.SP = S + 2 * PAD
        self.NB = S // TB


def _tiles(W, mt):
    din, dout = W.shape
    kc, nm = din // 128, dout // mt
    return np.ascontiguousarray(W.reshape(kc, 128, nm, mt).transpose(2, 1, 0, 3))


class WLayout:
    def __init__(self, cfg):
        c = cfg
        self.off = {}
        self.NG = max(1, c.DEPTH)
        self.total = [0] * self.NG
        kd, kq, kv, kf = c.NDC, c.QL // 128, c.KVL // 128, c.NFC
        for j in range(c.LA):
            g = 2 * j
            self._add(g, ("a_qk", j), c.A_HEADS + c.A_KV, kd, 128)
            self._add(g, ("a_v", j), 1, kd, c.A_KV * 128)
            self._add(g, ("a_o", j), c.NDC, c.A_HEADS, 128)
        for j in range(c.LB):
            g = 2 * j + 1
            self._add(g, ("b_in", j), (c.QL + c.KVL) // 128, kd, 128)
            self._add(g, ("b_inr", j), 1, kd, 64)
            self._add(g, ("b_qn", j), c.B_HEADS, kq, 128)
            self._add(g, ("b_qr", j), c.B_HEADS, kq, 64)
            self._add(g, ("b_kn", j), c.B_HEADS, kv, 128)
            self._add(g, ("b_v", j), c.B_HEADS, kv, 128)
            self._add(g, ("b_o", j), c.NDC, c.B_HEADS, 128)
        for l in range(c.DEPTH):
            self._add(l, ("f_in", l), 2 * c.NFC, kd, 128)
            self._add(l, ("f_out", l), c.NDC, kf, 128)
        self.CW = 2048
        self.rows = [max(8, -(-(-(-t // self.CW)) // 8) * 8) for t in self.total]

    def _add(self, g, key, nt, kc, mt):
        self.off[key] = (g, self.total[g], nt, kc, mt)
        self.total[g] += nt * 128 * kc * mt

    def tile(self, key, t):
        g, off, nt, kc, mt = self.off[key]
        return g, off + t * 128 * kc * mt, kc, mt


def pack_weights(cfg, wl, inp):
    c = cfg
    flats = [np.zeros(r * wl.CW, np.float32) for r in wl.rows]

    def put(key, arr):
        g, off, nt, kc, mt = wl.off[key]
        assert arr.shape == (nt, 128, kc, mt), (key, arr.shape, (nt, 128, kc, mt))
        flats[g][off:off + arr.size] = arr.reshape(-1)

    qd, kd = c.A_HEADS * 128, c.A_KV * 128
    for j in range(c.LA):
        w = np.asarray(inp["a_w_qkv"][j])
        put(("a_qk", j), _tiles(w[:, :qd + kd], 128))
        put(("a_v", j), _tiles(w[:, qd + kd:], kd))
        put(("a_o", j), _tiles(np.asarray(inp["a_w_o"][j]), 128))
    for j in range(c.LB):
        w = np.asarray(inp["b_w_in"][j])
        put(("b_in", j), _tiles(w[:, :c.QL + c.KVL], 128))
        put(("b_inr", j), _tiles(w[:, c.QL + c.KVL:], 64))
        w = np.asarray(inp["b_w_q_up"][j]).reshape(c.QL, c.B_HEADS, 192)
        put(("b_qn", j), _tiles(np.ascontiguousarray(w[:, :, :128]).reshape(c.QL, -1), 128))
        put(("b_qr", j), _tiles(np.ascontiguousarray(w[:, :, 128:]).reshape(c.QL, -1), 64))
        w = np.asarray(inp["b_w_kv_up"][j]).reshape(c.KVL, c.B_HEADS, 256)
        put(("b_kn", j), _tiles(np.ascontiguousarray(w[:, :, :128]).reshape(c.KVL, -1), 128))
        put(("b_v", j), _tiles(np.ascontiguousarray(w[:, :, 128:]).reshape(c.KVL, -1), 128))
        put(("b_o", j), _tiles(np.asarray(inp["b_w_o"][j]), 128))
    for l in range(c.DEPTH):
        w = np.asarray(inp["f_w_in"][l])
        t = _tiles(w, 128)
        inter = np.empty_like(t)
        inter[0::2] = t[:c.NFC]
        inter[1::2] = t[c.NFC:]
        put(("f_in", l), inter)
        put(("f_out", l), _tiles(np.asarray(inp["f_w_out"][l]), 128))
    return [f.reshape(r, wl.CW) for f, r in zip(flats, wl.rows)]


class CLayout:
    def __init__(self, cfg):
        c = cfg
        self.off = {}
        n = 0
        for key, w in ([(("gmix", l), c.NDC) for l in range(c.DEPTH)] + [(("gffn", l), c.NDC) for l in range(c.DEPTH)]
                       + [(("gfin", 0), c.NDC)]
                       + [(("bqn", j), c.QL // 128) for j in range(c.LB)] + [(("bkvn", j), c.KVL // 128) for j in range(c.LB)]
                       + [(("cw", l, k), 2 * c.NFC) for l in range(c.DEPTH) for k in range(3)]
                       + [(("cb", l), 2 * c.NFC) for l in range(c.DEPTH)]
                       + [(("sink", j), c.A_HEADS) for j in range(c.LA)]):
            self.off[key] = n
            n += w
        self.n = n


def pack_consts(cfg, cl, inp):
    c = cfg
    out = np.zeros((128, cl.n), np.float32)

    def colmajor(v):
        v = np.asarray(v, np.float32)
        return v.reshape(-1, 128).T

    def inter(v):
        t = colmajor(v)
        o = np.empty_like(t)
        o[:, 0::2] = t[:, :c.NFC]
        o[:, 1::2] = t[:, c.NFC:]
        return o

    for l in range(c.DEPTH):
        out[:, cl.off[("gmix", l)]:][:, :c.NDC] = colmajor(inp["norm_mix"][l])
        out[:, cl.off[("gffn", l)]:][:, :c.NDC] = colmajor(inp["norm_ffn"][l])
        for k in range(3):
            out[:, cl.off[("cw", l, k)]:][:, :2 * c.NFC] = inter(inp["f_conv_w"][l][k])
        out[:, cl.off[("cb", l)]:][:, :2 * c.NFC] = inter(inp["f_conv_b"][l])
    out[:, cl.off[("gfin", 0)]:][:, :c.NDC] = colmajor(inp["norm_final"])
    for j in range(c.LB):
        out[:, cl.off[("bqn", j)]:][:, :c.QL // 128] = colmajor(inp["b_q_norm"][j])
        out[:, cl.off[("bkvn", j)]:][:, :c.KVL // 128] = colmajor(inp["b_kv_norm"][j])
    for j in range(c.LA):
        out[:, cl.off[("sink", j)]:][:, :c.A_HEADS] = np.broadcast_to(np.asarray(inp["a_sink"][j], np.float32)[None, :], (128, c.A_HEADS))
    return out


def make_tables(cfg):
    c = cfg
    f32 = np.float32

    def tab(dim, pos):
        inv = (f32(1.0) / (f32(THETA) ** (np.arange(0, dim, 2, dtype=f32) / f32(dim)))).astype(f32)
        ang = (pos.astype(f32)[None, :] * inv[:, None]).astype(f32)
        cs, sn = np.cos(ang).astype(f32), np.sin(ang).astype(f32)
        return np.stack([np.concatenate([cs, cs], 0), np.concatenate([sn, -sn], 0)], 1)

    posA = np.arange(-PAD, c.S + PAD)
    tabA = np.ascontiguousarray(tab(128, posA))
    tabB = np.ascontiguousarray(tab(64, np.arange(c.S)))
    G = c.G
    cbf = np.zeros((128, 128 + 64 + 2 * G * 128 + 128), np.float32)
    cbf[:, 192 + 2 * G * 128:] = np.eye(128, dtype=np.float32)
    for m in range(128):
        cbf[(m + 64) % 128, m] = 1.0
    for m in range(64):
        cbf[(m + 32) % 64, 128 + m] = 1.0
    b = np.arange(128)[:, None]
    a = np.arange(128)[None, :]
    mL = (a <= b).astype(np.float32)
    mR = (b <= a).astype(np.float32)
    cbf[:, 192:192 + G * 128] = np.tile(mL, (1, G))
    cbf[:, 192 + G * 128:192 + 2 * G * 128] = np.tile(mR, (1, G))
    return tabA, tabB, cbf.astype(ml_dtypes.bfloat16), np.eye(128, dtype=np.float32)


class Sched:
    CE = ("pe", "act", "dve", "pool")

    def __init__(self, nc, stack):
        self.nc = nc
        self.stack = stack
        self.streams = {e: [] for e in self.CE + ("sp",)}
        self.sem = {e: stack.enter_context(nc.semaphore("tl_" + e)) for e in self.CE}
        self.cnt = {e: 0 for e in self.CE}
        self.dsem = {}
        self.waited = {e: {} for e in self.streams}
        self.lastw = {}
        self.readers = {}
        self.out_tokens = []

    def _wait(self, o, d):
        if d is None or d.get("tok") is None:
            return
        if d["eng"] == "pe" and o["eng"] == "pe" and not d.get("dma") and not o.get("dma"):
            return
        name, sem, v = d["tok"]
        w = self.waited[o["eng"]]
        if w.get(name, 0) >= v:
            return
        w[name] = v
        o["waits"].append((sem, v))

    def _track(self, o, reads, writes):
        deps = []
        for r in reads:
            if r in self.lastw:
                deps.append(self.lastw[r])
        for r in writes:
            if r in self.lastw:
                deps.append(self.lastw[r])
            deps.extend(self.readers.get(r, ()))
        for d in deps:
            if d is not o:
                self._wait(o, d)
        for r in reads:
            self.readers.setdefault(r, []).append(o)
        for r in writes:
            self.lastw[r] = o
            self.readers[r] = []

    def op(self, eng, fn, reads=(), writes=(), signal=True):
        o = {"eng": eng, "fn": fn, "waits": [], "tok": None, "inc": None}
        self._track(o, reads, writes)
        if signal:
            self.cnt[eng] += 1
            o["tok"] = ("tl_" + eng, self.sem[eng], self.cnt[eng])
            o["inc"] = (self.sem[eng], 1)
        else:
            assert eng == "pe"
            o["tok"] = ("tl_" + eng, self.sem[eng], self.cnt[eng] + 1)
        self.streams[eng].append(o)
        return o

    def dma(self, q, key, fn, reads=(), writes=(), is_out=False):
        if key not in self.dsem:
            self.dsem[key] = [self.stack.enter_context(self.nc.semaphore("d_" + key)), 0]
        ent = self.dsem[key]
        o = {"eng": q, "fn": fn, "waits": [], "dma": True}
        self._track(o, reads, writes)
        ent[1] += 16
        o["tok"] = ("d_" + key, ent[0], ent[1])
        o["inc"] = (ent[0], 16)
        self.streams[q].append(o)
        return o

    def barrier(self):
        toks = [("tl_" + e, self.sem[e], self.cnt[e]) for e in self.CE if self.cnt[e] > 0]
        toks += [("d_" + k, v[0], v[1]) for k, v in self.dsem.items() if v[1] > 0 and not k.startswith("cast")]
        for e in self.streams:
            o = {"eng": e, "fn": None, "waits": [], "tok": None, "inc": None}
            w = self.waited[e]
            for name, sem, v in toks:
                if name == "tl_" + e:
                    continue
                if w.get(name, 0) < v:
                    w[name] = v
                    o["waits"].append((sem, v))
            self.streams[e].append(o)
        self.lastw = {k: v for k, v in self.lastw.items() if isinstance(k, tuple) and k[0] == "wbf"}
        self.readers = {}

    def simulate(self):
        val = {}
        ptr = {e: 0 for e in self.streams}
        prog = True
        while prog:
            prog = False
            for e, ops in self.streams.items():
                while ptr[e] < len(ops):
                    o = ops[ptr[e]]
                    if any(val.get(id(sem), 0) < v for sem, v in o["waits"]):
                        break
                    if o.get("inc") is not None and o["fn"] is not None:
                        val[id(o["inc"][0])] = val.get(id(o["inc"][0]), 0) + o["inc"][1]
                    ptr[e] += 1
                    prog = True
        stuck = {e: (ptr[e], len(ops)) for e, ops in self.streams.items() if ptr[e] < len(ops)}
        return stuck

    def emit(self, block):
        nc = self.nc
        engmap = {"pe": "tensor", "act": "scalar", "dve": "vector", "pool": "gpsimd", "sp": "sync"}
        for e, ops in self.streams.items():
            def body(eng, ops=ops):
                for o in ops:
                    for sem, v in o["waits"]:
                        eng.wait_ge(sem, v)
                    if o["fn"] is None:
                        continue
                    ins = o["fn"](eng)
                    if o["inc"] is not None:
                        ins.then_inc(o["inc"][0], o["inc"][1])
            getattr(block, engmap[e])(body)


def build(cfg):
    c = cfg
    wl, cl = WLayout(c), CLayout(c)
    D, S, NSEQ, NDC, SPW, NB, G = c.D, c.S, c.NSEQ, c.NDC, c.SP, c.NB, c.G
    nc = bass.Bass("TRN2", target_bir_lowering=False)
    x_in = nc.dram_tensor("x_in", [NSEQ * S, D], F32, kind="ExternalInput").ap()
    wall = [nc.dram_tensor("wall%d" % g, [wl.rows[g], wl.CW], F32, kind="ExternalInput").ap() for g in range(wl.NG)]
    cst_d = nc.dram_tensor("consts", [128, cl.n], F32, kind="ExternalInput").ap()
    tabA_d = nc.dram_tensor("tabA", [128, 2, SPW], F32, kind="ExternalInput").ap()
    tabB_d = nc.dram_tensor("tabB", [64, 2, S], F32, kind="ExternalInput").ap()
    NCB = 192 + 2 * G * 128 + 128
    cbf_d = nc.dram_tensor("cbf", [128, NCB], BF16, kind="ExternalInput").ap()
    id_d = nc.dram_tensor("ident", [128, 128], F32, kind="ExternalInput").ap()
    y_out = nc.dram_tensor("y", [NSEQ * S, D], F32, kind="ExternalOutput").ap()
    wbf = [nc.dram_tensor("wbf%d" % g, [wl.rows[g], wl.CW], BF16, kind="Internal").ap() for g in range(wl.NG)]
    xs = [nc.dram_tensor("xs%d" % i, [NSEQ, D, SPW], F32, kind="Internal").ap() for i in range(2)]
    DO = c.B_HEADS * 128
    osc = nc.dram_tensor("osc", [NSEQ, DO, S], BF16, kind="Internal").ap()
    wflat = [w_.rearrange("r c -> (r c)") for w_ in wbf]

    def xview(buf, seq, c0, w):
        return xs[buf][seq].rearrange("(k p) c -> p k c", p=128)[:, :, c0:c0 + w]

    uid = [0]

    def SB(stack, name, shape, dt):
        uid[0] += 1
        return stack.enter_context(nc.sbuf_tensor("%s_%d" % (name, uid[0]), shape, dt))

    with ExitStack() as top:
        sch = Sched(nc, top)
        cst = top.enter_context(nc.sbuf_tensor("cst", [128, cl.n], F32))
        cbf = top.enter_context(nc.sbuf_tensor("cbf_s", [128, NCB], BF16))
        ones = top.enter_context(nc.sbuf_tensor("ones", [128, 128], BF16))
        epsb = top.enter_context(nc.sbuf_tensor("epsb", [128, 1], F32))
        sinkexp = top.enter_context(nc.sbuf_tensor("sinkexp", [128, max(1, c.LA) * c.A_HEADS], F32))
        pp = [top.enter_context(nc.psum_tensor("pp%d" % i, [128, 2, 512], F32)) for i in range(4)]
        RA = cbf[:, 0:128]
        RB = cbf[0:64, 128:192]
        maskL = cbf[:, 192:192 + G * 128]
        maskR = cbf[:, 192 + G * 128:192 + 2 * G * 128]
        identb = cbf[:, 192 + 2 * G * 128:192 + 2 * G * 128 + 128]

        def bank(k):
            return pp[k // 2][:, k % 2, :]

        def cc(key, i=0, n=1):
            o = cl.off[key]
            return cst[:, o + i:o + i + n]

        sch.dma("sp", "cst", lambda e: e.dma_start(out=cst[:], in_=cst_d[:, :]), writes=["cst"])
        sch.dma("sp", "cbf", lambda e: e.dma_start(out=cbf[:], in_=cbf_d[:, :]), writes=["cbf"])
        sch.op("dve", lambda e: e.memset(ones[:], 1.0), writes=["ones"])
        sch.op("dve", lambda e: e.memset(epsb[:], EPS), writes=["epsb"])
        if c.LA > 0:
            o0 = cl.off[("sink", 0)]
            sch.op("act", lambda e: e.activation(out=sinkexp[:], in_=cst[:, o0:o0 + c.LA * c.A_HEADS], func=AF.Exp),
                   reads=["cst"], writes=["sinkexp"])

        class WPool:
            def __init__(self, st, name, n, kc, mt):
                self.name, self.n, self.i = name, n, 0
                self.tiles = [SB(st, "%s%d" % (name, i), [128, kc, mt], BF16) for i in range(n)]

            def load(self, key, t):
                g_, off, kc, mt = wl.tile(key, t)
                i = self.i % self.n
                self.i += 1
                tl = self.tiles[i]
                src = wflat[g_][off:off + 128 * kc * mt].rearrange("(p k m) -> p k m", p=128, k=kc)
                res = (self.name, i)
                sch.dma("sp", "%s%d" % (self.name, i), lambda e: e.dma_start(out=tl[:, 0:kc, 0:mt], in_=src), reads=[("wbf", g_)], writes=[res])
                return tl, res

        def rr(n):
            st = {"i": -1}

            def nxt():
                st["i"] = (st["i"] + 1) % n
                return st["i"]
            return nxt

        def norm_pre(xt, sq, rstd, W, gkey, hT, hres, stat_banks):
            nsub = 2
            w = W // nsub
            for sc in range(nsub):
                sb = stat_banks[sc % len(stat_banks)]
                for dc in range(NDC):
                    sqt = sq[dc % 2]
                    sch.op("act", lambda e, dc=dc, sqt=sqt, sc=sc: e.activation(out=sqt[:, 0:w], in_=xt[:, dc, sc * w:(sc + 1) * w], func=AF.Square),
                           reads=[("xt", sc)], writes=[("sq", dc % 2)])
                    sch.op("pe", lambda e, dc=dc, sqt=sqt, sb=sb: e.matmul(bank(sb)[:, 0:w], lhsT=ones[:], rhs=sqt[:, 0:w], start=(dc == 0), stop=(dc == NDC - 1)),
                           reads=[("sq", dc % 2), "ones"], writes=[("ps", sb)], signal=True)
                sch.op("act", lambda e, sb=sb: e.activation(out=rstd[:, 0:w], in_=bank(sb)[:, 0:w], func=AF.Sqrt, bias=epsb[:, 0:1], scale=1.0 / D),
                       reads=[("ps", sb), "epsb"], writes=["rstd"])
                sch.op("dve", lambda e: e.reciprocal(out=rstd[:, 0:w], in_=rstd[:, 0:w]),
                       reads=["rstd"], writes=["rstd"])
                for dc in range(NDC):
                    sch.op("dve", lambda e, dc=dc, sc=sc: e.scalar_tensor_tensor(out=hT[:, dc, sc * w:(sc + 1) * w], in0=xt[:, dc, sc * w:(sc + 1) * w], scalar=cc(gkey, dc),
                                                                                 in1=rstd[:, 0:w], op0=ALU.mult, op1=ALU.mult),
                           reads=[("xt", sc), "rstd", "cst"], writes=[hres])

        def norm_block(st_tiles, buf, seq, c0, W, gkey, hT, hres, stat_banks):
            xt, sq, rstd = st_tiles
            nsub = 2
            w = W // nsub
            assert w * nsub == W and w <= 384
            for sc in range(nsub):
                src = xview(buf, seq, c0 + sc * w, w)
                blocks = sorted(set([min(max((c0 + sc * w - PAD) // TB, 0), NB - 1), min(max((c0 + sc * w + w - 1 - PAD) // TB, 0), NB - 1)]))
                sch.dma("sp", "xt", lambda e, src=src: e.dma_start(out=xt[:, :, 0:w], in_=src),
                        reads=[("x", buf, seq, b) for b in blocks], writes=["xt"])
                sb = stat_banks[sc % len(stat_banks)]
                for dc in range(NDC):
                    sqt = sq[dc % 2]
                    sch.op("act", lambda e, dc=dc, sqt=sqt: e.activation(out=sqt[:, 0:w], in_=xt[:, dc, 0:w], func=AF.Square),
                           reads=["xt"], writes=[("sq", dc % 2)])
                    sch.op("pe", lambda e, dc=dc, sqt=sqt, sb=sb: e.matmul(bank(sb)[:, 0:w], lhsT=ones[:], rhs=sqt[:, 0:w], start=(dc == 0), stop=(dc == NDC - 1)),
                           reads=[("sq", dc % 2), "ones"], writes=[("ps", sb)], signal=True)
                sch.op("act", lambda e, sb=sb: e.activation(out=rstd[:, 0:w], in_=bank(sb)[:, 0:w], func=AF.Sqrt, bias=epsb[:, 0:1], scale=1.0 / D),
                       reads=[("ps", sb), "epsb"], writes=["rstd"])
                sch.op("dve", lambda e: e.reciprocal(out=rstd[:, 0:w], in_=rstd[:, 0:w]),
                       reads=["rstd"], writes=["rstd"])
                for dc in range(NDC):
                    sch.op("dve", lambda e, dc=dc, sc=sc: e.scalar_tensor_tensor(out=hT[:, dc, sc * w:(sc + 1) * w], in0=xt[:, dc, 0:w], scalar=cc(gkey, dc),
                                                                                 in1=rstd[:, 0:w], op0=ALU.mult, op1=ALU.mult),
                           reads=["xt", "rstd", "cst"], writes=[hres])

        def rope(ps_k, rot_k, w, R, npart, tab, tcol, qb, t1, t2, out_ap, out_res, rc):
            i = rc()
            qbt, t1t, t2t = qb[i], t1[i], t2[i]
            P = slice(0, npart)
            sch.op("act", lambda e: e.activation(out=qbt[P, 0:w], in_=bank(ps_k)[P, 0:w], func=AF.Copy),
                   reads=[("ps", ps_k)], writes=[("qb", i)])
            sch.op("pe", lambda e: e.matmul(bank(rot_k)[P, 0:w], lhsT=R, rhs=qbt[P, 0:w], start=True, stop=True),
                   reads=[("qb", i), "cbf"], writes=[("ps", rot_k)])
            sch.op("dve", lambda e: e.tensor_tensor(out=t1t[P, 0:w], in0=bank(ps_k)[P, 0:w], in1=tab[P, 0, tcol:tcol + w], op=ALU.mult),
                   reads=[("ps", ps_k), "tab"], writes=[("t1", i)])
            sch.op("dve", lambda e: e.tensor_tensor(out=t2t[P, 0:w], in0=bank(rot_k)[P, 0:w], in1=tab[P, 1, tcol:tcol + w], op=ALU.mult),
                   reads=[("ps", rot_k), "tab"], writes=[("t2", i)])
            sch.op("pool", lambda e: e.tensor_tensor(out=out_ap, in0=t1t[P, 0:w], in1=t2t[P, 0:w], op=ALU.add),
                   reads=[("t1", i), ("t2", i)], writes=[out_res])

        def out_proj(wp, wkey, nkc, act_tile, act_res, xin_buf, xout_buf, seq, blk, xres, xn, bankrr):
            c0 = PAD + blk * TB
            for m in range(NDC):
                wt, wres = wp.load(wkey, m)
                k = bankrr()
                i = m % 2
                srcx = xs[xin_buf][seq][m * 128:(m + 1) * 128, c0:c0 + TB]
                sch.dma("sp", "xres%d" % i, lambda e, i=i, srcx=srcx: e.dma_start(out=xres[i][:], in_=srcx),
                        reads=[("x", xin_buf, seq, blk)], writes=[("xres", i)])
                for kc in range(nkc):
                    sch.op("pe", lambda e, kc=kc, wt=wt, k=k: e.matmul(bank(k), lhsT=wt[:, kc, 0:128], rhs=act_tile[:, kc, :], start=(kc == 0), stop=(kc == nkc - 1)),
                           reads=[wres, act_res], writes=[("ps", k)], signal=(kc == nkc - 1))
                sch.op("dve", lambda e, i=i, k=k: e.tensor_tensor(out=xn[i][:], in0=bank(k), in1=xres[i][:], op=ALU.add),
                       reads=[("ps", k), ("xres", i)], writes=[("xn", i)])
                dst = xs[xout_buf][seq][m * 128:(m + 1) * 128, c0:c0 + TB]
                sch.dma("pool", "xn%d" % i, lambda e, i=i, dst=dst: e.dma_start(out=dst, in_=xn[i][:]),
                        reads=[("xn", i)], writes=[("x", xout_buf, seq, blk)])

        with ExitStack() as ph:
            xtok = SB(ph, "xtok", [128, 4, D], F32)
            xst = ph.enter_context(nc.sbuf_tensor("xst", [128, NDC, TB], F32))
            zt = ph.enter_context(nc.sbuf_tensor("zt", [128, NDC, PAD], F32))
            idt = ph.enter_context(nc.sbuf_tensor("idt", [128, 128], F32))
            sch.dma("sp", "idt", lambda e: e.dma_start(out=idt[:], in_=id_d[:, :]), writes=["idt"])
            def cast_group(g_):
                r0 = 0
                while r0 < wl.rows[g_]:
                    r1 = min(wl.rows[g_], r0 + 1024)
                    last = r1 >= wl.rows[g_]
                    sch.dma("pool", "cast%d" % g_, lambda e, r0=r0, r1=r1, g_=g_: e.dma_start(out=wbf[g_][r0:r1, :], in_=wall[g_][r0:r1, :]),
                            writes=([("wbf", g_)] if last else []))
                    r0 = r1
            cast_group(0)
            sch.op("dve", lambda e: e.memset(zt[:], 0.0), writes=["zt"])
            for b in range(2):
                for seq in range(NSEQ):
                    for side in range(2):
                        dst = xview(b, seq, 0 if side == 0 else PAD + S, PAD)
                        sch.dma("pool", "zpad", lambda e, dst=dst: e.dma_start(out=dst, in_=zt[:]), reads=["zt"])
            brr = rr(8)
            for seq in range(NSEQ):
                for blk in range(NB):
                    rbase = seq * S + blk * TB
                    src = x_in[rbase:rbase + TB, :].rearrange("(t p) d -> p t d", p=128)
                    sch.dma("sp", "xtok", lambda e, src=src: e.dma_start(out=xtok[:], in_=src), writes=["xtok"])
                    for dc in range(NDC):
                        k = brr()
                        for tt in range(4):
                            sch.op("pe", lambda e, k=k, tt=tt, dc=dc: e.transpose(bank(k)[:, tt * 128:(tt + 1) * 128], xtok[:, tt, dc * 128:(dc + 1) * 128], idt[:]),
                                   reads=["xtok", "idt"], writes=[("ps", k)], signal=(tt == 3))
                        if dc % 2 == 0:
                            sch.op("act", lambda e, k=k, dc=dc: e.activation(out=xst[:, dc, :], in_=bank(k), func=AF.Copy),
                                   reads=[("ps", k)], writes=[("xst", dc)])
                        else:
                            sch.op("dve", lambda e, k=k, dc=dc: e.tensor_copy(out=xst[:, dc, :], in_=bank(k)),
                                   reads=[("ps", k)], writes=[("xst", dc)])
                    dst = xview(0, seq, PAD + blk * TB, TB)
                    sch.dma("pool", "xst", lambda e, dst=dst: e.dma_start(out=dst, in_=xst[:]),
                            reads=[("xst", dc) for dc in range(NDC)], writes=[("x", 0, seq, blk)])
            for g_ in range(1, wl.NG):
                cast_group(g_)
            sch.barrier()

        def phase_A(l, ja, seq, bin_, bout):
            W = TB + 2 * PAD
            scale = 128.0 ** -0.5
            KV = c.A_KV
            GW = G * 128
            with ExitStack() as ph:
                def T(name, shape, dt):
                    return SB(ph, name, shape, dt)
                xt = T("a_xt", [128, NDC, W], F32)
                sq = [T("a_sq%d" % i, [128, W // 2], BF16) for i in range(2)]
                rstd = T("a_rstd", [128, W // 2], F32)
                hT = T("a_hT", [128, NDC, W], BF16)
                tab = T("a_tab", [128, 2, W], F32)
                qT = T("a_qT", [128, c.A_HEADS, TB], BF16)
                kT = T("a_kT", [128, KV, W], BF16)
                V = T("a_V", [128, W // 128, KV * 128], BF16)
                qb = [T("a_qb%d" % i, [128, 384], BF16) for i in range(4)]
                qb2 = [T("a_qc%d" % i, [128, 384], BF16) for i in range(4)]
                t1 = t2 = None
                PT = [T("a_PT%d" % i, [128, TB], BF16) for i in range(6)]
                den = [T("a_den%d" % i, [128, TB], F32) for i in range(2)]
                oT = T("a_oT", [128, c.A_HEADS, TB], BF16)
                xres = [T("a_xres%d" % i, [128, TB], F32) for i in range(2)]
                xn = [T("a_xn%d" % i, [128, TB], F32) for i in range(2)]
                wp = WPool(ph, "a_w", 3, NDC, 128)
                wv = T("a_wv", [128, NDC, KV * 128], BF16)
                vg, voff, vkc, vmt = wl.tile(("a_v", ja), 0)
                sch.dma("sp", "a_wv", lambda e: e.dma_start(out=wv[:], in_=wflat[vg][voff:voff + 128 * vkc * vmt].rearrange("(p k m) -> p k m", p=128, k=vkc)),
                        reads=[("wbf", vg)], writes=["wv"])
                brr = rr(8)
                rc = rr(4)
                ptr = rr(6)
                dr = rr(2)
                def load_x(blk):
                    for sc in range(2):
                        cs0 = blk * TB + sc * (W // 2)
                        src = xview(bin_, seq, cs0, W // 2)
                        blocks = sorted(set([min(max((cs0 - PAD) // TB, 0), NB - 1), min(max((cs0 + W // 2 - 1 - PAD) // TB, 0), NB - 1)]))
                        sch.dma("pool", "a_xt%d" % sc, lambda e, src=src, sc=sc: e.dma_start(out=xt[:, :, sc * (W // 2):(sc + 1) * (W // 2)], in_=src),
                                reads=[("x", bin_, seq, b_) for b_ in blocks], writes=[("xt", sc)])
                load_x(0)
                for blk in range(NB):
                    c0 = blk * TB
                    sch.dma("sp", "a_tab", lambda e, c0=c0: e.dma_start(out=tab[:], in_=tabA_d[:, :, c0:c0 + W]), writes=["tab"])
                    norm_pre(xt, sq, rstd, W, ("gmix", l), hT, "hT", [brr(), brr()])
                    if blk + 1 < NB:
                        load_x(blk + 1)
                    pend_rope = []
                    for t in range(0 if "A_noqk" not in DBG else 0, (c.A_HEADS + KV) if "A_noqk" not in DBG else 0):
                        wt, wres = wp.load(("a_qk", ja), t)
                        isq = t < c.A_HEADS
                        subs = [(PAD, TB)] if isq else [(0, W // 2), (W // 2, W // 2)]
                        for (s0, sw) in subs:
                            k = brr()
                            for kc in range(NDC):
                                sch.op("pe", lambda e, kc=kc, wt=wt, k=k, s0=s0, sw=sw: e.matmul(bank(k)[:, 0:sw], lhsT=wt[:, kc, :], rhs=hT[:, kc, s0:s0 + sw], start=(kc == 0), stop=(kc == NDC - 1)),
                                       reads=[wres, "hT"], writes=[("ps", k)], signal=(kc == NDC - 1))
                            for f in pend_rope:
                                f(brr())
                            pend_rope = []
                            if isq:
                                for hh in range(2):
                                    o_ap = qT[:, t, hh * 256:(hh + 1) * 256]
                                    pend_rope.append(rope_a(k, hh * 256, 256, RA, 128, tab, s0 + hh * 256, qb, qb2, o_ap, "qT", rc))
                            else:
                                o_ap = kT[:, t - c.A_HEADS, s0:s0 + sw]
                                pend_rope.append(rope_a(k, 0, sw, RA, 128, tab, s0, qb, qb2, o_ap, "kT", rc))
                    for tt in range(W // 128 if "A_nov" not in DBG else 0):
                        k = brr()
                        if tt == 1:
                            for f in pend_rope:
                                f(brr())
                            pend_rope = []
                        for kc in range(NDC):
                            sch.op("pe", lambda e, kc=kc, k=k, tt=tt: e.matmul(bank(k)[:, 0:KV * 128], lhsT=hT[:, kc, tt * 128:(tt + 1) * 128], rhs=wv[:, kc, :], start=(kc == 0), stop=(kc == NDC - 1)),
                                   reads=["hT", "wv"], writes=[("ps", k)], signal=(kc == NDC - 1))
                        sch.op("act", lambda e, k=k, tt=tt: e.activation(out=V[:, tt, :], in_=bank(k)[:, 0:KV * 128], func=AF.Copy),
                               reads=[("ps", k)], writes=["V"])
                    def att_scores(kh, n, blk=blk):
                        nbg = blk * 4 + n
                        js = [j for j in (0, 1, 2) if 0 <= nbg - 1 + j < S // 128]
                        pts = []
                        for j in js:
                            kt = n + j
                            k = brr()
                            sch.op("pe", lambda e, k=k, kt=kt: e.matmul(bank(k)[:, 0:GW].rearrange("p (g q) -> p g q", g=G), lhsT=kT[:, kh, kt * 128:(kt + 1) * 128],
                                                                     rhs=qT[:, kh * G:(kh + 1) * G, n * 128:(n + 1) * 128], start=True, stop=True),
                                   reads=["kT", "qT"], writes=[("ps", k)])
                            pi = ptr()
                            sch.op("act", lambda e, k=k, pi=pi: e.activation(out=PT[pi][:, 0:GW], in_=bank(k)[:, 0:GW], func=AF.Exp, scale=scale),
                                   reads=[("ps", k)], writes=[("PT", pi)])
                            if j != 1:
                                msk = maskL if j == 0 else maskR
                                sch.op("pool", lambda e, pi=pi, msk=msk: e.tensor_tensor(out=PT[pi][:, 0:GW], in0=PT[pi][:, 0:GW], in1=msk, op=ALU.mult),
                                       reads=[("PT", pi), "cbf"], writes=[("PT", pi)])
                            pts.append((pi, kt))
                        return (kh, n, pts)

                    def att_pv(st):
                        kh, n, pts = st
                        ko, kd = brr(), brr()
                        L = len(pts)
                        for idx, (pi, kt) in enumerate(pts):
                            sch.op("pe", lambda e, pi=pi, kt=kt, idx=idx: e.matmul(bank(ko)[:, 0:GW], lhsT=V[:, kt, kh * 128:(kh + 1) * 128], rhs=PT[pi][:, 0:GW], start=(idx == 0), stop=(idx == L - 1)),
                                   reads=["V", ("PT", pi)], writes=[("ps", ko)], signal=(idx == L - 1))
                        for idx, (pi, kt) in enumerate(pts):
                            sch.op("pe", lambda e, pi=pi, idx=idx: e.matmul(bank(kd)[:, 0:GW], lhsT=ones[:], rhs=PT[pi][:, 0:GW], start=(idx == 0), stop=(idx == L - 1)),
                                   reads=["ones", ("PT", pi)], writes=[("ps", kd)], signal=(idx == L - 1))
                        di = dr()
                        for g in range(G):
                            h = kh * G + g
                            sch.op("dve", lambda e, g=g, h=h: e.tensor_scalar(out=den[di][:, g * 128:(g + 1) * 128], in0=bank(kd)[:, g * 128:(g + 1) * 128],
                                                                         scalar1=sinkexp[:, ja * c.A_HEADS + h:ja * c.A_HEADS + h + 1], scalar2=None, op0=ALU.add),
                                   reads=[("ps", kd), "sinkexp"], writes=[("den", di)])
                        sch.op("dve", lambda e: e.reciprocal(out=den[di][:, 0:GW], in_=den[di][:, 0:GW]), reads=[("den", di)], writes=[("den", di)])
                        sch.op("dve", lambda e: e.tensor_tensor(out=oT[:, kh * G:(kh + 1) * G, n * 128:(n + 1) * 128], in0=bank(ko)[:, 0:GW].rearrange("p (g q) -> p g q", g=G),
                                                                in1=den[di][:, 0:GW].rearrange("p (g q) -> p g q", g=G), op=ALU.mult),
                               reads=[("ps", ko), ("den", di)], writes=["oT"])

                    prev = None
                    for kh in range(KV if "A_noattn" not in DBG else 0):
                        for n in range(4):
                            cur = att_scores(kh, n)
                            if prev is not None:
                                att_pv(prev)
                            prev = cur
                    if prev is not None:
                        att_pv(prev)
                    out_proj(wp, ("a_o", ja), c.A_HEADS, oT, "oT", bin_, bout, seq, blk, xres, xn, brr)
                sch.barrier()

        def rope_a(k, pc0, w, R, npart, tab, tcol, qb, qb2, out_ap, out_res, rc):
            i = rc()
            y1, y2 = qb[i], qb2[i]
            P = slice(0, npart)
            sch.op("dve", lambda e: e.tensor_tensor(out=y1[P, 0:w], in0=bank(k)[P, pc0:pc0 + w], in1=tab[P, 0, tcol:tcol + w], op=ALU.mult),
                   reads=[("ps", k), "tab"], writes=[("qb", i)])
            sch.op("dve", lambda e: e.tensor_tensor(out=y2[P, 0:w], in0=bank(k)[P, pc0:pc0 + w], in1=tab[P, 1, tcol:tcol + w], op=ALU.mult),
                   reads=[("ps", k), "tab"], writes=[("qb2", i)])

            def fin(k2):
                sch.op("pe", lambda e: e.matmul(bank(k2)[P, 0:w], lhsT=identb[P, P], rhs=y1[P, 0:w], start=True, stop=False),
                       reads=[("qb", i), "cbf"], writes=[("ps", k2)], signal=False)
                sch.op("pe", lambda e: e.matmul(bank(k2)[P, 0:w], lhsT=R, rhs=y2[P, 0:w], start=False, stop=True),
                       reads=[("qb2", i), "cbf"], writes=[("ps", k2)])
                sch.op("act", lambda e: e.activation(out=out_ap, in_=bank(k2)[P, 0:w], func=AF.Copy),
                       reads=[("ps", k2)], writes=[out_res])
            return fin

        def rope_cols(k, k2, pc0, w, R, npart, tab, tcol, qb, t1, t2, out_ap, out_res, rc, qb2=None):
            if "oldrope" not in DBG and "norope" not in DBG:
                fin = rope_a(k, pc0, w, R, npart, tab, tcol, qb, qb2, out_ap, out_res, rc)
                fin(k2)
                return
            if "norope" in DBG:
                sch.op("act", lambda e: e.activation(out=out_ap, in_=bank(k)[0:npart, pc0:pc0 + w], func=AF.Copy),
                       reads=[("ps", k)], writes=[out_res])
                return
            i = rc()
            qbt, t1t, t2t = qb[i], t1[i], t2[i]
            P = slice(0, npart)
            sch.op("act", lambda e: e.activation(out=qbt[P, 0:w], in_=bank(k)[P, pc0:pc0 + w], func=AF.Copy),
                   reads=[("ps", k)], writes=[("qb", i)])
            Ruse = R if "ropeones" not in DBG else ones[0:npart, 0:npart]
            if "rope_nomm" in DBG:
                k2 = k
            else:
                sch.op("pe", lambda e: e.matmul(bank(k2)[P, 0:w], lhsT=Ruse, rhs=qbt[P, 0:w], start=True, stop=True),
                       reads=[("qb", i), "cbf"], writes=[("ps", k2)])
            if "rope_nodve" in DBG:
                sch.op("act", lambda e: e.activation(out=out_ap, in_=bank(k2)[P, 0:w], func=AF.Copy),
                       reads=[("ps", k2)], writes=[out_res])
                return
            if "rope_nomul" in DBG:
                sch.op("dve", lambda e: e.memset(t1t[P, 0:w], 1.0), reads=[], writes=[("t1", i)])
                sch.op("dve", lambda e: e.memset(t2t[P, 0:w], 1.0), reads=[], writes=[("t2", i)])
                sch.op("pool", lambda e: e.tensor_tensor(out=out_ap, in0=t1t[P, 0:w], in1=t2t[P, 0:w], op=ALU.add),
                       reads=[("t1", i), ("t2", i), ("ps", k2)], writes=[out_res])
                return
            sch.op("dve", lambda e: e.tensor_tensor(out=t1t[P, 0:w], in0=bank(k)[P, pc0:pc0 + w], in1=tab[P, 0, tcol:tcol + w], op=ALU.mult),
                   reads=[("ps", k), "tab", ("qb", i)], writes=[("t1", i)])
            sch.op("dve", lambda e: e.tensor_tensor(out=t2t[P, 0:w], in0=bank(k2)[P, 0:w], in1=tab[P, 1, tcol:tcol + w], op=ALU.mult),
                   reads=[("ps", k2), "tab"], writes=[("t2", i)])
            if "rope_noadd" in DBG:
                sch.op("act", lambda e: e.activation(out=out_ap, in_=t1t[P, 0:w], func=AF.Copy),
                       reads=[("t1", i), ("t2", i)], writes=[out_res])
                return
            sch.op("pool" if "ropedve" not in DBG else "dve", lambda e: e.tensor_tensor(out=out_ap, in0=t1t[P, 0:w], in1=t2t[P, 0:w], op=ALU.add),
                   reads=[("t1", i), ("t2", i)], writes=[out_res])

        def phase_F(l, seq, bin_, bout):
            W = TB + 2
            NFC = c.NFC
            with ExitStack() as ph:
                def T(name, shape, dt):
                    return SB(ph, name, shape, dt)
                xt = T("f_xt", [128, NDC, W // 2], F32)
                sq = [T("f_sq%d" % i, [128, W // 2], BF16) for i in range(2)]
                rstd = T("f_rstd", [128, W // 2], F32)
                hT = T("f_hT", [128, NDC, W], BF16)
                aT = T("f_aT", [128, NFC, TB], BF16)
                tmp = [T("f_tmp%d" % i, [128, 2, 256], F32) for i in range(3)]
                sg = [T("f_sg%d" % i, [128, 2, 256], F32) for i in range(2)]
                xres = [T("f_xres%d" % i, [128, TB], F32) for i in range(2)]
                xn = [T("f_xn%d" % i, [128, TB], F32) for i in range(2)]
                wpi = WPool(ph, "f_wi", 4, NDC, 128)
                wpo = WPool(ph, "f_wo", 2, NFC, 128)
                prr = rr(3)
                trr = rr(3)
                srr = rr(2)
                orr_state = rr(2)

                def obank():
                    return 6 + orr_state()
                for blk in range(NB):
                    c0 = PAD + blk * TB - 1
                    norm_block((xt, sq, rstd), bin_, seq, c0, W, ("gffn", l), hT, "hT", [6, 7])
                    for i in range(NFC):
                        sgi = None
                        for which in range(2):
                            t = 2 * i + which
                            wt, wres = wpi.load(("f_in", l), t)
                            pi = prr()
                            for sc in range(2):
                                for kc in range(NDC):
                                    sch.op("pe", lambda e, kc=kc, wt=wt, pi=pi, sc=sc: e.matmul(pp[pi][:, sc, 0:258], lhsT=wt[:, kc, :], rhs=hT[:, kc, sc * 256:sc * 256 + 258], start=(kc == 0), stop=(kc == NDC - 1)),
                                           reads=[wres, "hT"], writes=[("pp", pi)], signal=(kc == NDC - 1 and sc == 1))
                            ti = trr()
                            w0, w1, w2, cb = cc(("cw", l, 0), t), cc(("cw", l, 1), t), cc(("cw", l, 2), t), cc(("cb", l), t)
                            sch.op("act", lambda e, ti=ti, pi=pi, w1=w1, cb=cb: e.activation(out=tmp[ti][:], in_=pp[pi][:, :, 1:257], func=AF.Identity, bias=cb, scale=w1),
                                   reads=[("pp", pi), "cst"], writes=[("tmp", ti)])
                            sch.op("dve", lambda e, ti=ti, pi=pi, w0=w0: e.scalar_tensor_tensor(out=tmp[ti][:], in0=pp[pi][:, :, 0:256], scalar=w0, in1=tmp[ti][:], op0=ALU.mult, op1=ALU.add),
                                   reads=[("pp", pi), ("tmp", ti), "cst"], writes=[("tmp", ti)])
                            sch.op("dve", lambda e, ti=ti, pi=pi, w2=w2: e.scalar_tensor_tensor(out=tmp[ti][:], in0=pp[pi][:, :, 2:258], scalar=w2, in1=tmp[ti][:], op0=ALU.mult, op1=ALU.add),
                                   reads=[("pp", pi), ("tmp", ti), "cst"], writes=[("tmp", ti)])
                            if which == 0:
                                sgi = srr()
                                sch.op("act", lambda e, ti=ti, sgi=sgi: e.activation(out=sg[sgi][:], in_=tmp[ti][:], func=AF.Silu),
                                       reads=[("tmp", ti)], writes=[("sg", sgi)])
                            else:
                                sch.op("pool", lambda e, ti=ti, sgi=sgi, i=i: e.tensor_tensor(out=aT[:, i, :].rearrange("p (s q) -> p s q", s=2), in0=tmp[ti][:], in1=sg[sgi][:], op=ALU.mult),
                                       reads=[("tmp", ti), ("sg", sgi)], writes=["aT"])
                    c1 = PAD + blk * TB
                    for m in range(NDC):
                        wt, wres = wpo.load(("f_out", l), m)
                        k = obank()
                        ii = m % 2
                        srcx = xs[bin_][seq][m * 128:(m + 1) * 128, c1:c1 + TB]
                        sch.dma("sp", "f_xres%d" % ii, lambda e, ii=ii, srcx=srcx: e.dma_start(out=xres[ii][:], in_=srcx),
                                reads=[("x", bin_, seq, blk)], writes=[("xres", ii)])
                        for kc in range(NFC):
                            sch.op("pe", lambda e, kc=kc, wt=wt, k=k: e.matmul(bank(k), lhsT=wt[:, kc, :], rhs=aT[:, kc, :], start=(kc == 0), stop=(kc == NFC - 1)),
                                   reads=[wres, "aT"], writes=[("ps", k)], signal=(kc == NFC - 1))
                        sch.op("dve", lambda e, ii=ii, k=k: e.tensor_tensor(out=xn[ii][:], in0=bank(k), in1=xres[ii][:], op=ALU.add),
                               reads=[("ps", k), ("xres", ii)], writes=[("xn", ii)])
                        dst = xs[bout][seq][m * 128:(m + 1) * 128, c1:c1 + TB]
                        sch.dma("pool", "f_xn%d" % ii, lambda e, ii=ii, dst=dst: e.dma_start(out=dst, in_=xn[ii][:]),
                                reads=[("xn", ii)], writes=[("x", bout, seq, blk)])
                sch.barrier()

        def phase_B(l, jb, seq, bin_, bout):
            scale = 192.0 ** -0.5
            KQ, KK = c.QL // 128, c.KVL // 128
            NH = c.B_HEADS
            NT = S // 128
            NQC = S // TB
            with ExitStack() as outer:
                def TO(name, shape, dt):
                    return SB(outer, name, shape, dt)
                cqn = TO("b_cqn", [128, KQ, S], BF16)
                ckvn = TO("b_ckvn", [128, KK, S], BF16)
                krT = TO("b_krT", [128, S], BF16)
                tab = TO("b_tab", [64, 2, S], F32)
                qb = [TO("b_qb%d" % i, [128, 256], BF16) for i in range(2)]
                qb2 = [TO("b_qc%d" % i, [128, 256], BF16) for i in range(2)]
                t1 = [TO("b_t1%d" % i, [128, 256], F32) for i in range(2)]
                t2 = [TO("b_t2%d" % i, [128, 256], F32) for i in range(2)]
                sch.dma("sp", "b_tab", lambda e: e.dma_start(out=tab[:], in_=tabB_d[:, :, :]), writes=["tab"])
                sch.op("dve", lambda e: e.memset(krT[64:128, :], 0.0), writes=["krT_hi"])
                rc = rr(2)
                with ExitStack() as ph:
                    def T(name, shape, dt):
                        return SB(ph, name, shape, dt)
                    xt = T("b1_xt", [128, NDC, TB // 2], F32)
                    sq = [T("b1_sq%d" % i, [128, TB], BF16) for i in range(2)]
                    rstd = T("b1_rstd", [128, TB], F32)
                    hT = T("b1_hT", [128, NDC, TB], BF16)
                    cf = T("b1_cf", [128, max(KQ, KK), TB], F32)
                    wp = WPool(ph, "b1_w", 3, NDC, 128)
                    brr = rr(6)
                    for blk in range(NB):
                        c0 = PAD + blk * TB
                        norm_block((xt, sq, rstd), bin_, seq, c0, TB, ("gmix", l), hT, "hT", [6, 7])
                        for grp, (nk, dst, gk) in enumerate(((KQ, cqn, ("bqn", jb)), (KK, ckvn, ("bkvn", jb)))):
                            for j in range(nk):
                                t = grp * KQ + j
                                wt, wres = wp.load(("b_in", jb), t)
                                k = brr()
                                for kc in range(NDC):
                                    sch.op("pe", lambda e, kc=kc, wt=wt, k=k: e.matmul(bank(k), lhsT=wt[:, kc, :], rhs=hT[:, kc, :], start=(kc == 0), stop=(kc == NDC - 1)),
                                           reads=[wres, "hT"], writes=[("ps", k)], signal=(kc == NDC - 1))
                                sch.op("act", lambda e, k=k, j=j: e.activation(out=cf[:, j, :], in_=bank(k), func=AF.Copy),
                                       reads=[("ps", k)], writes=[("cf", j)])
                                sqt = sq[j % 2]
                                sch.op("act", lambda e, k=k, sqt=sqt: e.activation(out=sqt[:], in_=bank(k), func=AF.Square),
                                       reads=[("ps", k)], writes=[("sq", j % 2)])
                                sch.op("pe", lambda e, j=j, sqt=sqt, nk=nk: e.matmul(bank(7), lhsT=ones[:], rhs=sqt[:], start=(j == 0), stop=(j == nk - 1)),
                                       reads=[("sq", j % 2), "ones"], writes=[("ps", 7)], signal=True)
                            sch.op("act", lambda e, nk=nk: e.activation(out=rstd[:], in_=bank(7), func=AF.Sqrt, bias=epsb[:, 0:1], scale=1.0 / (nk * 128)),
                                   reads=[("ps", 7), "epsb"], writes=["rstd"])
                            sch.op("dve", lambda e: e.reciprocal(out=rstd[:], in_=rstd[:]),
                                   reads=["rstd"], writes=["rstd"])
                            for j in range(nk):
                                sch.op("dve", lambda e, j=j, dst=dst, gk=gk, blk=blk: e.scalar_tensor_tensor(out=dst[:, j, blk * TB:(blk + 1) * TB], in0=cf[:, j, :], scalar=cc(gk, j), in1=rstd[:],
                                                                                                  op0=ALU.mult, op1=ALU.mult),
                                       reads=[("cf", j), "rstd", "cst"], writes=[("lat", grp)])
                        wt, wres = wp.load(("b_inr", jb), 0)
                        k, k2 = brr(), brr()
                        for kc in range(NDC):
                            sch.op("pe", lambda e, kc=kc, wt=wt, k=k: e.matmul(bank(k)[0:64, :], lhsT=wt[:, kc, 0:64], rhs=hT[:, kc, :], start=(kc == 0), stop=(kc == NDC - 1)),
                                   reads=[wres, "hT"], writes=[("ps", k)], signal=(kc == NDC - 1))
                        for hh in range(2):
                            rope_cols(k, k2, hh * 256, 256, RB, 64, tab, blk * TB + hh * 256, qb, t1, t2, krT[0:64, blk * TB + hh * 256:blk * TB + (hh + 1) * 256], "krT", rc, qb2)
                    sch.barrier()
                with ExitStack() as ph:
                    def T(name, shape, dt):
                        return SB(ph, name, shape, dt)
                    qn = T("b2_qn", [128, S], BF16)
                    qr = T("b2_qr", [128, S], BF16)
                    sch.op("dve", lambda e: e.memset(qr[64:128, :], 0.0), writes=["qr"])
                    kn = T("b2_kn", [128, S], BF16)
                    Vh = T("b2_V", [128, NT, 128], BF16)
                    PT = [T("b2_PT%d" % i, [128, TB], BF16) for i in range(4)]
                    rden = [T("b2_rd%d" % i, [128, TB], F32) for i in range(2)]
                    acc = [[T("b2_acc%d%d" % (i, j), [128, TB], F32) for j in range(2)] for i in range(2)]
                    hi = [T("b2_hi%d" % i, [128, TB], BF16) for i in range(2)]
                    lo = [T("b2_lo%d" % i, [128, TB], BF16) for i in range(2)]
                    oh = [T("b2_oh%d" % i, [128, S], BF16) for i in range(2)]
                    wq = WPool(ph, "b2_wqn", 2, KQ, 128)
                    wqr = WPool(ph, "b2_wqr", 2, KQ, 64)
                    wk = WPool(ph, "b2_wkn", 2, KK, 128)
                    wvp = WPool(ph, "b2_wv", 2, KK, 128)
                    srr = rr(4)
                    prr = rr(4)
                    arr = rr(2)
                    for h in range(NH):
                        wqt, wqres = wq.load(("b_qn", jb), h)
                        wqrt, wqrres = wqr.load(("b_qr", jb), h)
                        wkt, wkres = wk.load(("b_kn", jb), h)
                        wvt, wvres = wvp.load(("b_v", jb), h)
                        for qc in range(NQC):
                            cs = slice(qc * TB, (qc + 1) * TB)
                            k = srr()
                            for kc in range(KQ):
                                sch.op("pe", lambda e, kc=kc, k=k, cs=cs, wqt=wqt: e.matmul(bank(k), lhsT=wqt[:, kc, :], rhs=cqn[:, kc, cs], start=(kc == 0), stop=(kc == KQ - 1)),
                                       reads=[wqres, ("lat", 0)], writes=[("ps", k)], signal=(kc == KQ - 1))
                            sch.op("act", lambda e, k=k, cs=cs: e.activation(out=qn[:, cs], in_=bank(k), func=AF.Copy), reads=[("ps", k)], writes=["qn"])
                            k = srr()
                            for kc in range(KK):
                                sch.op("pe", lambda e, kc=kc, k=k, cs=cs, wkt=wkt: e.matmul(bank(k), lhsT=wkt[:, kc, :], rhs=ckvn[:, kc, cs], start=(kc == 0), stop=(kc == KK - 1)),
                                       reads=[wkres, ("lat", 1)], writes=[("ps", k)], signal=(kc == KK - 1))
                            sch.op("dve", lambda e, k=k, cs=cs: e.tensor_copy(out=kn[:, cs], in_=bank(k)), reads=[("ps", k)], writes=["kn"])
                            k, k2 = srr(), srr()
                            for kc in range(KQ):
                                sch.op("pe", lambda e, kc=kc, k=k, cs=cs, wqrt=wqrt: e.matmul(bank(k)[0:64, :], lhsT=wqrt[:, kc, :], rhs=cqn[:, kc, cs], start=(kc == 0), stop=(kc == KQ - 1)),
                                       reads=[wqrres, ("lat", 0)], writes=[("ps", k)], signal=(kc == KQ - 1))
                            for hh in range(2):
                                rope_cols(k, k2, hh * 256, 256, RB, 64, tab, qc * TB + hh * 256, qb, t1, t2, qr[0:64, qc * TB + hh * 256:qc * TB + (hh + 1) * 256], "qr", rc, qb2)
                        for t4 in range(NT // 4):
                            k = srr()
                            for tt in range(4):
                                tk = t4 * 4 + tt
                                for kc in range(KK):
                                    sch.op("pe", lambda e, kc=kc, k=k, tt=tt, tk=tk, wvt=wvt: e.matmul(bank(k)[:, tt * 128:(tt + 1) * 128], lhsT=ckvn[:, kc, tk * 128:(tk + 1) * 128], rhs=wvt[:, kc, :],
                                                                                                 start=(kc == 0), stop=(kc == KK - 1)),
                                           reads=[wvres, ("lat", 1)], writes=[("ps", k)], signal=(kc == KK - 1 and tt == 3))
                            sch.op("act", lambda e, k=k, t4=t4: e.activation(out=Vh[:, t4 * 4:(t4 + 1) * 4, :], in_=bank(k).rearrange("p (t d) -> p t d", t=4), func=AF.Copy),
                                   reads=[("ps", k)], writes=["Vh"])
                        oi = h % 2
                        for qc in range(NQC):
                            cs = slice(qc * TB, (qc + 1) * TB)
                            a = arr()
                            ko, kd = 4 + 2 * a, 5 + 2 * a
                            pend = []

                            def scores(m, cs=cs):
                                k = srr()
                                sch.op("pe", lambda e, k=k, m=m: e.matmul(bank(k), lhsT=kn[:, m * 128:(m + 1) * 128], rhs=qn[:, cs], start=True, stop=False),
                                       reads=["kn", "qn"], writes=[("ps", k)], signal=False)
                                sch.op("pe", lambda e, k=k, m=m: e.matmul(bank(k), lhsT=krT[:, m * 128:(m + 1) * 128], rhs=qr[:, cs], start=False, stop=True),
                                       reads=["krT", "qr"], writes=[("ps", k)])
                                pi = prr()
                                sch.op("act", lambda e, k=k, pi=pi: e.activation(out=PT[pi][:], in_=bank(k), func=AF.Exp, scale=scale),
                                       reads=[("ps", k)], writes=[("PT", pi)])
                                return pi

                            di = qc % 2

                            def pv(m, pi, ko=ko, kd=kd, di=di):
                                sch.op("pe", lambda e: e.matmul(bank(ko), lhsT=Vh[:, m, :], rhs=PT[pi][:], start=(m == 0), stop=(m == NT - 1)),
                                       reads=["Vh", ("PT", pi)], writes=[("ps", ko)], signal=(m == NT - 1))
                                eng = "pool" if m % 2 == 0 else "dve"
                                ac = acc[di][m % 2]
                                if m < 2:
                                    sch.op(eng, lambda e: e.tensor_copy(out=ac[:], in_=PT[pi][:]), reads=[("PT", pi)], writes=[("acc", di, m % 2)])
                                else:
                                    sch.op(eng, lambda e: e.tensor_tensor(out=ac[:], in0=ac[:], in1=PT[pi][:], op=ALU.add),
                                           reads=[("PT", pi), ("acc", di, m % 2)], writes=[("acc", di, m % 2)])
                            LOOK = 2
                            for m in range(NT + LOOK):
                                if m < NT:
                                    pend.append((m, scores(m)))
                                if m >= LOOK:
                                    mm, pi = pend.pop(0)
                                    pv(mm, pi)
                            a0, a1 = acc[di][0], acc[di][1]
                            sch.op("dve", lambda e, a0=a0, a1=a1: e.tensor_tensor(out=a0[:], in0=a0[:], in1=a1[:], op=ALU.add),
                                   reads=[("acc", di, 0), ("acc", di, 1)], writes=[("acc", di, 0)])
                            sch.op("dve", lambda e, a0=a0, di=di: e.tensor_copy(out=hi[di][:], in_=a0[:]), reads=[("acc", di, 0)], writes=[("hi", di)])
                            sch.op("dve", lambda e, a0=a0, di=di: e.tensor_tensor(out=lo[di][:], in0=a0[:], in1=hi[di][:], op=ALU.subtract),
                                   reads=[("acc", di, 0), ("hi", di)], writes=[("lo", di)])
                            sch.op("pe", lambda e, di=di, kd=kd: e.matmul(bank(kd), lhsT=ones[:], rhs=hi[di][:], start=True, stop=False),
                                   reads=["ones", ("hi", di)], writes=[("ps", kd)], signal=False)
                            sch.op("pe", lambda e, di=di, kd=kd: e.matmul(bank(kd), lhsT=ones[:], rhs=lo[di][:], start=False, stop=True),
                                   reads=["ones", ("lo", di)], writes=[("ps", kd)])
                            sch.op("dve", lambda e, di=di, kd=kd: e.reciprocal(out=rden[di][:], in_=bank(kd)), reads=[("ps", kd)], writes=[("rden", di)])
                            sch.op("dve", lambda e, di=di, ko=ko, oi=oi, cs=cs: e.tensor_tensor(out=oh[oi][:, cs], in0=bank(ko), in1=rden[di][:], op=ALU.mult),
                                   reads=[("ps", ko), ("rden", di)], writes=[("oh", oi)])
                        dst = osc[seq][h * 128:(h + 1) * 128, :]
                        sch.dma("pool", "b2_oh%d" % oi, lambda e, oi=oi, dst=dst: e.dma_start(out=dst, in_=oh[oi][:]),
                                reads=[("oh", oi)], writes=[("osc", seq)])
                    sch.barrier()
            with ExitStack() as ph:
                def T(name, shape, dt):
                    return SB(ph, name, shape, dt)
                ot = [T("b3_ot%d" % i, [128, NH, TB], BF16) for i in range(2)]
                xres = [T("b3_xres%d" % i, [128, TB], F32) for i in range(2)]
                xn = [T("b3_xn%d" % i, [128, TB], F32) for i in range(2)]
                wp = WPool(ph, "b3_w", 3, NH, 128)
                brr = rr(8)
                for blk in range(NB):
                    oi = blk % 2
                    src = osc[seq].rearrange("(k p) c -> p k c", p=128)[:, :, blk * TB:(blk + 1) * TB]
                    sch.dma("sp", "b3_ot%d" % oi, lambda e, oi=oi, src=src: e.dma_start(out=ot[oi][:], in_=src),
                            reads=[("osc", seq)], writes=[("ot", oi)])
                    out_proj(wp, ("b_o", jb), NH, ot[oi], ("ot", oi), bin_, bout, seq, blk, xres, xn, brr)
                sch.barrier()

        def phase_Y(seq, bin_):
            with ExitStack() as ph:
                def T(name, shape, dt):
                    return SB(ph, name, shape, dt)
                xt = T("y_xt", [128, NDC, TB // 2], F32)
                sq = [T("y_sq%d" % i, [128, TB // 2], BF16) for i in range(2)]
                rstd = T("y_rstd", [128, TB // 2], F32)
                yT = T("y_yT", [128, NDC, TB], F32)
                ytok = [T("y_tok%d" % i, [128, D], F32) for i in range(2)]
                idt = T("y_idt", [128, 128], F32)
                sch.dma("sp", "y_idt", lambda e: e.dma_start(out=idt[:], in_=id_d[:, :]), writes=["idt"])
                brr = rr(6)
                for blk in range(NB):
                    norm_block((xt, sq, rstd), bin_, seq, PAD + blk * TB, TB, ("gfin", 0), yT, "yT", [6, 7])
                    for tt in range(4):
                        yi = tt % 2
                        for d4 in range(NDC // 4 if NDC >= 4 else 1):
                            nd = min(4, NDC)
                            k = brr()
                            for dd in range(nd):
                                dc = d4 * 4 + dd
                                sch.op("pe", lambda e, k=k, dd=dd, dc=dc, tt=tt: e.transpose(bank(k)[:, dd * 128:(dd + 1) * 128], yT[:, dc, tt * 128:(tt + 1) * 128], idt[:]),
                                       reads=["yT", "idt"], writes=[("ps", k)], signal=(dd == nd - 1))
                            eng = "act" if d4 % 2 == 0 else "dve"
                            if eng == "act":
                                sch.op("act", lambda e, k=k, d4=d4, yi=yi, nd=nd: e.activation(out=ytok[yi][:, d4 * 512:d4 * 512 + nd * 128], in_=bank(k)[:, 0:nd * 128], func=AF.Copy),
                                       reads=[("ps", k)], writes=[("ytok", yi, d4)])
                            else:
                                sch.op("dve", lambda e, k=k, d4=d4, yi=yi, nd=nd: e.tensor_copy(out=ytok[yi][:, d4 * 512:d4 * 512 + nd * 128], in_=bank(k)[:, 0:nd * 128]),
                                       reads=[("ps", k)], writes=[("ytok", yi, d4)])
                        r0 = seq * S + blk * TB + tt * 128
                        o = sch.dma("pool", "y_tok%d" % yi, lambda e, yi=yi, r0=r0: e.dma_start(out=y_out[r0:r0 + 128, :], in_=ytok[yi][:]),
                                    reads=[("ytok", yi, d4) for d4 in range(max(1, NDC // 4))], writes=[])
                        sch.out_tokens.append(o)
                sch.barrier()

        hl = 0
        for l in range(c.DEPTH):
            for seq in range(NSEQ):
                if l % 2 == 0:
                    if "noA" not in DBG:
                        phase_A(l, l // 2, seq, hl % 2, (hl + 1) % 2)
                else:
                    phase_B(l, l // 2, seq, hl % 2, (hl + 1) % 2)
            hl += 1
            for seq in range(NSEQ):
                if "noF" not in DBG:
                    phase_F(l, seq, hl % 2, (hl + 1) % 2)
            hl += 1
        for seq in range(NSEQ):
            phase_Y(seq, hl % 2)
        stuck = sch.simulate()
        assert not stuck, stuck
        print('streams', {e: len(o) for e, o in sch.streams.items()}, 'nsem', len(sch.dsem) + 4, flush=True)
        with nc.Block() as block:
            sch.emit(block)
    return nc


def _prep_common(cfg, inp):
    wl, cl = WLayout(cfg), CLayout(cfg)
    wall = pack_weights(cfg, wl, inp)
    consts = pack_consts(cfg, cl, inp)
    tabA, tabB, cbf, ident = make_tables(cfg)
    d = {"consts": consts, "tabA": tabA, "tabB": tabB, "cbf": cbf, "ident": ident}
    for g, w_ in enumerate(wall):
        d["wall%d" % g] = w_
    return d


def run(cfg, seqs, inp, trace=False):
    nc = build(cfg)
    common = _prep_common(cfg, inp)
    in_maps = []
    for s in seqs:
        m = dict(common)
        m["x_in"] = np.ascontiguousarray(s.reshape(cfg.NSEQ * cfg.S, cfg.D))
        in_maps.append(m)
    res = run_bass_kernel_spmd(nc, in_maps, core_ids=list(range(len(seqs))), trace=trace)
    outs = [np.asarray(r["y"]).reshape(cfg.NSEQ, cfg.S, cfg.D) for r in res.results]
    return outs, res


def kernel(x_prompt, x_sample, **w):
    cfg = Cfg()
    xp = np.asarray(x_prompt, np.float32)
    xsm = np.asarray(x_sample, np.float32)
    zero = np.zeros_like(xsm[0])
    seqs = [np.stack([xp[2 * i], xp[2 * i + 1]]) for i in range(4)] + [np.stack([xsm[i], zero]) for i in range(4)]
    outs, _ = run(cfg, seqs, w)
    y_prompt = np.concatenate([outs[i] for i in range(4)], axis=0).astype(np.float32)
    y_sample = np.stack([outs[4 + i][0] for i in range(4)], axis=0).astype(np.float32)
    return (y_prompt, y_sample)
```

```python
from contextlib import ExitStack
import numpy as np
import ml_dtypes
import concourse.bass as bass
import concourse.mybir as mybir
from concourse.bass_utils import run_bass_kernel_spmd

F32 = mybir.dt.float32
BF16 = mybir.dt.bfloat16
AF = mybir.ActivationFunctionType
ALU = mybir.AluOpType
DBG = set()
PAD = 128
TB = 512
EPS = 1e-6
THETA = 10000.0


class Cfg:
    def __init__(self, D=2048, S=4096, NSEQ=2, DEPTH=4, A_KV=4, B_HEADS=16, QL=512, KVL=512, DFF=5632):
        self.D, self.S, self.NSEQ, self.DEPTH = D, S, NSEQ, DEPTH
        self.A_HEADS = D // 128
        self.A_KV = A_KV
        self.G = self.A_HEADS // A_KV
        self.B_HEADS, self.QL, self.KVL, self.DFF = B_HEADS, QL, KVL, DFF
        self.NDC = D // 128
        self.NFC = DFF // 128
        self.LA = (DEPTH + 1) // 2
        self.LB = DEPTH // 2
        self.SP = S + 2 * PAD
        self.NB = S // TB


def _tiles(W, mt):
    din, dout = W.shape
    kc, nm = din // 128, dout // mt
    return np.ascontiguousarray(W.reshape(kc, 128, nm, mt).transpose(2, 1, 0, 3))


class WLayout:
    def __init__(self, cfg):
        c = cfg
        self.off = {}
        self.NG = max(1, c.DEPTH)
        self.total = [0] * self.NG
        kd, kq, kv, kf = c.NDC, c.QL // 128, c.KVL // 128, c.NFC
        for j in range(c.LA):
            g = 2 * j
            self._add(g, ("a_qk", j), c.A_HEADS + c.A_KV, kd, 128)
            self._add(g, ("a_v", j), 1, kd, c.A_KV * 128)
            self._add(g, ("a_o", j), c.NDC, c.A_HEADS, 128)
        for j in range(c.LB):
            g = 2 * j + 1
            self._add(g, ("b_in", j), (c.QL + c.KVL) // 128, kd, 128)
            self._add(g, ("b_inr", j), 1, kd, 64)
            self._add(g, ("b_qn", j), c.B_HEADS, kq, 128)
            self._add(g, ("b_qr", j), c.B_HEADS, kq, 64)
            self._add(g, ("b_kn", j), c.B_HEADS, kv, 128)
            self._add(g, ("b_v", j), c.B_HEADS, kv, 128)
            self._add(g, ("b_o", j), c.NDC, c.B_HEADS, 128)
        for l in range(c.DEPTH):
            self._add(l, ("f_in", l), 2 * c.NFC, kd, 128)
            self._add(l, ("f_out", l), c.NDC, kf, 128)
        self.CW = 2048
        self.rows = [max(8, -(-(-(-t // self.CW)) // 8) * 8) for t in self.total]

    def _add(self, g, key, nt, kc, mt):
        self.off[key] = (g, self.total[g], nt, kc, mt)
        self.total[g] += nt * 128 * kc * mt

    def tile(self, key, t):
        g, off, nt, kc, mt = self.off[key]
        return g, off + t * 128 * kc * mt, kc, mt


def pack_weights(cfg, wl, inp):
    c = cfg
    flats = [np.zeros(r * wl.CW, np.float32) for r in wl.rows]

    def put(key, arr):
        g, off, nt, kc, mt = wl.off[key]
        assert arr.shape == (nt, 128, kc, mt), (key, arr.shape, (nt, 128, kc, mt))
        flats[g][off:off + arr.size] = arr.reshape(-1)

    qd, kd = c.A_HEADS * 128, c.A_KV * 128
    for j in range(c.LA):
        w = np.asarray(inp["a_w_qkv"][j])
        put(("a_qk", j), _tiles(w[:, :qd + kd], 128))
        put(("a_v", j), _tiles(w[:, qd + kd:], kd))
        put(("a_o", j), _tiles(np.asarray(inp["a_w_o"][j]), 128))
    for j in range(c.LB):
        w = np.asarray(inp["b_w_in"][j])
        put(("b_in", j), _tiles(w[:, :c.QL + c.KVL], 128))
        put(("b_inr", j), _tiles(w[:, c.QL + c.KVL:], 64))
        w = np.asarray(inp["b_w_q_up"][j]).reshape(c.QL, c.B_HEADS, 192)
        put(("b_qn", j), _tiles(np.ascontiguousarray(w[:, :, :128]).reshape(c.QL, -1), 128))
        put(("b_qr", j), _tiles(np.ascontiguousarray(w[:, :, 128:]).reshape(c.QL, -1), 64))
        w = np.asarray(inp["b_w_kv_up"][j]).reshape(c.KVL, c.B_HEADS, 256)
        put(("b_kn", j), _tiles(np.ascontiguousarray(w[:, :, :128]).reshape(c.KVL, -1), 128))
        put(("b_v", j), _tiles(np.ascontiguousarray(w[:, :, 128:]).reshape(c.KVL, -1), 128))
        put(("b_o", j), _tiles(np.asarray(inp["b_w_o"][j]), 128))
    for l in range(c.DEPTH):
        w = np.asarray(inp["f_w_in"][l])
        t = _tiles(w, 128)
        inter = np.empty_like(t)
        inter[0::2] = t[:c.NFC]
        inter[1::2] = t[c.NFC:]
        put(("f_in", l), inter)
        put(("f_out", l), _tiles(np.asarray(inp["f_w_out"][l]), 128))
    return [f.reshape(r, wl.CW) for f, r in zip(flats, wl.rows)]


class CLayout:
    def __init__(self, cfg):
        c = cfg
        self.off = {}
        n = 0
        for key, w in ([(("gmix", l), c.NDC) for l in range(c.DEPTH)] + [(("gffn", l), c.NDC) for l in range(c.DEPTH)]
                       + [(("gfin", 0), c.NDC)]
                       + [(("bqn", j), c.QL // 128) for j in range(c.LB)] + [(("bkvn", j), c.KVL // 128) for j in range(c.LB)]
                       + [(("cw", l, k), 2 * c.NFC) for l in range(c.DEPTH) for k in range(3)]
                       + [(("cb", l), 2 * c.NFC) for l in range(c.DEPTH)]
                       + [(("sink", j), c.A_HEADS) for j in range(c.LA)]):
            self.off[key] = n
            n += w
        self.n = n


def pack_consts(cfg, cl, inp):
    c = cfg
    out = np.zeros((128, cl.n), np.float32)

    def colmajor(v):
        v = np.asarray(v, np.float32)
        return v.reshape(-1, 128).T

    def inter(v):
        t = colmajor(v)
        o = np.empty_like(t)
        o[:, 0::2] = t[:, :c.NFC]
        o[:, 1::2] = t[:, c.NFC:]
        return o

    for l in range(c.DEPTH):
        out[:, cl.off[("gmix", l)]:][:, :c.NDC] = colmajor(inp["norm_mix"][l])
        out[:, cl.off[("gffn", l)]:][:, :c.NDC] = colmajor(inp["norm_ffn"][l])
        for k in range(3):
            out[:, cl.off[("cw", l, k)]:][:, :2 * c.NFC] = inter(inp["f_conv_w"][l][k])
        out[:, cl.off[("cb", l)]:][:, :2 * c.NFC] = inter(inp["f_conv_b"][l])
    out[:, cl.off[("gfin", 0)]:][:, :c.NDC] = colmajor(inp["norm_final"])
    for j in range(c.LB):
        out[:, cl.off[("bqn", j)]:][:, :c.QL // 128] = colmajor(inp["b_q_norm"][j])
        out[:, cl.off[("bkvn", j)]:][:, :c.KVL // 128] = colmajor(inp["b_kv_norm"][j])
    for j in range(c.LA):
        out[:, cl.off[("sink", j)]:][:, :c.A_HEADS] = np.broadcast_to(np.asarray(inp["a_sink"][j], np.float32)[None, :], (128, c.A_HEADS))
    return out


def make_tables(cfg):
    c = cfg
    f32 = np.float32

    def tab(dim, pos):
        inv = (f32(1.0) / (f32(THETA) ** (np.arange(0, dim, 2, dtype=f32) / f32(dim)))).astype(f32)
        ang = (pos.astype(f32)[None, :] * inv[:, None]).astype(f32)
        cs, sn = np.cos(ang).astype(f32), np.sin(ang).astype(f32)
        return np.stack([np.concatenate([cs, cs], 0), np.concatenate([sn, -sn], 0)], 1)

    posA = np.arange(-PAD, c.S + PAD)
    tabA = np.ascontiguousarray(tab(128, posA))
    tabB = np.ascontiguousarray(tab(64, np.arange(c.S)))
    G = c.G
    cbf = np.zeros((128, 128 + 64 + 2 * G * 128 + 128), np.float32)
    cbf[:, 192 + 2 * G * 128:] = np.eye(128, dtype=np.float32)
    for m in range(128):
        cbf[(m + 64) % 128, m] = 1.0
    for m in range(64):
        cbf[(m + 32) % 64, 128 + m] = 1.0
    b = np.arange(128)[:, None]
    a = np.arange(128)[None, :]
    mL = (a <= b).astype(np.float32)
    mR = (b <= a).astype(np.float32)
    cbf[:, 192:192 + G * 128] = np.tile(mL, (1, G))
    cbf[:, 192 + G * 128:192 + 2 * G * 128] = np.tile(mR, (1, G))
    return tabA, tabB, cbf.astype(ml_dtypes.bfloat16), np.eye(128, dtype=np.float32)


class Sched:
    CE = ("pe", "act", "dve", "pool")

    def __init__(self, nc, stack):
        self.nc = nc
        self.stack = stack
        self.streams = {e: [] for e in self.CE + ("sp",)}
        self.sem = {e: stack.enter_context(nc.semaphore("tl_" + e)) for e in self.CE}
        self.cnt = {e: 0 for e in self.CE}
        self.dsem = {}
        self.waited = {e: {} for e in self.streams}
        self.lastw = {}
        self.readers = {}
        self.out_tokens = []

    def _wait(self, o, d):
        if d is None or d.get("tok") is None:
            return
        if d["eng"] == "pe" and o["eng"] == "pe" and not d.get("dma") and not o.get("dma"):
            return
        name, sem, v = d["tok"]
        w = self.waited[o["eng"]]
        if w.get(name, 0) >= v:
            return
        w[name] = v
        o["waits"].append((sem, v))

    def _track(self, o, reads, writes):
        deps = []
        for r in reads:
            if r in self.lastw:
                deps.append(self.lastw[r])
        for r in writes:
            if r in self.lastw:
                deps.append(self.lastw[r])
            deps.extend(self.readers.get(r, ()))
        for d in deps:
            if d is not o:
                self._wait(o, d)
        for r in reads:
            self.readers.setdefault(r, []).append(o)
        for r in writes:
            self.lastw[r] = o
            self.readers[r] = []

    def op(self, eng, fn, reads=(), writes=(), signal=True):
        o = {"eng": eng, "fn": fn, "waits": [], "tok": None, "inc": None}
        self._track(o, reads, writes)
        if signal:
            self.cnt[eng] += 1
            o["tok"] = ("tl_" + eng, self.sem[eng], self.cnt[eng])
            o["inc"] = (self.sem[eng], 1)
        else:
            assert eng == "pe"
            o["tok"] = ("tl_" + eng, self.sem[eng], self.cnt[eng] + 1)
        self.streams[eng].append(o)
        return o

    def dma(self, q, key, fn, reads=(), writes=(), is_out=False):
        if key not in self.dsem:
            self.dsem[key] = [self.stack.enter_context(self.nc.semaphore("d_" + key)), 0]
        ent = self.dsem[key]
        o = {"eng": q, "fn": fn, "waits": [], "dma": True}
        self._track(o, reads, writes)
        ent[1] += 16
        o["tok"] = ("d_" + key, ent[0], ent[1])
        o["inc"] = (ent[0], 16)
        self.streams[q].append(o)
        return o

    def barrier(self):
        toks = [("tl_" + e, self.sem[e], self.cnt[e]) for e in self.CE if self.cnt[e] > 0]
        toks += [("d_" + k, v[0], v[1]) for k, v in self.dsem.items() if v[1] > 0 and not k.startswith("cast")]
        for e in self.streams:
            o = {"eng": e, "fn": None, "waits": [], "tok": None, "inc": None}
            w = self.waited[e]
            for name, sem, v in toks:
                if name == "tl_" + e:
                    continue
                if w.get(name, 0) < v:
                    w[name] = v
                    o["waits"].append((sem, v))
            self.streams[e].append(o)
        self.lastw = {k: v for k, v in self.lastw.items() if isinstance(k, tuple) and k[0] == "wbf"}
        self.readers = {}

    def simulate(self):
        val = {}
        ptr = {e: 0 for e in self.streams}
        prog = True
        while prog:
            prog = False
            for e, ops in self.streams.items():
                while ptr[e] < len(ops):
                    o = ops[ptr[e]]
                    if any(val.get(id(sem), 0) < v for sem, v in o["waits"]):
                        break
                    if o.get("inc") is not None and o["fn"] is not None:
                        val[id(o["inc"][0])] = val.get(id(o["inc"][0]), 0) + o["inc"][1]
                    ptr[e] += 1
                    prog = True
        stuck = {e: (ptr[e], len(ops)) for e, ops in self.streams.items() if ptr[e] < len(ops)}
        return stuck

    def emit(self, block):
        nc = self.nc
        engmap = {"pe": "tensor", "act": "scalar", "dve": "vector", "pool": "gpsimd", "sp": "sync"}
        for e, ops in self.streams.items():
            def body(eng, ops=ops):
                for o in ops:
                    for sem, v in o["waits"]:
                        eng.wait_ge(sem, v)
                    if o["fn"] is None:
                        continue
                    ins = o["fn"](eng)
                    if o["inc"] is not None:
                        ins.then_inc(o["inc"][0], o["inc"][1])
            getattr(block, engmap[e])(body)


def build(cfg):
    c = cfg
    wl, cl = WLayout(c), CLayout(c)
    D, S, NSEQ, NDC, SPW, NB, G = c.D, c.S, c.NSEQ, c.NDC, c.SP, c.NB, c.G
    nc = bass.Bass("TRN2", target_bir_lowering=False)
    x_in = nc.dram_tensor("x_in", [NSEQ * S, D], F32, kind="ExternalInput").ap()
    wall = [nc.dram_tensor("wall%d" % g, [wl.rows[g], wl.CW], F32, kind="ExternalInput").ap() for g in range(wl.NG)]
    cst_d = nc.dram_tensor("consts", [128, cl.n], F32, kind="ExternalInput").ap()
    tabA_d = nc.dram_tensor("tabA", [128, 2, SPW], F32, kind="ExternalInput").ap()
    tabB_d = nc.dram_tensor("tabB", [64, 2, S], F32, kind="ExternalInput").ap()
    NCB = 192 + 2 * G * 128 + 128
    cbf_d = nc.dram_tensor("cbf", [128, NCB], BF16, kind="ExternalInput").ap()
    id_d = nc.dram_tensor("ident", [128, 128], F32, kind="ExternalInput").ap()
    y_out = nc.dram_tensor("y", [NSEQ * S, D], F32, kind="ExternalOutput").ap()
    wbf = [nc.dram_tensor("wbf%d" % g, [wl.rows[g], wl.CW], BF16, kind="Internal").ap() for g in range(wl.NG)]
    xs = [nc.dram_tensor("xs%d" % i, [NSEQ, D, SPW], F32, kind="Internal").ap() for i in range(2)]
    DO = c.B_HEADS * 128
    osc = nc.dram_tensor("osc", [NSEQ, DO, S], BF16, kind="Internal").ap()
    wflat = [w_.rearrange("r c -> (r c)") for w_ in wbf]

    def xview(buf, seq, c0, w):
        return xs[buf][seq].rearrange("(k p) c -> p k c", p=128)[:, :, c0:c0 + w]

    uid = [0]

    def SB(stack, name, shape, dt):
        uid[0] += 1
        return stack.enter_context(nc.sbuf_tensor("%s_%d" % (name, uid[0]), shape, dt))

    with ExitStack() as top:
        sch = Sched(nc, top)
        cst = top.enter_context(nc.sbuf_tensor("cst", [128, cl.n], F32))
        cbf = top.enter_context(nc.sbuf_tensor("cbf_s", [128, NCB], BF16))
        ones = top.enter_context(nc.sbuf_tensor("ones", [128, 128], BF16))
        epsb = top.enter_context(nc.sbuf_tensor("epsb", [128, 1], F32))
        sinkexp = top.enter_context(nc.sbuf_tensor("sinkexp", [128, max(1, c.LA) * c.A_HEADS], F32))
        pp = [top.enter_context(nc.psum_tensor("pp%d" % i, [128, 2, 512], F32)) for i in range(4)]
        RA = cbf[:, 0:128]
        RB = cbf[0:64, 128:192]
        maskL = cbf[:, 192:192 + G * 128]
        maskR = cbf[:, 192 + G * 128:192 + 2 * G * 128]
        identb = cbf[:, 192 + 2 * G * 128:192 + 2 * G * 128 + 128]

        def bank(k):
            return pp[k // 2][:, k % 2, :]

        def cc(key, i=0, n=1):
            o = cl.off[key]
            return cst[:, o + i:o + i + n]

        sch.dma("sp", "cst", lambda e: e.dma_start(out=cst[:], in_=cst_d[:, :]), writes=["cst"])
        sch.dma("sp", "cbf", lambda e: e.dma_start(out=cbf[:], in_=cbf_d[:, :]), writes=["cbf"])
        sch.op("dve", lambda e: e.memset(ones[:], 1.0), writes=["ones"])
        sch.op("dve", lambda e: e.memset(epsb[:], EPS), writes=["epsb"])
        if c.LA > 0:
            o0 = cl.off[("sink", 0)]
            sch.op("act", lambda e: e.activation(out=sinkexp[:], in_=cst[:, o0:o0 + c.LA * c.A_HEADS], func=AF.Exp),
                   reads=["cst"], writes=["sinkexp"])

        class WPool:
            def __init__(self, st, name, n, kc, mt):
                self.name, self.n, self.i = name, n, 0
                self.tiles = [SB(st, "%s%d" % (name, i), [128, kc, mt], BF16) for i in range(n)]

            def load(self, key, t):
                g_, off, kc, mt = wl.tile(key, t)
                i = self.i % self.n
                self.i += 1
                tl = self.tiles[i]
                src = wflat[g_][off:off + 128 * kc * mt].rearrange("(p k m) -> p k m", p=128, k=kc)
                res = (self.name, i)
                sch.dma("sp", "%s%d" % (self.name, i), lambda e: e.dma_start(out=tl[:, 0:kc, 0:mt], in_=src), reads=[("wbf", g_)], writes=[res])
                return tl, res

        def rr(n):
            st = {"i": -1}

            def nxt():
                st["i"] = (st["i"] + 1) % n
                return st["i"]
            return nxt

        def norm_pre(xt, sq, rstd, W, gkey, hT, hres, stat_banks):
            nsub = 2
            w = W // nsub
            for sc in range(nsub):
                sb = stat_banks[sc % len(stat_banks)]
                for dc in range(NDC):
                    sqt = sq[dc % 2]
                    sch.op("act", lambda e, dc=dc, sqt=sqt, sc=sc: e.activation(out=sqt[:, 0:w], in_=xt[:, dc, sc * w:(sc + 1) * w], func=AF.Square),
                           reads=[("xt", sc)], writes=[("sq", dc % 2)])
                    sch.op("pe", lambda e, dc=dc, sqt=sqt, sb=sb: e.matmul(bank(sb)[:, 0:w], lhsT=ones[:], rhs=sqt[:, 0:w], start=(dc == 0), stop=(dc == NDC - 1)),
                           reads=[("sq", dc % 2), "ones"], writes=[("ps", sb)], signal=True)
                sch.op("act", lambda e, sb=sb: e.activation(out=rstd[:, 0:w], in_=bank(sb)[:, 0:w], func=AF.Sqrt, bias=epsb[:, 0:1], scale=1.0 / D),
                       reads=[("ps", sb), "epsb"], writes=["rstd"])
                sch.op("dve", lambda e: e.reciprocal(out=rstd[:, 0:w], in_=rstd[:, 0:w]),
                       reads=["rstd"], writes=["rstd"])
                for dc in range(NDC):
                    sch.op("dve", lambda e, dc=dc, sc=sc: e.scalar_tensor_tensor(out=hT[:, dc, sc * w:(sc + 1) * w], in0=xt[:, dc, sc * w:(sc + 1) * w], scalar=cc(gkey, dc),
                                                                                 in1=rstd[:, 0:w], op0=ALU.mult, op1=ALU.mult),
                           reads=[("xt", sc), "rstd", "cst"], writes=[hres])

        def norm_block(st_tiles, buf, seq, c0, W, gkey, hT, hres, stat_banks):
            xt, sq, rstd = st_tiles
            nsub = 2
            w = W // nsub
            assert w * nsub == W and w <= 384
            for sc in range(nsub):
                src = xview(buf, seq, c0 + sc * w, w)
                blocks = sorted(set([min(max((c0 + sc * w - PAD) // TB, 0), NB - 1), min(max((c0 + sc * w + w - 1 - PAD) // TB, 0), NB - 1)]))
                sch.dma("sp", "xt", lambda e, src=src: e.dma_start(out=xt[:, :, 0:w], in_=src),
                        reads=[("x", buf, seq, b) for b in blocks], writes=["xt"])
                sb = stat_banks[sc % len(stat_banks)]
                for dc in range(NDC):
                    sqt = sq[dc % 2]
                    sch.op("act", lambda e, dc=dc, sqt=sqt: e.activation(out=sqt[:, 0:w], in_=xt[:, dc, 0:w], func=AF.Square),
                           reads=["xt"], writes=[("sq", dc % 2)])
                    sch.op("pe", lambda e, dc=dc, sqt=sqt, sb=sb: e.matmul(bank(sb)[:, 0:w], lhsT=ones[:], rhs=sqt[:, 0:w], start=(dc == 0), stop=(dc == NDC - 1)),
                           reads=[("sq", dc % 2), "ones"], writes=[("ps", sb)], signal=True)
                sch.op("act", lambda e, sb=sb: e.activation(out=rstd[:, 0:w], in_=bank(sb)[:, 0:w], func=AF.Sqrt, bias=epsb[:, 0:1], scale=1.0 / D),
                       reads=[("ps", sb), "epsb"], writes=["rstd"])
                sch.op("dve", lambda e: e.reciprocal(out=rstd[:, 0:w], in_=rstd[:, 0:w]),
                       reads=["rstd"], writes=["rstd"])
                for dc in range(NDC):
                    sch.op("dve", lambda e, dc=dc, sc=sc: e.scalar_tensor_tensor(out=hT[:, dc, sc * w:(sc + 1) * w], in0=xt[:, dc, 0:w], scalar=cc(gkey, dc),
                                                                                 in1=rstd[:, 0:w], op0=ALU.mult, op1=ALU.mult),
                           reads=["xt", "rstd", "cst"], writes=[hres])

        def rope(ps_k, rot_k, w, R, npart, tab, tcol, qb, t1, t2, out_ap, out_res, rc):
            i = rc()
            qbt, t1t, t2t = qb[i], t1[i], t2[i]
            P = slice(0, npart)
            sch.op("act", lambda e: e.activation(out=qbt[P, 0:w], in_=bank(ps_k)[P, 0:w], func=AF.Copy),
                   reads=[("ps", ps_k)], writes=[("qb", i)])
            sch.op("pe", lambda e: e.matmul(bank(rot_k)[P, 0:w], lhsT=R, rhs=qbt[P, 0:w], start=True, stop=True),
                   reads=[("qb", i), "cbf"], writes=[("ps", rot_k)])
            sch.op("dve", lambda e: e.tensor_tensor(out=t1t[P, 0:w], in0=bank(ps_k)[P, 0:w], in1=tab[P, 0, tcol:tcol + w], op=ALU.mult),
                   reads=[("ps", ps_k), "tab"], writes=[("t1", i)])
            sch.op("dve", lambda e: e.tensor_tensor(out=t2t[P, 0:w], in0=bank(rot_k)[P, 0:w], in1=tab[P, 1, tcol:tcol + w], op=ALU.mult),
                   reads=[("ps", rot_k), "tab"], writes=[("t2", i)])
            sch.op("pool", lambda e: e.tensor_tensor(out=out_ap, in0=t1t[P, 0:w], in1=t2t[P, 0:w], op=ALU.add),
                   reads=[("t1", i), ("t2", i)], writes=[out_res])

        def out_proj(wp, wkey, nkc, act_tile, act_res, xin_buf, xout_buf, seq, blk, xres, xn, bankrr):
            c0 = PAD + blk * TB
            for m in range(NDC):
                wt, wres = wp.load(wkey, m)
                k = bankrr()
                i = m % 2
                srcx = xs[xin_buf][seq][m * 128:(m + 1) * 128, c0:c0 + TB]
                sch.dma("sp", "xres%d" % i, lambda e, i=i, srcx=srcx: e.dma_start(out=xres[i][:], in_=srcx),
                        reads=[("x", xin_buf, seq, blk)], writes=[("xres", i)])
                for kc in range(nkc):
                    sch.op("pe", lambda e, kc=kc, wt=wt, k=k: e.matmul(bank(k), lhsT=wt[:, kc, 0:128], rhs=act_tile[:, kc, :], start=(kc == 0), stop=(kc == nkc - 1)),
                           reads=[wres, act_res], writes=[("ps", k)], signal=(kc == nkc - 1))
                sch.op("dve", lambda e, i=i, k=k: e.tensor_tensor(out=xn[i][:], in0=bank(k), in1=xres[i][:], op=ALU.add),
                       reads=[("ps", k), ("xres", i)], writes=[("xn", i)])
                dst = xs[xout_buf][seq][m * 128:(m + 1) * 128, c0:c0 + TB]
                sch.dma("pool", "xn%d" % i, lambda e, i=i, dst=dst: e.dma_start(out=dst, in_=xn[i][:]),
                        reads=[("xn", i)], writes=[("x", xout_buf, seq, blk)])

        with ExitStack() as ph:
            xtok = SB(ph, "xtok", [128, 4, D], F32)
            xst = ph.enter_context(nc.sbuf_tensor("xst", [128, NDC, TB], F32))
            zt = ph.enter_context(nc.sbuf_tensor("zt", [128, NDC, PAD], F32))
            idt = ph.enter_context(nc.sbuf_tensor("idt", [128, 128], F32))
            sch.dma("sp", "idt", lambda e: e.dma_start(out=idt[:], in_=id_d[:, :]), writes=["idt"])
            def cast_group(g_):
                r0 = 0
                while r0 < wl.rows[g_]:
                    r1 = min(wl.rows[g_], r0 + 1024)
                    last = r1 >= wl.rows[g_]
                    sch.dma("pool", "cast%d" % g_, lambda e, r0=r0, r1=r1, g_=g_: e.dma_start(out=wbf[g_][r0:r1, :], in_=wall[g_][r0:r1, :]),
                            writes=([("wbf", g_)] if last else []))
                    r0 = r1
            cast_group(0)
            sch.op("dve", lambda e: e.memset(zt[:], 0.0), writes=["zt"])
            for b in range(2):
                for seq in range(NSEQ):
                    for side in range(2):
                        dst = xview(b, seq, 0 if side == 0 else PAD + S, PAD)
                        sch.dma("pool", "zpad", lambda e, dst=dst: e.dma_start(out=dst, in_=zt[:]), reads=["zt"])
            brr = rr(8)
            for seq in range(NSEQ):
                for blk in range(NB):
                    rbase = seq * S + blk * TB
                    src = x_in[rbase:rbase + TB, :].rearrange("(t p) d -> p t d", p=128)
                    sch.dma("sp", "xtok", lambda e, src=src: e.dma_start(out=xtok[:], in_=src), writes=["xtok"])
                    for dc in range(NDC):
                        k = brr()
                        for tt in range(4):
                            sch.op("pe", lambda e, k=k, tt=tt, dc=dc: e.transpose(bank(k)[:, tt * 128:(tt + 1) * 128], xtok[:, tt, dc * 128:(dc + 1) * 128], idt[:]),
                                   reads=["xtok", "idt"], writes=[("ps", k)], signal=(tt == 3))
                        if dc % 2 == 0:
                            sch.op("act", lambda e, k=k, dc=dc: e.activation(out=xst[:, dc, :], in_=bank(k), func=AF.Copy),
                                   reads=[("ps", k)], writes=[("xst", dc)])
                        else:
                            sch.op("dve", lambda e, k=k, dc=dc: e.tensor_copy(out=xst[:, dc, :], in_=bank(k)),
                                   reads=[("ps", k)], writes=[("xst", dc)])
                    dst = xview(0, seq, PAD + blk * TB, TB)
                    sch.dma("pool", "xst", lambda e, dst=dst: e.dma_start(out=dst, in_=xst[:]),
                            reads=[("xst", dc) for dc in range(NDC)], writes=[("x", 0, seq, blk)])
            for g_ in range(1, wl.NG):
                cast_group(g_)
            sch.barrier()

        def phase_A(l, ja, seq, bin_, bout):
            W = TB + 2 * PAD
            scale = 128.0 ** -0.5
            KV = c.A_KV
            GW = G * 128
            with ExitStack() as ph:
                def T(name, shape, dt):
                    return SB(ph, name, shape, dt)
                xt = T("a_xt", [128, NDC, W], F32)
                sq = [T("a_sq%d" % i, [128, W // 2], BF16) for i in range(2)]
                rstd = T("a_rstd", [128, W // 2], F32)
                hT = T("a_hT", [128, NDC, W], BF16)
                tab = T("a_tab", [128, 2, W], F32)
                qT = T("a_qT", [128, c.A_HEADS, TB], BF16)
                kT = T("a_kT", [128, KV, W], BF16)
                V = T("a_V", [128, W // 128, KV * 128], BF16)
                qb = [T("a_qb%d" % i, [128, 384], BF16) for i in range(4)]
                qb2 = [T("a_qc%d" % i, [128, 384], BF16) for i in range(4)]
                t1 = t2 = None
                PT = [T("a_PT%d" % i, [128, TB], BF16) for i in range(6)]
                den = [T("a_den%d" % i, [128, TB], F32) for i in range(2)]
                oT = T("a_oT", [128, c.A_HEADS, TB], BF16)
                xres = [T("a_xres%d" % i, [128, TB], F32) for i in range(2)]
                xn = [T("a_xn%d" % i, [128, TB], F32) for i in range(2)]
                wp = WPool(ph, "a_w", 3, NDC, 128)
                wv = T("a_wv", [128, NDC, KV * 128], BF16)
                vg, voff, vkc, vmt = wl.tile(("a_v", ja), 0)
                sch.dma("sp", "a_wv", lambda e: e.dma_start(out=wv[:], in_=wflat[vg][voff:voff + 128 * vkc * vmt].rearrange("(p k m) -> p k m", p=128, k=vkc)),
                        reads=[("wbf", vg)], writes=["wv"])
                brr = rr(8)
                rc = rr(4)
                ptr = rr(6)
                dr = rr(2)
                def load_x(blk):
                    for sc in range(2):
                        cs0 = blk * TB + sc * (W // 2)
                        src = xview(bin_, seq, cs0, W // 2)
                        blocks = sorted(set([min(max((cs0 - PAD) // TB, 0), NB - 1), min(max((cs0 + W // 2 - 1 - PAD) // TB, 0), NB - 1)]))
                        sch.dma("pool", "a_xt%d" % sc, lambda e, src=src, sc=sc: e.dma_start(out=xt[:, :, sc * (W // 2):(sc + 1) * (W // 2)], in_=src),
                                reads=[("x", bin_, seq, b_) for b_ in blocks], writes=[("xt", sc)])
                load_x(0)
                for blk in range(NB):
                    c0 = blk * TB
                    sch.dma("sp", "a_tab", lambda e, c0=c0: e.dma_start(out=tab[:], in_=tabA_d[:, :, c0:c0 + W]), writes=["tab"])
                    norm_pre(xt, sq, rstd, W, ("gmix", l), hT, "hT", [brr(), brr()])
                    if blk + 1 < NB:
                        load_x(blk + 1)
                    pend_rope = []
                    for t in range(0 if "A_noqk" not in DBG else 0, (c.A_HEADS + KV) if "A_noqk" not in DBG else 0):
                        wt, wres = wp.load(("a_qk", ja), t)
                        isq = t < c.A_HEADS
                        subs = [(PAD, TB)] if isq else [(0, W // 2), (W // 2, W // 2)]
                        for (s0, sw) in subs:
                            k = brr()
                            for kc in range(NDC):
                                sch.op("pe", lambda e, kc=kc, wt=wt, k=k, s0=s0, sw=sw: e.matmul(bank(k)[:, 0:sw], lhsT=wt[:, kc, :], rhs=hT[:, kc, s0:s0 + sw], start=(kc == 0), stop=(kc == NDC - 1)),
                                       reads=[wres, "hT"], writes=[("ps", k)], signal=(kc == NDC - 1))
                            for f in pend_rope:
                                f(brr())
                            pend_rope = []
                            if isq:
                                for hh in range(2):
                                    o_ap = qT[:, t, hh * 256:(hh + 1) * 256]
                                    pend_rope.append(rope_a(k, hh * 256, 256, RA, 128, tab, s0 + hh * 256, qb, qb2, o_ap, "qT", rc))
                            else:
                                o_ap = kT[:, t - c.A_HEADS, s0:s0 + sw]
                                pend_rope.append(rope_a(k, 0, sw, RA, 128, tab, s0, qb, qb2, o_ap, "kT", rc))
                    for tt in range(W // 128 if "A_nov" not in DBG else 0):
                        k = brr()
                        if tt == 1:
                            for f in pend_rope:
                                f(brr())
                            pend_rope = []
                        for kc in range(NDC):
                            sch.op("pe", lambda e, kc=kc, k=k, tt=tt: e.matmul(bank(k)[:, 0:KV * 128], lhsT=hT[:, kc, tt * 128:(tt + 1) * 128], rhs=wv[:, kc, :], start=(kc == 0), stop=(kc == NDC - 1)),
                                   reads=["hT", "wv"], writes=[("ps", k)], signal=(kc == NDC - 1))
                        sch.op("act", lambda e, k=k, tt=tt: e.activation(out=V[:, tt, :], in_=bank(k)[:, 0:KV * 128], func=AF.Copy),
                               reads=[("ps", k)], writes=["V"])
                    def att_scores(kh, n, blk=blk):
                        nbg = blk * 4 + n
                        js = [j for j in (0, 1, 2) if 0 <= nbg - 1 + j < S // 128]
                        pts = []
                        for j in js:
                            kt = n + j
                            k = brr()
                            sch.op("pe", lambda e, k=k, kt=kt: e.matmul(bank(k)[:, 0:GW].rearrange("p (g q) -> p g q", g=G), lhsT=kT[:, kh, kt * 128:(kt + 1) * 128],
                                                                     rhs=qT[:, kh * G:(kh + 1) * G, n * 128:(n + 1) * 128], start=True, stop=True),
                                   reads=["kT", "qT"], writes=[("ps", k)])
                            pi = ptr()
                            sch.op("act", lambda e, k=k, pi=pi: e.activation(out=PT[pi][:, 0:GW], in_=bank(k)[:, 0:GW], func=AF.Exp, scale=scale),
                                   reads=[("ps", k)], writes=[("PT", pi)])
                            if j != 1:
                                msk = maskL if j == 0 else maskR
                                sch.op("pool", lambda e, pi=pi, msk=msk: e.tensor_tensor(out=PT[pi][:, 0:GW], in0=PT[pi][:, 0:GW], in1=msk, op=ALU.mult),
                                       reads=[("PT", pi), "cbf"], writes=[("PT", pi)])
                            pts.append((pi, kt))
                        return (kh, n, pts)

                    def att_pv(st):
                        kh, n, pts = st
                        ko, kd = brr(), brr()
                        L = len(pts)
                        for idx, (pi, kt) in enumerate(pts):
                            sch.op("pe", lambda e, pi=pi, kt=kt, idx=idx: e.matmul(bank(ko)[:, 0:GW], lhsT=V[:, kt, kh * 128:(kh + 1) * 128], rhs=PT[pi][:, 0:GW], start=(idx == 0), stop=(idx == L - 1)),
                                   reads=["V", ("PT", pi)], writes=[("ps", ko)], signal=(idx == L - 1))
                        for idx, (pi, kt) in enumerate(pts):
                            sch.op("pe", lambda e, pi=pi, idx=idx: e.matmul(bank(kd)[:, 0:GW], lhsT=ones[:], rhs=PT[pi][:, 0:GW], start=(idx == 0), stop=(idx == L - 1)),
                                   reads=["ones", ("PT", pi)], writes=[("ps", kd)], signal=(idx == L - 1))
                        di = dr()
                        for g in range(G):
                            h = kh * G + g
                            sch.op("dve", lambda e, g=g, h=h: e.tensor_scalar(out=den[di][:, g * 128:(g + 1) * 128], in0=bank(kd)[:, g * 128:(g + 1) * 128],
                                                                         scalar1=sinkexp[:, ja * c.A_HEADS + h:ja * c.A_HEADS + h + 1], scalar2=None, op0=ALU.add),
                                   reads=[("ps", kd), "sinkexp"], writes=[("den", di)])
                        sch.op("dve", lambda e: e.reciprocal(out=den[di][:, 0:GW], in_=den[di][:, 0:GW]), reads=[("den", di)], writes=[("den", di)])
                        sch.op("dve", lambda e: e.tensor_tensor(out=oT[:, kh * G:(kh + 1) * G, n * 128:(n + 1) * 128], in0=bank(ko)[:, 0:GW].rearrange("p (g q) -> p g q", g=G),
                                                                in1=den[di][:, 0:GW].rearrange("p (g q) -> p g q", g=G), op=ALU.mult),
                               reads=[("ps", ko), ("den", di)], writes=["oT"])

                    prev = None
                    for kh in range(KV if "A_noattn" not in DBG else 0):
                        for n in range(4):
                            cur = att_scores(kh, n)
                            if prev is not None:
                                att_pv(prev)
                            prev = cur
                    if prev is not None:
                        att_pv(prev)
                    out_proj(wp, ("a_o", ja), c.A_HEADS, oT, "oT", bin_, bout, seq, blk, xres, xn, brr)
                sch.barrier()

        def rope_a(k, pc0, w, R, npart, tab, tcol, qb, qb2, out_ap, out_res, rc):
            i = rc()
            y1, y2 = qb[i], qb2[i]
            P = slice(0, npart)
            sch.op("dve", lambda e: e.tensor_tensor(out=y1[P, 0:w], in0=bank(k)[P, pc0:pc0 + w], in1=tab[P, 0, tcol:tcol + w], op=ALU.mult),
                   reads=[("ps", k), "tab"], writes=[("qb", i)])
            sch.op("dve", lambda e: e.tensor_tensor(out=y2[P, 0:w], in0=bank(k)[P, pc0:pc0 + w], in1=tab[P, 1, tcol:tcol + w], op=ALU.mult),
                   reads=[("ps", k), "tab"], writes=[("qb2", i)])

            def fin(k2):
                sch.op("pe", lambda e: e.matmul(bank(k2)[P, 0:w], lhsT=identb[P, P], rhs=y1[P, 0:w], start=True, stop=False),
                       reads=[("qb", i), "cbf"], writes=[("ps", k2)], signal=False)
                sch.op("pe", lambda e: e.matmul(bank(k2)[P, 0:w], lhsT=R, rhs=y2[P, 0:w], start=False, stop=True),
                       reads=[("qb2", i), "cbf"], writes=[("ps", k2)])
                sch.op("act", lambda e: e.activation(out=out_ap, in_=bank(k2)[P, 0:w], func=AF.Copy),
                       reads=[("ps", k2)], writes=[out_res])
            return fin

        def rope_cols(k, k2, pc0, w, R, npart, tab, tcol, qb, t1, t2, out_ap, out_res, rc, qb2=None):
            if "oldrope" not in DBG and "norope" not in DBG:
                fin = rope_a(k, pc0, w, R, npart, tab, tcol, qb, qb2, out_ap, out_res, rc)
                fin(k2)
                return
            if "norope" in DBG:
                sch.op("act", lambda e: e.activation(out=out_ap, in_=bank(k)[0:npart, pc0:pc0 + w], func=AF.Copy),
                       reads=[("ps", k)], writes=[out_res])
                return
            i = rc()
            qbt, t1t, t2t = qb[i], t1[i], t2[i]
            P = slice(0, npart)
            sch.op("act", lambda e: e.activation(out=qbt[P, 0:w], in_=bank(k)[P, pc0:pc0 + w], func=AF.Copy),
                   reads=[("ps", k)], writes=[("qb", i)])
            Ruse = R if "ropeones" not in DBG else ones[0:npart, 0:npart]
            if "rope_nomm" in DBG:
                k2 = k
            else:
                sch.op("pe", lambda e: e.matmul(bank(k2)[P, 0:w], lhsT=Ruse, rhs=qbt[P, 0:w], start=True, stop=True),
                       reads=[("qb", i), "cbf"], writes=[("ps", k2)])
            if "rope_nodve" in DBG:
                sch.op("act", lambda e: e.activation(out=out_ap, in_=bank(k2)[P, 0:w], func=AF.Copy),
                       reads=[("ps", k2)], writes=[out_res])
                return
            if "rope_nomul" in DBG:
                sch.op("dve", lambda e: e.memset(t1t[P, 0:w], 1.0), reads=[], writes=[("t1", i)])
                sch.op("dve", lambda e: e.memset(t2t[P, 0:w], 1.0), reads=[], writes=[("t2", i)])
                sch.op("pool", lambda e: e.tensor_tensor(out=out_ap, in0=t1t[P, 0:w], in1=t2t[P, 0:w], op=ALU.add),
                       reads=[("t1", i), ("t2", i), ("ps", k2)], writes=[out_res])
                return
            sch.op("dve", lambda e: e.tensor_tensor(out=t1t[P, 0:w], in0=bank(k)[P, pc0:pc0 + w], in1=tab[P, 0, tcol:tcol + w], op=ALU.mult),
                   reads=[("ps", k), "tab", ("qb", i)], writes=[("t1", i)])
            sch.op("dve", lambda e: e.tensor_tensor(out=t2t[P, 0:w], in0=bank(k2)[P, 0:w], in1=tab[P, 1, tcol:tcol + w], op=ALU.mult),
                   reads=[("ps", k2), "tab"], writes=[("t2", i)])
            if "rope_noadd" in DBG:
                sch.op("act", lambda e: e.activation(out=out_ap, in_=t1t[P, 0:w], func=AF.Copy),
                       reads=[("t1", i), ("t2", i)], writes=[out_res])
                return
            sch.op("pool" if "ropedve" not in DBG else "dve", lambda e: e.tensor_tensor(out=out_ap, in0=t1t[P, 0:w], in1=t2t[P, 0:w], op=ALU.add),
                   reads=[("t1", i), ("t2", i)], writes=[out_res])

        def phase_F(l, seq, bin_, bout):
            W = TB + 2
            NFC = c.NFC
            with ExitStack() as ph:
                def T(name, shape, dt):
                    return SB(ph, name, shape, dt)
                xt = T("f_xt", [128, NDC, W // 2], F32)
                sq = [T("f_sq%d" % i, [128, W // 2], BF16) for i in range(2)]
                rstd = T("f_rstd", [128, W // 2], F32)
                hT = T("f_hT", [128, NDC, W], BF16)
                aT = T("f_aT", [128, NFC, TB], BF16)
                tmp = [T("f_tmp%d" % i, [128, 2, 256], F32) for i in range(3)]
                sg = [T("f_sg%d" % i, [128, 2, 256], F32) for i in range(2)]
                xres = [T("f_xres%d" % i, [128, TB], F32) for i in range(2)]
                xn = [T("f_xn%d" % i, [128, TB], F32) for i in range(2)]
                wpi = WPool(ph, "f_wi", 4, NDC, 128)
                wpo = WPool(ph, "f_wo", 2, NFC, 128)
                prr = rr(3)
                trr = rr(3)
                srr = rr(2)
                orr_state = rr(2)

                def obank():
                    return 6 + orr_state()
                for blk in range(NB):
                    c0 = PAD + blk * TB - 1
                    norm_block((xt, sq, rstd), bin_, seq, c0, W, ("gffn", l), hT, "hT", [6, 7])
                    for i in range(NFC):
                        sgi = None
                        for which in range(2):
                            t = 2 * i + which
                            wt, wres = wpi.load(("f_in", l), t)
                            pi = prr()
                            for sc in range(2):
                                for kc in range(NDC):
                                    sch.op("pe", lambda e, kc=kc, wt=wt, pi=pi, sc=sc: e.matmul(pp[pi][:, sc, 0:258], lhsT=wt[:, kc, :], rhs=hT[:, kc, sc * 256:sc * 256 + 258], start=(kc == 0), stop=(kc == NDC - 1)),
                                           reads=[wres, "hT"], writes=[("pp", pi)], signal=(kc == NDC - 1 and sc == 1))
                            ti = trr()
                            w0, w1, w2, cb = cc(("cw", l, 0), t), cc(("cw", l, 1), t), cc(("cw", l, 2), t), cc(("cb", l), t)
                            sch.op("act", lambda e, ti=ti, pi=pi, w1=w1, cb=cb: e.activation(out=tmp[ti][:], in_=pp[pi][:, :, 1:257], func=AF.Identity, bias=cb, scale=w1),
                                   reads=[("pp", pi), "cst"], writes=[("tmp", ti)])
                            sch.op("dve", lambda e, ti=ti, pi=pi, w0=w0: e.scalar_tensor_tensor(out=tmp[ti][:], in0=pp[pi][:, :, 0:256], scalar=w0, in1=tmp[ti][:], op0=ALU.mult, op1=ALU.add),
                                   reads=[("pp", pi), ("tmp", ti), "cst"], writes=[("tmp", ti)])
                            sch.op("dve", lambda e, ti=ti, pi=pi, w2=w2: e.scalar_tensor_tensor(out=tmp[ti][:], in0=pp[pi][:, :, 2:258], scalar=w2, in1=tmp[ti][:], op0=ALU.mult, op1=ALU.add),
                                   reads=[("pp", pi), ("tmp", ti), "cst"], writes=[("tmp", ti)])
                            if which == 0:
                                sgi = srr()
                                sch.op("act", lambda e, ti=ti, sgi=sgi: e.activation(out=sg[sgi][:], in_=tmp[ti][:], func=AF.Silu),
                                       reads=[("tmp", ti)], writes=[("sg", sgi)])
                            else:
                                sch.op("pool", lambda e, ti=ti, sgi=sgi, i=i: e.tensor_tensor(out=aT[:, i, :].rearrange("p (s q) -> p s q", s=2), in0=tmp[ti][:], in1=sg[sgi][:], op=ALU.mult),
                                       reads=[("tmp", ti), ("sg", sgi)], writes=["aT"])
                    c1 = PAD + blk * TB
                    for m in range(NDC):
                        wt, wres = wpo.load(("f_out", l), m)
                        k = obank()
                        ii = m % 2
                        srcx = xs[bin_][seq][m * 128:(m + 1) * 128, c1:c1 + TB]
                        sch.dma("sp", "f_xres%d" % ii, lambda e, ii=ii, srcx=srcx: e.dma_start(out=xres[ii][:], in_=srcx),
                                reads=[("x", bin_, seq, blk)], writes=[("xres", ii)])
                        for kc in range(NFC):
                            sch.op("pe", lambda e, kc=kc, wt=wt, k=k: e.matmul(bank(k), lhsT=wt[:, kc, :], rhs=aT[:, kc, :], start=(kc == 0), stop=(kc == NFC - 1)),
                                   reads=[wres, "aT"], writes=[("ps", k)], signal=(kc == NFC - 1))
                        sch.op("dve", lambda e, ii=ii, k=k: e.tensor_tensor(out=xn[ii][:], in0=bank(k), in1=xres[ii][:], op=ALU.add),
                               reads=[("ps", k), ("xres", ii)], writes=[("xn", ii)])
                        dst = xs[bout][seq][m * 128:(m + 1) * 128, c1:c1 + TB]
                        sch.dma("pool", "f_xn%d" % ii, lambda e, ii=ii, dst=dst: e.dma_start(out=dst, in_=xn[ii][:]),
                                reads=[("xn", ii)], writes=[("x", bout, seq, blk)])
                sch.barrier()

        def phase_B(l, jb, seq, bin_, bout):
            scale = 192.0 ** -0.5
            KQ, KK = c.QL // 128, c.KVL // 128
            NH = c.B_HEADS
            NT = S // 128
            NQC = S // TB
            with ExitStack() as outer:
                def TO(name, shape, dt):
                    return SB(outer, name, shape, dt)
                cqn = TO("b_cqn", [128, KQ, S], BF16)
                ckvn = TO("b_ckvn", [128, KK, S], BF16)
                krT = TO("b_krT", [128, S], BF16)
                tab = TO("b_tab", [64, 2, S], F32)
                qb = [TO("b_qb%d" % i, [128, 256], BF16) for i in range(2)]
                qb2 = [TO("b_qc%d" % i, [128, 256], BF16) for i in range(2)]
                t1 = [TO("b_t1%d" % i, [128, 256], F32) for i in range(2)]
                t2 = [TO("b_t2%d" % i, [128, 256], F32) for i in range(2)]
                sch.dma("sp", "b_tab", lambda e: e.dma_start(out=tab[:], in_=tabB_d[:, :, :]), writes=["tab"])
                sch.op("dve", lambda e: e.memset(krT[64:128, :], 0.0), writes=["krT_hi"])
                rc = rr(2)
                with ExitStack() as ph:
                    def T(name, shape, dt):
                        return SB(ph, name, shape, dt)
                    xt = T("b1_xt", [128, NDC, TB // 2], F32)
                    sq = [T("b1_sq%d" % i, [128, TB], BF16) for i in range(2)]
                    rstd = T("b1_rstd", [128, TB], F32)
                    hT = T("b1_hT", [128, NDC, TB], BF16)
                    cf = T("b1_cf", [128, max(KQ, KK), TB], F32)
                    wp = WPool(ph, "b1_w", 3, NDC, 128)
                    brr = rr(6)
                    for blk in range(NB):
                        c0 = PAD + blk * TB
                        norm_block((xt, sq, rstd), bin_, seq, c0, TB, ("gmix", l), hT, "hT", [6, 7])
                        for grp, (nk, dst, gk) in enumerate(((KQ, cqn, ("bqn", jb)), (KK, ckvn, ("bkvn", jb)))):
                            for j in range(nk):
                                t = grp * KQ + j
                                wt, wres = wp.load(("b_in", jb), t)
                                k = brr()
                                for kc in range(NDC):
                                    sch.op("pe", lambda e, kc=kc, wt=wt, k=k: e.matmul(bank(k), lhsT=wt[:, kc, :], rhs=hT[:, kc, :], start=(kc == 0), stop=(kc == NDC - 1)),
                                           reads=[wres, "hT"], writes=[("ps", k)], signal=(kc == NDC - 1))
                                sch.op("act", lambda e, k=k, j=j: e.activation(out=cf[:, j, :], in_=bank(k), func=AF.Copy),
                                       reads=[("ps", k)], writes=[("cf", j)])
                                sqt = sq[j % 2]
                                sch.op("act", lambda e, k=k, sqt=sqt: e.activation(out=sqt[:], in_=bank(k), func=AF.Square),
                                       reads=[("ps", k)], writes=[("sq", j % 2)])
                                sch.op("pe", lambda e, j=j, sqt=sqt, nk=nk: e.matmul(bank(7), lhsT=ones[:], rhs=sqt[:], start=(j == 0), stop=(j == nk - 1)),
                                       reads=[("sq", j % 2), "ones"], writes=[("ps", 7)], signal=True)
                            sch.op("act", lambda e, nk=nk: e.activation(out=rstd[:], in_=bank(7), func=AF.Sqrt, bias=epsb[:, 0:1], scale=1.0 / (nk * 128)),
                                   reads=[("ps", 7), "epsb"], writes=["rstd"])
                            sch.op("dve", lambda e: e.reciprocal(out=rstd[:], in_=rstd[:]),
                                   reads=["rstd"], writes=["rstd"])
                            for j in range(nk):
                                sch.op("dve", lambda e, j=j, dst=dst, gk=gk, blk=blk: e.scalar_tensor_tensor(out=dst[:, j, blk * TB:(blk + 1) * TB], in0=cf[:, j, :], scalar=cc(gk, j), in1=rstd[:],
                                                                                                  op0=ALU.mult, op1=ALU.mult),
                                       reads=[("cf", j), "rstd", "cst"], writes=[("lat", grp)])
                        wt, wres = wp.load(("b_inr", jb), 0)
                        k, k2 = brr(), brr()
                        for kc in range(NDC):
                            sch.op("pe", lambda e, kc=kc, wt=wt, k=k: e.matmul(bank(k)[0:64, :], lhsT=wt[:, kc, 0:64], rhs=hT[:, kc, :], start=(kc == 0), stop=(kc == NDC - 1)),
                                   reads=[wres, "hT"], writes=[("ps", k)], signal=(kc == NDC - 1))
                        for hh in range(2):
                            rope_cols(k, k2, hh * 256, 256, RB, 64, tab, blk * TB + hh * 256, qb, t1, t2, krT[0:64, blk * TB + hh * 256:blk * TB + (hh + 1) * 256], "krT", rc, qb2)
                    sch.barrier()
                with ExitStack() as ph:
                    def T(name, shape, dt):
                        return SB(ph, name, shape, dt)
                    qn = T("b2_qn", [128, S], BF16)
                    qr = T("b2_qr", [128, S], BF16)
                    sch.op("dve", lambda e: e.memset(qr[64:128, :], 0.0), writes=["qr"])
                    kn = T("b2_kn", [128, S], BF16)
                    Vh = T("b2_V", [128, NT, 128], BF16)
                    PT = [T("b2_PT%d" % i, [128, TB], BF16) for i in range(4)]
                    rden = [T("b2_rd%d" % i, [128, TB], F32) for i in range(2)]
                    acc = [[T("b2_acc%d%d" % (i, j), [128, TB], F32) for j in range(2)] for i in range(2)]
                    hi = [T("b2_hi%d" % i, [128, TB], BF16) for i in range(2)]
                    lo = [T("b2_lo%d" % i, [128, TB], BF16) for i in range(2)]
                    oh = [T("b2_oh%d" % i, [128, S], BF16) for i in range(2)]
                    wq = WPool(ph, "b2_wqn", 2, KQ, 128)
                    wqr = WPool(ph, "b2_wqr", 2, KQ, 64)
                    wk = WPool(ph, "b2_wkn", 2, KK, 128)
                    wvp = WPool(ph, "b2_wv", 2, KK, 128)
                    srr = rr(4)
                    prr = rr(4)
                    arr = rr(2)
                    for h in range(NH):
                        wqt, wqres = wq.load(("b_qn", jb), h)
                        wqrt, wqrres = wqr.load(("b_qr", jb), h)
                        wkt, wkres = wk.load(("b_kn", jb), h)
                        wvt, wvres = wvp.load(("b_v", jb), h)
                        for qc in range(NQC):
                            cs = slice(qc * TB, (qc + 1) * TB)
                            k = srr()
                            for kc in range(KQ):
                                sch.op("pe", lambda e, kc=kc, k=k, cs=cs, wqt=wqt: e.matmul(bank(k), lhsT=wqt[:, kc, :], rhs=cqn[:, kc, cs], start=(kc == 0), stop=(kc == KQ - 1)),
                                       reads=[wqres, ("lat", 0)], writes=[("ps", k)], signal=(kc == KQ - 1))
                            sch.op("act", lambda e, k=k, cs=cs: e.activation(out=qn[:, cs], in_=bank(k), func=AF.Copy), reads=[("ps", k)], writes=["qn"])
                            k = srr()
                            for kc in range(KK):
                                sch.op("pe", lambda e, kc=kc, k=k, cs=cs, wkt=wkt: e.matmul(bank(k), lhsT=wkt[:, kc, :], rhs=ckvn[:, kc, cs], start=(kc == 0), stop=(kc == KK - 1)),
                                       reads=[wkres, ("lat", 1)], writes=[("ps", k)], signal=(kc == KK - 1))
                            sch.op("act", lambda e, k=k, cs=cs: e.activation(out=kn[:, cs], in_=bank(k), func=AF.Copy), reads=[("ps", k)], writes=["kn"])
                            k, k2 = srr(), srr()
                            for kc in range(KQ):
                                sch.op("pe", lambda e, kc=kc, k=k, cs=cs, wqrt=wqrt: e.matmul(bank(k)[0:64, :], lhsT=wqrt[:, kc, :], rhs=cqn[:, kc, cs], start=(kc == 0), stop=(kc == KQ - 1)),
                                       reads=[wqrres, ("lat", 0)], writes=[("ps", k)], signal=(kc == KQ - 1))
                            for hh in range(2):
                                rope_cols(k, k2, hh * 256, 256, RB, 64, tab, qc * TB + hh * 256, qb, t1, t2, qr[0:64, qc * TB + hh * 256:qc * TB + (hh + 1) * 256], "qr", rc, qb2)
                        for t4 in range(NT // 4):
                            k = srr()
                            for tt in range(4):
                                tk = t4 * 4 + tt
                                for kc in range(KK):
                                    sch.op("pe", lambda e, kc=kc, k=k, tt=tt, tk=tk, wvt=wvt: e.matmul(bank(k)[:, tt * 128:(tt + 1) * 128], lhsT=ckvn[:, kc, tk * 128:(tk + 1) * 128], rhs=wvt[:, kc, :],
                                                                                                 start=(kc == 0), stop=(kc == KK - 1)),
                                           reads=[wvres, ("lat", 1)], writes=[("ps", k)], signal=(kc == KK - 1 and tt == 3))
                            sch.op("act", lambda e, k=k, t4=t4: e.activation(out=Vh[:, t4 * 4:(t4 + 1) * 4, :], in_=bank(k).rearrange("p (t d) -> p t d", t=4), func=AF.Copy),
                                   reads=[("ps", k)], writes=["Vh"])
                        oi = h % 2
                        for qc in range(NQC):
                            cs = slice(qc * TB, (qc + 1) * TB)
                            a = arr()
                            ko, kd = 4 + 2 * a, 5 + 2 * a
                            pend = []

                            def scores(m, cs=cs):
                                k = srr()
                                sch.op("pe", lambda e, k=k, m=m: e.matmul(bank(k), lhsT=kn[:, m * 128:(m + 1) * 128], rhs=qn[:, cs], start=True, stop=False),
                                       reads=["kn", "qn"], writes=[("ps", k)], signal=False)
                                sch.op("pe", lambda e, k=k, m=m: e.matmul(bank(k), lhsT=krT[:, m * 128:(m + 1) * 128], rhs=qr[:, cs], start=False, stop=True),
                                       reads=["krT", "qr"], writes=[("ps", k)])
                                pi = prr()
                                sch.op("act", lambda e, k=k, pi=pi: e.activation(out=PT[pi][:], in_=bank(k), func=AF.Exp, scale=scale),
                                       reads=[("ps", k)], writes=[("PT", pi)])
                                return pi

                            di = qc % 2

                            def pv(m, pi, ko=ko, kd=kd, di=di):
                                sch.op("pe", lambda e: e.matmul(bank(ko), lhsT=Vh[:, m, :], rhs=PT[pi][:], start=(m == 0), stop=(m == NT - 1)),
                                       reads=["Vh", ("PT", pi)], writes=[("ps", ko)], signal=(m == NT - 1))
                                if m % 8 == 7:
                                    sch.op("pe", lambda e: e.matmul(bank(kd), lhsT=ones[:], rhs=PT[pi][:], start=(m == 7), stop=False),
                                           reads=["ones", ("PT", pi)], writes=[("ps", kd)], signal=True)
                                    return
                                eng = "pool" if m % 2 == 0 else "dve"
                                ac = acc[di][m % 2]
                                if m < 2:
                                    sch.op(eng, lambda e: e.tensor_copy(out=ac[:], in_=PT[pi][:]), reads=[("PT", pi)], writes=[("acc", di, m % 2)])
                                else:
                                    sch.op(eng, lambda e: e.tensor_tensor(out=ac[:], in0=ac[:], in1=PT[pi][:], op=ALU.add),
                                           reads=[("PT", pi), ("acc", di, m % 2)], writes=[("acc", di, m % 2)])
                            LOOK = 2
                            for m in range(NT + LOOK):
                                if m < NT:
                                    pend.append((m, scores(m)))
                                if m >= LOOK:
                                    mm, pi = pend.pop(0)
                                    pv(mm, pi)
                            a0, a1 = acc[di][0], acc[di][1]
                            sch.op("dve", lambda e, a0=a0, a1=a1: e.tensor_tensor(out=a0[:], in0=a0[:], in1=a1[:], op=ALU.add),
                                   reads=[("acc", di, 0), ("acc", di, 1)], writes=[("acc", di, 0)])
                            sch.op("dve", lambda e, a0=a0, di=di: e.tensor_copy(out=hi[di][:], in_=a0[:]), reads=[("acc", di, 0)], writes=[("hi", di)])
                            sch.op("dve", lambda e, a0=a0, di=di: e.tensor_tensor(out=lo[di][:], in0=a0[:], in1=hi[di][:], op=ALU.subtract),
                                   reads=[("acc", di, 0), ("hi", di)], writes=[("lo", di)])
                            sch.op("pe", lambda e, di=di, kd=kd: e.matmul(bank(kd), lhsT=ones[:], rhs=hi[di][:], start=(NT < 8), stop=False),
                                   reads=["ones", ("hi", di)], writes=[("ps", kd)], signal=False)
                            sch.op("pe", lambda e, di=di, kd=kd: e.matmul(bank(kd), lhsT=ones[:], rhs=lo[di][:], start=False, stop=True),
                                   reads=["ones", ("lo", di)], writes=[("ps", kd)])
                            sch.op("dve", lambda e, di=di, kd=kd: e.reciprocal(out=rden[di][:], in_=bank(kd)), reads=[("ps", kd)], writes=[("rden", di)])
                            sch.op("dve", lambda e, di=di, ko=ko, oi=oi, cs=cs: e.tensor_tensor(out=oh[oi][:, cs], in0=bank(ko), in1=rden[di][:], op=ALU.mult),
                                   reads=[("ps", ko), ("rden", di)], writes=[("oh", oi)])
                        dst = osc[seq][h * 128:(h + 1) * 128, :]
                        sch.dma("pool", "b2_oh%d" % oi, lambda e, oi=oi, dst=dst: e.dma_start(out=dst, in_=oh[oi][:]),
                                reads=[("oh", oi)], writes=[("osc", seq)])
                    sch.barrier()
            with ExitStack() as ph:
                def T(name, shape, dt):
                    return SB(ph, name, shape, dt)
                ot = [T("b3_ot%d" % i, [128, NH, TB], BF16) for i in range(2)]
                xres = [T("b3_xres%d" % i, [128, TB], F32) for i in range(2)]
                xn = [T("b3_xn%d" % i, [128, TB], F32) for i in range(2)]
                wp = WPool(ph, "b3_w", 3, NH, 128)
                brr = rr(8)
                for blk in range(NB):
                    oi = blk % 2
                    src = osc[seq].rearrange("(k p) c -> p k c", p=128)[:, :, blk * TB:(blk + 1) * TB]
                    sch.dma("sp", "b3_ot%d" % oi, lambda e, oi=oi, src=src: e.dma_start(out=ot[oi][:], in_=src),
                            reads=[("osc", seq)], writes=[("ot", oi)])
                    out_proj(wp, ("b_o", jb), NH, ot[oi], ("ot", oi), bin_, bout, seq, blk, xres, xn, brr)
                sch.barrier()

        def phase_Y(seq, bin_):
            with ExitStack() as ph:
                def T(name, shape, dt):
                    return SB(ph, name, shape, dt)
                xt = T("y_xt", [128, NDC, TB // 2], F32)
                sq = [T("y_sq%d" % i, [128, TB // 2], BF16) for i in range(2)]
                rstd = T("y_rstd", [128, TB // 2], F32)
                yT = T("y_yT", [128, NDC, TB], F32)
                ytok = [T("y_tok%d" % i, [128, D], F32) for i in range(2)]
                idt = T("y_idt", [128, 128], F32)
                sch.dma("sp", "y_idt", lambda e: e.dma_start(out=idt[:], in_=id_d[:, :]), writes=["idt"])
                brr = rr(6)
                for blk in range(NB):
                    norm_block((xt, sq, rstd), bin_, seq, PAD + blk * TB, TB, ("gfin", 0), yT, "yT", [6, 7])
                    for tt in range(4):
                        yi = tt % 2
                        for d4 in range(NDC // 4 if NDC >= 4 else 1):
                            nd = min(4, NDC)
                            k = brr()
                            for dd in range(nd):
                                dc = d4 * 4 + dd
                                sch.op("pe", lambda e, k=k, dd=dd, dc=dc, tt=tt: e.transpose(bank(k)[:, dd * 128:(dd + 1) * 128], yT[:, dc, tt * 128:(tt + 1) * 128], idt[:]),
                                       reads=["yT", "idt"], writes=[("ps", k)], signal=(dd == nd - 1))
                            eng = "act" if d4 % 2 == 0 else "dve"
                            if eng == "act":
                                sch.op("act", lambda e, k=k, d4=d4, yi=yi, nd=nd: e.activation(out=ytok[yi][:, d4 * 512:d4 * 512 + nd * 128], in_=bank(k)[:, 0:nd * 128], func=AF.Copy),
                                       reads=[("ps", k)], writes=[("ytok", yi, d4)])
                            else:
                                sch.op("dve", lambda e, k=k, d4=d4, yi=yi, nd=nd: e.tensor_copy(out=ytok[yi][:, d4 * 512:d4 * 512 + nd * 128], in_=bank(k)[:, 0:nd * 128]),
                                       reads=[("ps", k)], writes=[("ytok", yi, d4)])
                        r0 = seq * S + blk * TB + tt * 128
                        o = sch.dma("pool", "y_tok%d" % yi, lambda e, yi=yi, r0=r0: e.dma_start(out=y_out[r0:r0 + 128, :], in_=ytok[yi][:]),
                                    reads=[("ytok", yi, d4) for d4 in range(max(1, NDC // 4))], writes=[])
                        sch.out_tokens.append(o)
                sch.barrier()

        hl = 0
        for l in range(c.DEPTH):
            for seq in range(NSEQ):
                if l % 2 == 0:
                    if "noA" not in DBG:
                        phase_A(l, l // 2, seq, hl % 2, (hl + 1) % 2)
                else:
                    phase_B(l, l // 2, seq, hl % 2, (hl + 1) % 2)
            hl += 1
            for seq in range(NSEQ):
                if "noF" not in DBG:
                    phase_F(l, seq, hl % 2, (hl + 1) % 2)
            hl += 1
        for seq in range(NSEQ):
            phase_Y(seq, hl % 2)
        stuck = sch.simulate()
        assert not stuck, stuck
        print('streams', {e: len(o) for e, o in sch.streams.items()}, 'nsem', len(sch.dsem) + 4, flush=True)
        with nc.Block() as block:
            sch.emit(block)
    return nc


def _prep_common(cfg, inp):
    wl, cl = WLayout(cfg), CLayout(cfg)
    wall = pack_weights(cfg, wl, inp)
    consts = pack_consts(cfg, cl, inp)
    tabA, tabB, cbf, ident = make_tables(cfg)
    d = {"consts": consts, "tabA": tabA, "tabB": tabB, "cbf": cbf, "ident": ident}
    for g, w_ in enumerate(wall):
        d["wall%d" % g] = w_
    return d


def run(cfg, seqs, inp, trace=False):
    nc = build(cfg)
    common = _prep_common(cfg, inp)
    in_maps = []
    for s in seqs:
        m = dict(common)
        m["x_in"] = np.ascontiguousarray(s.reshape(cfg.NSEQ * cfg.S, cfg.D))
        in_maps.append(m)
    res = run_bass_kernel_spmd(nc, in_maps, core_ids=list(range(len(seqs))), trace=trace)
    outs = [np.asarray(r["y"]).reshape(cfg.NSEQ, cfg.S, cfg.D) for r in res.results]
    return outs, res


def kernel(x_prompt, x_sample, **w):
    cfg = Cfg()
    xp = np.asarray(x_prompt, np.float32)
    xsm = np.asarray(x_sample, np.float32)
    zero = np.zeros_like(xsm[0])
    seqs = [np.stack([xp[2 * i], xp[2 * i + 1]]) for i in range(4)] + [np.stack([xsm[i], zero]) for i in range(4)]
    outs, _ = run(cfg, seqs, w)
    y_prompt = np.concatenate([outs[i] for i in range(4)], axis=0).astype(np.float32)
    y_sample = np.stack([outs[4 + i][0] for i in range(4)], axis=0).astype(np.float32)
    return (y_prompt, y_sample)
```

```python
from contextlib import ExitStack
import numpy as np
import ml_dtypes
import concourse.bass as bass
import concourse.mybir as mybir
from concourse.bass_utils import run_bass_kernel_spmd

F32 = mybir.dt.float32
BF16 = mybir.dt.bfloat16
AF = mybir.ActivationFunctionType
ALU = mybir.AluOpType
DBG = set()
PAD = 128
TB = 512
EPS = 1e-6
THETA = 10000.0


class Cfg:
    def __init__(self, D=2048, S=4096, NSEQ=2, DEPTH=4, A_KV=4, B_HEADS=16, QL=512, KVL=512, DFF=5632):
        self.D, self.S, self.NSEQ, self.DEPTH = D, S, NSEQ, DEPTH
        self.A_HEADS = D // 128
        self.A_KV = A_KV
        self.G = self.A_HEADS // A_KV
        self.B_HEADS, self.QL, self.KVL, self.DFF = B_HEADS, QL, KVL, DFF
        self.NDC = D // 128
        self.NFC = DFF // 128
        self.LA = (DEPTH + 1) // 2
        self.LB = DEPTH // 2
        self.SP = S + 2 * PAD
        self.NB = S // TB


def _tiles(W, mt):
    din, dout = W.shape
    kc, nm = din // 128, dout // mt
    return np.ascontiguousarray(W.reshape(kc, 128, nm, mt).transpose(2, 1, 0, 3))


class WLayout:
    def __init__(self, cfg):
        c = cfg
        self.off = {}
        self.NG = max(1, c.DEPTH)
        self.total = [0] * self.NG
        kd, kq, kv, kf = c.NDC, c.QL // 128, c.KVL // 128, c.NFC
        for j in range(c.LA):
            g = 2 * j
            self._add(g, ("a_qk", j), c.A_HEADS + c.A_KV, kd, 128)
            self._add(g, ("a_v", j), 1, kd, c.A_KV * 128)
            self._add(g, ("a_o", j), c.NDC, c.A_HEADS, 128)
        for j in range(c.LB):
            g = 2 * j + 1
            self._add(g, ("b_in", j), (c.QL + c.KVL) // 128, kd, 128)
            self._add(g, ("b_inr", j), 1, kd, 64)
            self._add(g, ("b_qn", j), c.B_HEADS, kq, 128)
            self._add(g, ("b_qr", j), c.B_HEADS, kq, 64)
            self._add(g, ("b_kn", j), c.B_HEADS, kv, 128)
            self._add(g, ("b_v", j), c.B_HEADS, kv, 128)
            self._add(g, ("b_o", j), c.NDC, c.B_HEADS, 128)
        for l in range(c.DEPTH):
            self._add(l, ("f_in", l), 2 * c.NFC, kd, 128)
            self._add(l, ("f_out", l), c.NDC, kf, 128)
        self.CW = 2048
        self.rows = [max(8, -(-(-(-t // self.CW)) // 8) * 8) for t in self.total]

    def _add(self, g, key, nt, kc, mt):
        self.off[key] = (g, self.total[g], nt, kc, mt)
        self.total[g] += nt * 128 * kc * mt

    def tile(self, key, t):
        g, off, nt, kc, mt = self.off[key]
        return g, off + t * 128 * kc * mt, kc, mt


def pack_weights(cfg, wl, inp):
    c = cfg
    flats = [np.zeros(r * wl.CW, np.float32) for r in wl.rows]

    def put(key, arr):
        g, off, nt, kc, mt = wl.off[key]
        assert arr.shape == (nt, 128, kc, mt), (key, arr.shape, (nt, 128, kc, mt))
        flats[g][off:off + arr.size] = arr.reshape(-1)

    qd, kd = c.A_HEADS * 128, c.A_KV * 128
    for j in range(c.LA):
        w = np.asarray(inp["a_w_qkv"][j])
        put(("a_qk", j), _tiles(w[:, :qd + kd], 128))
        put(("a_v", j), _tiles(w[:, qd + kd:], kd))
        put(("a_o", j), _tiles(np.asarray(inp["a_w_o"][j]), 128))
    for j in range(c.LB):
        w = np.asarray(inp["b_w_in"][j])
        put(("b_in", j), _tiles(w[:, :c.QL + c.KVL], 128))
        put(("b_inr", j), _tiles(w[:, c.QL + c.KVL:], 64))
        w = np.asarray(inp["b_w_q_up"][j]).reshape(c.QL, c.B_HEADS, 192)
        put(("b_qn", j), _tiles(np.ascontiguousarray(w[:, :, :128]).reshape(c.QL, -1), 128))
        put(("b_qr", j), _tiles(np.ascontiguousarray(w[:, :, 128:]).reshape(c.QL, -1), 64))
        w = np.asarray(inp["b_w_kv_up"][j]).reshape(c.KVL, c.B_HEADS, 256)
        put(("b_kn", j), _tiles(np.ascontiguousarray(w[:, :, :128]).reshape(c.KVL, -1), 128))
        put(("b_v", j), _tiles(np.ascontiguousarray(w[:, :, 128:]).reshape(c.KVL, -1), 128))
        put(("b_o", j), _tiles(np.asarray(inp["b_w_o"][j]), 128))
    for l in range(c.DEPTH):
        w = np.asarray(inp["f_w_in"][l])
        t = _tiles(w, 128)
        inter = np.empty_like(t)
        inter[0::2] = t[:c.NFC]
        inter[1::2] = t[c.NFC:]
        put(("f_in", l), inter)
        put(("f_out", l), _tiles(np.asarray(inp["f_w_out"][l]), 128))
    return [f.reshape(r, wl.CW) for f, r in zip(flats, wl.rows)]


class CLayout:
    def __init__(self, cfg):
        c = cfg
        self.off = {}
        n = 0
        for key, w in ([(("gmix", l), c.NDC) for l in range(c.DEPTH)] + [(("gffn", l), c.NDC) for l in range(c.DEPTH)]
                       + [(("gfin", 0), c.NDC)]
                       + [(("bqn", j), c.QL // 128) for j in range(c.LB)] + [(("bkvn", j), c.KVL // 128) for j in range(c.LB)]
                       + [(("cw", l, k), 2 * c.NFC) for l in range(c.DEPTH) for k in range(3)]
                       + [(("cb", l), 2 * c.NFC) for l in range(c.DEPTH)]
                       + [(("sink", j), c.A_HEADS) for j in range(c.LA)]):
            self.off[key] = n
            n += w
        self.n = n


def pack_consts(cfg, cl, inp):
    c = cfg
    out = np.zeros((128, cl.n), np.float32)

    def colmajor(v):
        v = np.asarray(v, np.float32)
        return v.reshape(-1, 128).T

    def inter(v):
        t = colmajor(v)
        o = np.empty_like(t)
        o[:, 0::2] = t[:, :c.NFC]
        o[:, 1::2] = t[:, c.NFC:]
        return o

    for l in range(c.DEPTH):
        out[:, cl.off[("gmix", l)]:][:, :c.NDC] = colmajor(inp["norm_mix"][l])
        out[:, cl.off[("gffn", l)]:][:, :c.NDC] = colmajor(inp["norm_ffn"][l])
        for k in range(3):
            out[:, cl.off[("cw", l, k)]:][:, :2 * c.NFC] = inter(inp["f_conv_w"][l][k])
        out[:, cl.off[("cb", l)]:][:, :2 * c.NFC] = inter(inp["f_conv_b"][l])
    out[:, cl.off[("gfin", 0)]:][:, :c.NDC] = colmajor(inp["norm_final"])
    for j in range(c.LB):
        out[:, cl.off[("bqn", j)]:][:, :c.QL // 128] = colmajor(inp["b_q_norm"][j])
        out[:, cl.off[("bkvn", j)]:][:, :c.KVL // 128] = colmajor(inp["b_kv_norm"][j])
    for j in range(c.LA):
        out[:, cl.off[("sink", j)]:][:, :c.A_HEADS] = np.broadcast_to(np.asarray(inp["a_sink"][j], np.float32)[None, :], (128, c.A_HEADS))
    return out


def make_tables(cfg):
    c = cfg
    f32 = np.float32

    def tab(dim, pos):
        inv = (f32(1.0) / (f32(THETA) ** (np.arange(0, dim, 2, dtype=f32) / f32(dim)))).astype(f32)
        ang = (pos.astype(f32)[None, :] * inv[:, None]).astype(f32)
        cs, sn = np.cos(ang).astype(f32), np.sin(ang).astype(f32)
        return np.stack([np.concatenate([cs, cs], 0), np.concatenate([sn, -sn], 0)], 1)

    posA = np.arange(-PAD, c.S + PAD)
    tabA = np.ascontiguousarray(tab(128, posA))
    tabB = np.ascontiguousarray(tab(64, np.arange(c.S)))
    G = c.G
    cbf = np.zeros((128, 128 + 64 + 2 * G * 128 + 128), np.float32)
    cbf[:, 192 + 2 * G * 128:] = np.eye(128, dtype=np.float32)
    for m in range(128):
        cbf[(m + 64) % 128, m] = 1.0
    for m in range(64):
        cbf[(m + 32) % 64, 128 + m] = 1.0
    b = np.arange(128)[:, None]
    a = np.arange(128)[None, :]
    mL = (a <= b).astype(np.float32)
    mR = (b <= a).astype(np.float32)
    cbf[:, 192:192 + G * 128] = np.tile(mL, (1, G))
    cbf[:, 192 + G * 128:192 + 2 * G * 128] = np.tile(mR, (1, G))
    return tabA, tabB, cbf.astype(ml_dtypes.bfloat16), np.eye(128, dtype=np.float32)


class Sched:
    CE = ("pe", "act", "dve", "pool")

    def __init__(self, nc, stack):
        self.nc = nc
        self.stack = stack
        self.streams = {e: [] for e in self.CE + ("sp",)}
        self.sem = {e: stack.enter_context(nc.semaphore("tl_" + e)) for e in self.CE}
        self.cnt = {e: 0 for e in self.CE}
        self.dsem = {}
        self.waited = {e: {} for e in self.streams}
        self.lastw = {}
        self.readers = {}
        self.out_tokens = []

    def _wait(self, o, d):
        if d is None or d.get("tok") is None:
            return
        if d["eng"] == "pe" and o["eng"] == "pe" and not d.get("dma") and not o.get("dma"):
            return
        name, sem, v = d["tok"]
        w = self.waited[o["eng"]]
        if w.get(name, 0) >= v:
            return
        w[name] = v
        o["waits"].append((sem, v))

    def _track(self, o, reads, writes):
        deps = []
        for r in reads:
            if r in self.lastw:
                deps.append(self.lastw[r])
        for r in writes:
            if r in self.lastw:
                deps.append(self.lastw[r])
            deps.extend(self.readers.get(r, ()))
        for d in deps:
            if d is not o:
                self._wait(o, d)
        for r in reads:
            self.readers.setdefault(r, []).append(o)
        for r in writes:
            self.lastw[r] = o
            self.readers[r] = []

    def op(self, eng, fn, reads=(), writes=(), signal=True):
        o = {"eng": eng, "fn": fn, "waits": [], "tok": None, "inc": None}
        self._track(o, reads, writes)
        if signal:
            self.cnt[eng] += 1
            o["tok"] = ("tl_" + eng, self.sem[eng], self.cnt[eng])
            o["inc"] = (self.sem[eng], 1)
        else:
            assert eng == "pe"
            o["tok"] = ("tl_" + eng, self.sem[eng], self.cnt[eng] + 1)
        self.streams[eng].append(o)
        return o

    def dma(self, q, key, fn, reads=(), writes=(), is_out=False):
        if key not in self.dsem:
            self.dsem[key] = [self.stack.enter_context(self.nc.semaphore("d_" + key)), 0]
        ent = self.dsem[key]
        o = {"eng": q, "fn": fn, "waits": [], "dma": True}
        self._track(o, reads, writes)
        ent[1] += 16
        o["tok"] = ("d_" + key, ent[0], ent[1])
        o["inc"] = (ent[0], 16)
        self.streams[q].append(o)
        return o

    def barrier(self):
        toks = [("tl_" + e, self.sem[e], self.cnt[e]) for e in self.CE if self.cnt[e] > 0]
        toks += [("d_" + k, v[0], v[1]) for k, v in self.dsem.items() if v[1] > 0 and not k.startswith("cast")]
        for e in self.streams:
            o = {"eng": e, "fn": None, "waits": [], "tok": None, "inc": None}
            w = self.waited[e]
            for name, sem, v in toks:
                if name == "tl_" + e:
                    continue
                if w.get(name, 0) < v:
                    w[name] = v
                    o["waits"].append((sem, v))
            self.streams[e].append(o)
        self.lastw = {k: v for k, v in self.lastw.items() if isinstance(k, tuple) and k[0] == "wbf"}
        self.readers = {}

    def simulate(self):
        val = {}
        ptr = {e: 0 for e in self.streams}
        prog = True
        while prog:
            prog = False
            for e, ops in self.streams.items():
                while ptr[e] < len(ops):
                    o = ops[ptr[e]]
                    if any(val.get(id(sem), 0) < v for sem, v in o["waits"]):
                        break
                    if o.get("inc") is not None and o["fn"] is not None:
                        val[id(o["inc"][0])] = val.get(id(o["inc"][0]), 0) + o["inc"][1]
                    ptr[e] += 1
                    prog = True
        stuck = {e: (ptr[e], len(ops)) for e, ops in self.streams.items() if ptr[e] < len(ops)}
        return stuck

    def emit(self, block):
        nc = self.nc
        engmap = {"pe": "tensor", "act": "scalar", "dve": "vector", "pool": "gpsimd", "sp": "sync"}
        for e, ops in self.streams.items():
            def body(eng, ops=ops):
                for o in ops:
                    for sem, v in o["waits"]:
                        eng.wait_ge(sem, v)
                    if o["fn"] is None:
                        continue
                    ins = o["fn"](eng)
                    if o["inc"] is not None:
                        ins.then_inc(o["inc"][0], o["inc"][1])
            getattr(block, engmap[e])(body)


def build(cfg):
    c = cfg
    wl, cl = WLayout(c), CLayout(c)
    D, S, NSEQ, NDC, SPW, NB, G = c.D, c.S, c.NSEQ, c.NDC, c.SP, c.NB, c.G
    nc = bass.Bass("TRN2", target_bir_lowering=False)
    x_in = nc.dram_tensor("x_in", [NSEQ * S, D], F32, kind="ExternalInput").ap()
    wall = [nc.dram_tensor("wall%d" % g, [wl.rows[g], wl.CW], F32, kind="ExternalInput").ap() for g in range(wl.NG)]
    cst_d = nc.dram_tensor("consts", [128, cl.n], F32, kind="ExternalInput").ap()
    tabA_d = nc.dram_tensor("tabA", [128, 2, SPW], F32, kind="ExternalInput").ap()
    tabB_d = nc.dram_tensor("tabB", [64, 2, S], F32, kind="ExternalInput").ap()
    NCB = 192 + 2 * G * 128 + 128
    cbf_d = nc.dram_tensor("cbf", [128, NCB], BF16, kind="ExternalInput").ap()
    id_d = nc.dram_tensor("ident", [128, 128], F32, kind="ExternalInput").ap()
    y_out = nc.dram_tensor("y", [NSEQ * S, D], F32, kind="ExternalOutput").ap()
    wbf = [nc.dram_tensor("wbf%d" % g, [wl.rows[g], wl.CW], BF16, kind="Internal").ap() for g in range(wl.NG)]
    xs = [nc.dram_tensor("xs%d" % i, [NSEQ, D, SPW], F32, kind="Internal").ap() for i in range(2)]
    DO = c.B_HEADS * 128
    osc = nc.dram_tensor("osc", [NSEQ, DO, S], BF16, kind="Internal").ap()
    wflat = [w_.rearrange("r c -> (r c)") for w_ in wbf]

    def xview(buf, seq, c0, w):
        return xs[buf][seq].rearrange("(k p) c -> p k c", p=128)[:, :, c0:c0 + w]

    uid = [0]

    def SB(stack, name, shape, dt):
        uid[0] += 1
        return stack.enter_context(nc.sbuf_tensor("%s_%d" % (name, uid[0]), shape, dt))

    with ExitStack() as top:
        sch = Sched(nc, top)
        cst = top.enter_context(nc.sbuf_tensor("cst", [128, cl.n], F32))
        cbf = top.enter_context(nc.sbuf_tensor("cbf_s", [128, NCB], BF16))
        ones = top.enter_context(nc.sbuf_tensor("ones", [128, 128], BF16))
        epsb = top.enter_context(nc.sbuf_tensor("epsb", [128, 1], F32))
        sinkexp = top.enter_context(nc.sbuf_tensor("sinkexp", [128, max(1, c.LA) * c.A_HEADS], F32))
        pp = [top.enter_context(nc.psum_tensor("pp%d" % i, [128, 2, 512], F32)) for i in range(4)]
        RA = cbf[:, 0:128]
        RB = cbf[0:64, 128:192]
        maskL = cbf[:, 192:192 + G * 128]
        maskR = cbf[:, 192 + G * 128:192 + 2 * G * 128]
        identb = cbf[:, 192 + 2 * G * 128:192 + 2 * G * 128 + 128]

        def bank(k):
            return pp[k // 2][:, k % 2, :]

        def cc(key, i=0, n=1):
            o = cl.off[key]
            return cst[:, o + i:o + i + n]

        sch.dma("sp", "cst", lambda e: e.dma_start(out=cst[:], in_=cst_d[:, :]), writes=["cst"])
        sch.dma("sp", "cbf", lambda e: e.dma_start(out=cbf[:], in_=cbf_d[:, :]), writes=["cbf"])
        sch.op("dve", lambda e: e.memset(ones[:], 1.0), writes=["ones"])
        sch.op("dve", lambda e: e.memset(epsb[:], EPS), writes=["epsb"])
        if c.LA > 0:
            o0 = cl.off[("sink", 0)]
            sch.op("act", lambda e: e.activation(out=sinkexp[:], in_=cst[:, o0:o0 + c.LA * c.A_HEADS], func=AF.Exp),
                   reads=["cst"], writes=["sinkexp"])

        class WPool:
            def __init__(self, st, name, n, kc, mt):
                self.name, self.n, self.i = name, n, 0
                self.tiles = [SB(st, "%s%d" % (name, i), [128, kc, mt], BF16) for i in range(n)]

            def load(self, key, t):
                g_, off, kc, mt = wl.tile(key, t)
                i = self.i % self.n
                self.i += 1
                tl = self.tiles[i]
                src = wflat[g_][off:off + 128 * kc * mt].rearrange("(p k m) -> p k m", p=128, k=kc)
                res = (self.name, i)
                sch.dma("sp", "%s%d" % (self.name, i), lambda e: e.dma_start(out=tl[:, 0:kc, 0:mt], in_=src), reads=[("wbf", g_)], writes=[res])
                return tl, res

        def rr(n):
            st = {"i": -1}

            def nxt():
                st["i"] = (st["i"] + 1) % n
                return st["i"]
            return nxt

        def norm_pre(xt, sq, rstd, W, gkey, hT, hres, stat_banks):
            nsub = 2
            w = W // nsub
            for sc in range(nsub):
                sb = stat_banks[sc % len(stat_banks)]
                for dc in range(NDC):
                    sqt = sq[dc % 2]
                    sch.op("act", lambda e, dc=dc, sqt=sqt, sc=sc: e.activation(out=sqt[:, 0:w], in_=xt[:, dc, sc * w:(sc + 1) * w], func=AF.Square),
                           reads=[("xt", sc)], writes=[("sq", dc % 2)])
                    sch.op("pe", lambda e, dc=dc, sqt=sqt, sb=sb: e.matmul(bank(sb)[:, 0:w], lhsT=ones[:], rhs=sqt[:, 0:w], start=(dc == 0), stop=(dc == NDC - 1)),
                           reads=[("sq", dc % 2), "ones"], writes=[("ps", sb)], signal=True)
                sch.op("act", lambda e, sb=sb: e.activation(out=rstd[:, 0:w], in_=bank(sb)[:, 0:w], func=AF.Sqrt, bias=epsb[:, 0:1], scale=1.0 / D),
                       reads=[("ps", sb), "epsb"], writes=["rstd"])
                sch.op("dve", lambda e: e.reciprocal(out=rstd[:, 0:w], in_=rstd[:, 0:w]),
                       reads=["rstd"], writes=["rstd"])
                for dc in range(NDC):
                    sch.op("dve", lambda e, dc=dc, sc=sc: e.scalar_tensor_tensor(out=hT[:, dc, sc * w:(sc + 1) * w], in0=xt[:, dc, sc * w:(sc + 1) * w], scalar=cc(gkey, dc),
                                                                                 in1=rstd[:, 0:w], op0=ALU.mult, op1=ALU.mult),
                           reads=[("xt", sc), "rstd", "cst"], writes=[hres])

        def norm_block(st_tiles, buf, seq, c0, W, gkey, hT, hres, stat_banks):
            xt, sq, rstd = st_tiles
            nsub = 2
            w = W // nsub
            assert w * nsub == W and w <= 384
            for sc in range(nsub):
                src = xview(buf, seq, c0 + sc * w, w)
                blocks = sorted(set([min(max((c0 + sc * w - PAD) // TB, 0), NB - 1), min(max((c0 + sc * w + w - 1 - PAD) // TB, 0), NB - 1)]))
                sch.dma("sp", "xt", lambda e, src=src: e.dma_start(out=xt[:, :, 0:w], in_=src),
                        reads=[("x", buf, seq, b) for b in blocks], writes=["xt"])
                sb = stat_banks[sc % len(stat_banks)]
                for dc in range(NDC):
                    sqt = sq[dc % 2]
                    sch.op("act", lambda e, dc=dc, sqt=sqt: e.activation(out=sqt[:, 0:w], in_=xt[:, dc, 0:w], func=AF.Square),
                           reads=["xt"], writes=[("sq", dc % 2)])
                    sch.op("pe", lambda e, dc=dc, sqt=sqt, sb=sb: e.matmul(bank(sb)[:, 0:w], lhsT=ones[:], rhs=sqt[:, 0:w], start=(dc == 0), stop=(dc == NDC - 1)),
                           reads=[("sq", dc % 2), "ones"], writes=[("ps", sb)], signal=True)
                sch.op("act", lambda e, sb=sb: e.activation(out=rstd[:, 0:w], in_=bank(sb)[:, 0:w], func=AF.Sqrt, bias=epsb[:, 0:1], scale=1.0 / D),
                       reads=[("ps", sb), "epsb"], writes=["rstd"])
                sch.op("dve", lambda e: e.reciprocal(out=rstd[:, 0:w], in_=rstd[:, 0:w]),
                       reads=["rstd"], writes=["rstd"])
                for dc in range(NDC):
                    sch.op("dve", lambda e, dc=dc, sc=sc: e.scalar_tensor_tensor(out=hT[:, dc, sc * w:(sc + 1) * w], in0=xt[:, dc, 0:w], scalar=cc(gkey, dc),
                                                                                 in1=rstd[:, 0:w], op0=ALU.mult, op1=ALU.mult),
                           reads=["xt", "rstd", "cst"], writes=[hres])

        def rope(ps_k, rot_k, w, R, npart, tab, tcol, qb, t1, t2, out_ap, out_res, rc):
            i = rc()
            qbt, t1t, t2t = qb[i], t1[i], t2[i]
            P = slice(0, npart)
            sch.op("act", lambda e: e.activation(out=qbt[P, 0:w], in_=bank(ps_k)[P, 0:w], func=AF.Copy),
                   reads=[("ps", ps_k)], writes=[("qb", i)])
            sch.op("pe", lambda e: e.matmul(bank(rot_k)[P, 0:w], lhsT=R, rhs=qbt[P, 0:w], start=True, stop=True),
                   reads=[("qb", i), "cbf"], writes=[("ps", rot_k)])
            sch.op("dve", lambda e: e.tensor_tensor(out=t1t[P, 0:w], in0=bank(ps_k)[P, 0:w], in1=tab[P, 0, tcol:tcol + w], op=ALU.mult),
                   reads=[("ps", ps_k), "tab"], writes=[("t1", i)])
            sch.op("dve", lambda e: e.tensor_tensor(out=t2t[P, 0:w], in0=bank(rot_k)[P, 0:w], in1=tab[P, 1, tcol:tcol + w], op=ALU.mult),
                   reads=[("ps", rot_k), "tab"], writes=[("t2", i)])
            sch.op("pool", lambda e: e.tensor_tensor(out=out_ap, in0=t1t[P, 0:w], in1=t2t[P, 0:w], op=ALU.add),
                   reads=[("t1", i), ("t2", i)], writes=[out_res])

        def out_proj(wp, wkey, nkc, act_tile, act_res, xin_buf, xout_buf, seq, blk, xres, xn, bankrr):
            c0 = PAD + blk * TB
            for m in range(NDC):
                wt, wres = wp.load(wkey, m)
                k = bankrr()
                i = m % 2
                srcx = xs[xin_buf][seq][m * 128:(m + 1) * 128, c0:c0 + TB]
                sch.dma("sp", "xres%d" % i, lambda e, i=i, srcx=srcx: e.dma_start(out=xres[i][:], in_=srcx),
                        reads=[("x", xin_buf, seq, blk)], writes=[("xres", i)])
                for kc in range(nkc):
                    sch.op("pe", lambda e, kc=kc, wt=wt, k=k: e.matmul(bank(k), lhsT=wt[:, kc, 0:128], rhs=act_tile[:, kc, :], start=(kc == 0), stop=(kc == nkc - 1)),
                           reads=[wres, act_res], writes=[("ps", k)], signal=(kc == nkc - 1))
                sch.op("dve", lambda e, i=i, k=k: e.tensor_tensor(out=xn[i][:], in0=bank(k), in1=xres[i][:], op=ALU.add),
                       reads=[("ps", k), ("xres", i)], writes=[("xn", i)])
                dst = xs[xout_buf][seq][m * 128:(m + 1) * 128, c0:c0 + TB]
                sch.dma("pool", "xn%d" % i, lambda e, i=i, dst=dst: e.dma_start(out=dst, in_=xn[i][:]),
                        reads=[("xn", i)], writes=[("x", xout_buf, seq, blk)])

        with ExitStack() as ph:
            xtok = SB(ph, "xtok", [128, 4, D], F32)
            xst = ph.enter_context(nc.sbuf_tensor("xst", [128, NDC, TB], F32))
            zt = ph.enter_context(nc.sbuf_tensor("zt", [128, NDC, PAD], F32))
            idt = ph.enter_context(nc.sbuf_tensor("idt", [128, 128], F32))
            sch.dma("sp", "idt", lambda e: e.dma_start(out=idt[:], in_=id_d[:, :]), writes=["idt"])
            def cast_group(g_):
                r0 = 0
                while r0 < wl.rows[g_]:
                    r1 = min(wl.rows[g_], r0 + 1024)
                    last = r1 >= wl.rows[g_]
                    sch.dma("pool", "cast%d" % g_, lambda e, r0=r0, r1=r1, g_=g_: e.dma_start(out=wbf[g_][r0:r1, :], in_=wall[g_][r0:r1, :]),
                            writes=([("wbf", g_)] if last else []))
                    r0 = r1
            cast_group(0)
            sch.op("dve", lambda e: e.memset(zt[:], 0.0), writes=["zt"])
            for b in range(2):
                for seq in range(NSEQ):
                    for side in range(2):
                        dst = xview(b, seq, 0 if side == 0 else PAD + S, PAD)
                        sch.dma("pool", "zpad", lambda e, dst=dst: e.dma_start(out=dst, in_=zt[:]), reads=["zt"])
            brr = rr(8)
            for seq in range(NSEQ):
                for blk in range(NB):
                    rbase = seq * S + blk * TB
                    src = x_in[rbase:rbase + TB, :].rearrange("(t p) d -> p t d", p=128)
                    sch.dma("sp", "xtok", lambda e, src=src: e.dma_start(out=xtok[:], in_=src), writes=["xtok"])
                    for dc in range(NDC):
                        k = brr()
                        for tt in range(4):
                            sch.op("pe", lambda e, k=k, tt=tt, dc=dc: e.transpose(bank(k)[:, tt * 128:(tt + 1) * 128], xtok[:, tt, dc * 128:(dc + 1) * 128], idt[:]),
                                   reads=["xtok", "idt"], writes=[("ps", k)], signal=(tt == 3))
                        if dc % 2 == 0:
                            sch.op("act", lambda e, k=k, dc=dc: e.activation(out=xst[:, dc, :], in_=bank(k), func=AF.Copy),
                                   reads=[("ps", k)], writes=[("xst", dc)])
                        else:
                            sch.op("dve", lambda e, k=k, dc=dc: e.tensor_copy(out=xst[:, dc, :], in_=bank(k)),
                                   reads=[("ps", k)], writes=[("xst", dc)])
                    dst = xview(0, seq, PAD + blk * TB, TB)
                    sch.dma("pool", "xst", lambda e, dst=dst: e.dma_start(out=dst, in_=xst[:]),
                            reads=[("xst", dc) for dc in range(NDC)], writes=[("x", 0, seq, blk)])
            for g_ in range(1, wl.NG):
                cast_group(g_)
            sch.barrier()

        def phase_A(l, ja, seq, bin_, bout):
            W = TB + 2 * PAD
            scale = 128.0 ** -0.5
            KV = c.A_KV
            GW = G * 128
            with ExitStack() as ph:
                def T(name, shape, dt):
                    return SB(ph, name, shape, dt)
                xt = T("a_xt", [128, NDC, W], F32)
                sq = [T("a_sq%d" % i, [128, W // 2], BF16) for i in range(2)]
                rstd = T("a_rstd", [128, W // 2], F32)
                hT = T("a_hT", [128, NDC, W], BF16)
                tab = T("a_tab", [128, 2, W], F32)
                qT = T("a_qT", [128, c.A_HEADS, TB], BF16)
                kT = T("a_kT", [128, KV, W], BF16)
                V = T("a_V", [128, W // 128, KV * 128], BF16)
                qb = [T("a_qb%d" % i, [128, 384], BF16) for i in range(4)]
                qb2 = [T("a_qc%d" % i, [128, 384], BF16) for i in range(4)]
                t1 = t2 = None
                PT = [T("a_PT%d" % i, [128, TB], BF16) for i in range(6)]
                den = [T("a_den%d" % i, [128, TB], F32) for i in range(2)]
                oT = T("a_oT", [128, c.A_HEADS, TB], BF16)
                xres = [T("a_xres%d" % i, [128, TB], F32) for i in range(2)]
                xn = [T("a_xn%d" % i, [128, TB], F32) for i in range(2)]
                wp = WPool(ph, "a_w", 3, NDC, 128)
                wv = T("a_wv", [128, NDC, KV * 128], BF16)
                vg, voff, vkc, vmt = wl.tile(("a_v", ja), 0)
                sch.dma("sp", "a_wv", lambda e: e.dma_start(out=wv[:], in_=wflat[vg][voff:voff + 128 * vkc * vmt].rearrange("(p k m) -> p k m", p=128, k=vkc)),
                        reads=[("wbf", vg)], writes=["wv"])
                brr = rr(8)
                rc = rr(4)
                ptr = rr(6)
                dr = rr(2)
                def load_x(blk):
                    for sc in range(2):
                        cs0 = blk * TB + sc * (W // 2)
                        src = xview(bin_, seq, cs0, W // 2)
                        blocks = sorted(set([min(max((cs0 - PAD) // TB, 0), NB - 1), min(max((cs0 + W // 2 - 1 - PAD) // TB, 0), NB - 1)]))
                        sch.dma("pool", "a_xt%d" % sc, lambda e, src=src, sc=sc: e.dma_start(out=xt[:, :, sc * (W // 2):(sc + 1) * (W // 2)], in_=src),
                                reads=[("x", bin_, seq, b_) for b_ in blocks], writes=[("xt", sc)])
                load_x(0)
                for blk in range(NB):
                    c0 = blk * TB
                    sch.dma("sp", "a_tab", lambda e, c0=c0: e.dma_start(out=tab[:], in_=tabA_d[:, :, c0:c0 + W]), writes=["tab"])
                    norm_pre(xt, sq, rstd, W, ("gmix", l), hT, "hT", [brr(), brr()])
                    if blk + 1 < NB:
                        load_x(blk + 1)
                    pend_rope = []
                    for t in range(0 if "A_noqk" not in DBG else 0, (c.A_HEADS + KV) if "A_noqk" not in DBG else 0):
                        wt, wres = wp.load(("a_qk", ja), t)
                        isq = t < c.A_HEADS
                        subs = [(PAD, TB)] if isq else [(0, W // 2), (W // 2, W // 2)]
                        for (s0, sw) in subs:
                            k = brr()
                            for kc in range(NDC):
                                sch.op("pe", lambda e, kc=kc, wt=wt, k=k, s0=s0, sw=sw: e.matmul(bank(k)[:, 0:sw], lhsT=wt[:, kc, :], rhs=hT[:, kc, s0:s0 + sw], start=(kc == 0), stop=(kc == NDC - 1)),
                                       reads=[wres, "hT"], writes=[("ps", k)], signal=(kc == NDC - 1))
                            for f in pend_rope:
                                f(brr())
                            pend_rope = []
                            if isq:
                                for hh in range(2):
                                    o_ap = qT[:, t, hh * 256:(hh + 1) * 256]
                                    pend_rope.append(rope_a(k, hh * 256, 256, RA, 128, tab, s0 + hh * 256, qb, qb2, o_ap, "qT", rc))
                            else:
                                o_ap = kT[:, t - c.A_HEADS, s0:s0 + sw]
                                pend_rope.append(rope_a(k, 0, sw, RA, 128, tab, s0, qb, qb2, o_ap, "kT", rc))
                    for tt in range(W // 128 if "A_nov" not in DBG else 0):
                        k = brr()
                        if tt == 1:
                            for f in pend_rope:
                                f(brr())
                            pend_rope = []
                        for kc in range(NDC):
                            sch.op("pe", lambda e, kc=kc, k=k, tt=tt: e.matmul(bank(k)[:, 0:KV * 128], lhsT=hT[:, kc, tt * 128:(tt + 1) * 128], rhs=wv[:, kc, :], start=(kc == 0), stop=(kc == NDC - 1)),
                                   reads=["hT", "wv"], writes=[("ps", k)], signal=(kc == NDC - 1))
                        sch.op("act", lambda e, k=k, tt=tt: e.activation(out=V[:, tt, :], in_=bank(k)[:, 0:KV * 128], func=AF.Copy),
                               reads=[("ps", k)], writes=["V"])
                    def att_scores(kh, n, blk=blk):
                        nbg = blk * 4 + n
                        js = [j for j in (0, 1, 2) if 0 <= nbg - 1 + j < S // 128]
                        pts = []
                        for j in js:
                            kt = n + j
                            k = brr()
                            sch.op("pe", lambda e, k=k, kt=kt: e.matmul(bank(k)[:, 0:GW].rearrange("p (g q) -> p g q", g=G), lhsT=kT[:, kh, kt * 128:(kt + 1) * 128],
                                                                     rhs=qT[:, kh * G:(kh + 1) * G, n * 128:(n + 1) * 128], start=True, stop=True),
                                   reads=["kT", "qT"], writes=[("ps", k)])
                            pi = ptr()
                            sch.op("act", lambda e, k=k, pi=pi: e.activation(out=PT[pi][:, 0:GW], in_=bank(k)[:, 0:GW], func=AF.Exp, scale=scale),
                                   reads=[("ps", k)], writes=[("PT", pi)])
                            if j != 1:
                                msk = maskL if j == 0 else maskR
                                sch.op("pool", lambda e, pi=pi, msk=msk: e.tensor_tensor(out=PT[pi][:, 0:GW], in0=PT[pi][:, 0:GW], in1=msk, op=ALU.mult),
                                       reads=[("PT", pi), "cbf"], writes=[("PT", pi)])
                            pts.append((pi, kt))
                        return (kh, n, pts)

                    def att_pv(st):
                        kh, n, pts = st
                        ko, kd = brr(), brr()
                        L = len(pts)
                        for idx, (pi, kt) in enumerate(pts):
                            sch.op("pe", lambda e, pi=pi, kt=kt, idx=idx: e.matmul(bank(ko)[:, 0:GW], lhsT=V[:, kt, kh * 128:(kh + 1) * 128], rhs=PT[pi][:, 0:GW], start=(idx == 0), stop=(idx == L - 1)),
                                   reads=["V", ("PT", pi)], writes=[("ps", ko)], signal=(idx == L - 1))
                        for idx, (pi, kt) in enumerate(pts):
                            sch.op("pe", lambda e, pi=pi, idx=idx: e.matmul(bank(kd)[:, 0:GW], lhsT=ones[:], rhs=PT[pi][:, 0:GW], start=(idx == 0), stop=(idx == L - 1)),
                                   reads=["ones", ("PT", pi)], writes=[("ps", kd)], signal=(idx == L - 1))
                        di = dr()
                        for g in range(G):
                            h = kh * G + g
                            sch.op("dve", lambda e, g=g, h=h: e.tensor_scalar(out=den[di][:, g * 128:(g + 1) * 128], in0=bank(kd)[:, g * 128:(g + 1) * 128],
                                                                         scalar1=sinkexp[:, ja * c.A_HEADS + h:ja * c.A_HEADS + h + 1], scalar2=None, op0=ALU.add),
                                   reads=[("ps", kd), "sinkexp"], writes=[("den", di)])
                        sch.op("dve", lambda e: e.reciprocal(out=den[di][:, 0:GW], in_=den[di][:, 0:GW]), reads=[("den", di)], writes=[("den", di)])
                        sch.op("dve", lambda e: e.tensor_tensor(out=oT[:, kh * G:(kh + 1) * G, n * 128:(n + 1) * 128], in0=bank(ko)[:, 0:GW].rearrange("p (g q) -> p g q", g=G),
                                                                in1=den[di][:, 0:GW].rearrange("p (g q) -> p g q", g=G), op=ALU.mult),
                               reads=[("ps", ko), ("den", di)], writes=["oT"])

                    prev = None
                    for kh in range(KV if "A_noattn" not in DBG else 0):
                        for n in range(4):
                            cur = att_scores(kh, n)
                            if prev is not None:
                                att_pv(prev)
                            prev = cur
                    if prev is not None:
                        att_pv(prev)
                    out_proj(wp, ("a_o", ja), c.A_HEADS, oT, "oT", bin_, bout, seq, blk, xres, xn, brr)
                sch.barrier()

        def rope_a(k, pc0, w, R, npart, tab, tcol, qb, qb2, out_ap, out_res, rc):
            i = rc()
            y1, y2 = qb[i], qb2[i]
            P = slice(0, npart)
            sch.op("dve", lambda e: e.tensor_tensor(out=y1[P, 0:w], in0=bank(k)[P, pc0:pc0 + w], in1=tab[P, 0, tcol:tcol + w], op=ALU.mult),
                   reads=[("ps", k), "tab"], writes=[("qb", i)])
            sch.op("dve", lambda e: e.tensor_tensor(out=y2[P, 0:w], in0=bank(k)[P, pc0:pc0 + w], in1=tab[P, 1, tcol:tcol + w], op=ALU.mult),
                   reads=[("ps", k), "tab"], writes=[("qb2", i)])

            def fin(k2):
                sch.op("pe", lambda e: e.matmul(bank(k2)[P, 0:w], lhsT=identb[P, P], rhs=y1[P, 0:w], start=True, stop=False),
                       reads=[("qb", i), "cbf"], writes=[("ps", k2)], signal=False)
                sch.op("pe", lambda e: e.matmul(bank(k2)[P, 0:w], lhsT=R, rhs=y2[P, 0:w], start=False, stop=True),
                       reads=[("qb2", i), "cbf"], writes=[("ps", k2)])
                sch.op("act", lambda e: e.activation(out=out_ap, in_=bank(k2)[P, 0:w], func=AF.Copy),
                       reads=[("ps", k2)], writes=[out_res])
            return fin

        def rope_cols(k, k2, pc0, w, R, npart, tab, tcol, qb, t1, t2, out_ap, out_res, rc, qb2=None):
            if "oldrope" not in DBG and "norope" not in DBG:
                fin = rope_a(k, pc0, w, R, npart, tab, tcol, qb, qb2, out_ap, out_res, rc)
                fin(k2)
                return
            if "norope" in DBG:
                sch.op("act", lambda e: e.activation(out=out_ap, in_=bank(k)[0:npart, pc0:pc0 + w], func=AF.Copy),
                       reads=[("ps", k)], writes=[out_res])
                return
            i = rc()
            qbt, t1t, t2t = qb[i], t1[i], t2[i]
            P = slice(0, npart)
            sch.op("act", lambda e: e.activation(out=qbt[P, 0:w], in_=bank(k)[P, pc0:pc0 + w], func=AF.Copy),
                   reads=[("ps", k)], writes=[("qb", i)])
            Ruse = R if "ropeones" not in DBG else ones[0:npart, 0:npart]
            if "rope_nomm" in DBG:
                k2 = k
            else:
                sch.op("pe", lambda e: e.matmul(bank(k2)[P, 0:w], lhsT=Ruse, rhs=qbt[P, 0:w], start=True, stop=True),
                       reads=[("qb", i), "cbf"], writes=[("ps", k2)])
            if "rope_nodve" in DBG:
                sch.op("act", lambda e: e.activation(out=out_ap, in_=bank(k2)[P, 0:w], func=AF.Copy),
                       reads=[("ps", k2)], writes=[out_res])
                return
            if "rope_nomul" in DBG:
                sch.op("dve", lambda e: e.memset(t1t[P, 0:w], 1.0), reads=[], writes=[("t1", i)])
                sch.op("dve", lambda e: e.memset(t2t[P, 0:w], 1.0), reads=[], writes=[("t2", i)])
                sch.op("pool", lambda e: e.tensor_tensor(out=out_ap, in0=t1t[P, 0:w], in1=t2t[P, 0:w], op=ALU.add),
                       reads=[("t1", i), ("t2", i), ("ps", k2)], writes=[out_res])
                return
            sch.op("dve", lambda e: e.tensor_tensor(out=t1t[P, 0:w], in0=bank(k)[P, pc0:pc0 + w], in1=tab[P, 0, tcol:tcol + w], op=ALU.mult),
                   reads=[("ps", k), "tab", ("qb", i)], writes=[("t1", i)])
            sch.op("dve", lambda e: e.tensor_tensor(out=t2t[P, 0:w], in0=bank(k2)[P, 0:w], in1=tab[P, 1, tcol:tcol + w], op=ALU.mult),
                   reads=[("ps", k2), "tab"], writes=[("t2", i)])
            if "rope_noadd" in DBG:
                sch.op("act", lambda e: e.activation(out=out_ap, in_=t1t[P, 0:w], func=AF.Copy),
                       reads=[("t1", i), ("t2", i)], writes=[out_res])
                return
            sch.op("pool" if "ropedve" not in DBG else "dve", lambda e: e.tensor_tensor(out=out_ap, in0=t1t[P, 0:w], in1=t2t[P, 0:w], op=ALU.add),
                   reads=[("t1", i), ("t2", i)], writes=[out_res])

        def phase_F(l, seq, bin_, bout):
            W = TB + 2
            NFC = c.NFC
            with ExitStack() as ph:
                def T(name, shape, dt):
                    return SB(ph, name, shape, dt)
                xt = T("f_xt", [128, NDC, W // 2], F32)
                sq = [T("f_sq%d" % i, [128, W // 2], BF16) for i in range(2)]
                rstd = T("f_rstd", [128, W // 2], F32)
                hT = T("f_hT", [128, NDC, W], BF16)
                aT = T("f_aT", [128, NFC, TB], BF16)
                tmp = [T("f_tmp%d" % i, [128, 2, 256], F32) for i in range(3)]
                sg = [T("f_sg%d" % i, [128, 2, 256], F32) for i in range(2)]
                xres = [T("f_xres%d" % i, [128, TB], F32) for i in range(2)]
                xn = [T("f_xn%d" % i, [128, TB], F32) for i in range(2)]
                wpi = WPool(ph, "f_wi", 4, NDC, 128)
                wpo = WPool(ph, "f_wo", 2, NFC, 128)
                prr = rr(3)
                trr = rr(3)
                srr = rr(2)
                orr_state = rr(2)

                def obank():
                    return 6 + orr_state()
                for blk in range(NB):
                    c0 = PAD + blk * TB - 1
                    norm_block((xt, sq, rstd), bin_, seq, c0, W, ("gffn", l), hT, "hT", [6, 7])
                    for i in range(NFC):
                        sgi = None
                        for which in range(2):
                            t = 2 * i + which
                            wt, wres = wpi.load(("f_in", l), t)
                            pi = prr()
                            for sc in range(2):
                                for kc in range(NDC):
                                    sch.op("pe", lambda e, kc=kc, wt=wt, pi=pi, sc=sc: e.matmul(pp[pi][:, sc, 0:258], lhsT=wt[:, kc, :], rhs=hT[:, kc, sc * 256:sc * 256 + 258], start=(kc == 0), stop=(kc == NDC - 1)),
                                           reads=[wres, "hT"], writes=[("pp", pi)], signal=(kc == NDC - 1 and sc == 1))
                            ti = trr()
                            w0, w1, w2, cb = cc(("cw", l, 0), t), cc(("cw", l, 1), t), cc(("cw", l, 2), t), cc(("cb", l), t)
                            sch.op("act", lambda e, ti=ti, pi=pi, w1=w1, cb=cb: e.activation(out=tmp[ti][:], in_=pp[pi][:, :, 1:257], func=AF.Identity, bias=cb, scale=w1),
                                   reads=[("pp", pi), "cst"], writes=[("tmp", ti)])
                            sch.op("dve", lambda e, ti=ti, pi=pi, w0=w0: e.scalar_tensor_tensor(out=tmp[ti][:], in0=pp[pi][:, :, 0:256], scalar=w0, in1=tmp[ti][:], op0=ALU.mult, op1=ALU.add),
                                   reads=[("pp", pi), ("tmp", ti), "cst"], writes=[("tmp", ti)])
                            sch.op("dve", lambda e, ti=ti, pi=pi, w2=w2: e.scalar_tensor_tensor(out=tmp[ti][:], in0=pp[pi][:, :, 2:258], scalar=w2, in1=tmp[ti][:], op0=ALU.mult, op1=ALU.add),
                                   reads=[("pp", pi), ("tmp", ti), "cst"], writes=[("tmp", ti)])
                            if which == 0:
                                sgi = srr()
                                sch.op("act", lambda e, ti=ti, sgi=sgi: e.activation(out=sg[sgi][:], in_=tmp[ti][:], func=AF.Silu),
                                       reads=[("tmp", ti)], writes=[("sg", sgi)])
                            else:
                                sch.op("pool", lambda e, ti=ti, sgi=sgi, i=i: e.tensor_tensor(out=aT[:, i, :].rearrange("p (s q) -> p s q", s=2), in0=tmp[ti][:], in1=sg[sgi][:], op=ALU.mult),
                                       reads=[("tmp", ti), ("sg", sgi)], writes=["aT"])
                    c1 = PAD + blk * TB
                    for m in range(NDC):
                        wt, wres = wpo.load(("f_out", l), m)
                        k = obank()
                        ii = m % 2
                        srcx = xs[bin_][seq][m * 128:(m + 1) * 128, c1:c1 + TB]
                        sch.dma("sp", "f_xres%d" % ii, lambda e, ii=ii, srcx=srcx: e.dma_start(out=xres[ii][:], in_=srcx),
                                reads=[("x", bin_, seq, blk)], writes=[("xres", ii)])
                        for kc in range(NFC):
                            sch.op("pe", lambda e, kc=kc, wt=wt, k=k: e.matmul(bank(k), lhsT=wt[:, kc, :], rhs=aT[:, kc, :], start=(kc == 0), stop=(kc == NFC - 1)),
                                   reads=[wres, "aT"], writes=[("ps", k)], signal=(kc == NFC - 1))
                        sch.op("dve", lambda e, ii=ii, k=k: e.tensor_tensor(out=xn[ii][:], in0=bank(k), in1=xres[ii][:], op=ALU.add),
                               reads=[("ps", k), ("xres", ii)], writes=[("xn", ii)])
                        dst = xs[bout][seq][m * 128:(m + 1) * 128, c1:c1 + TB]
                        sch.dma("pool", "f_xn%d" % ii, lambda e, ii=ii, dst=dst: e.dma_start(out=dst, in_=xn[ii][:]),
                                reads=[("xn", ii)], writes=[("x", bout, seq, blk)])
                sch.barrier()

        def phase_B(l, jb, seq, bin_, bout):
            scale = 192.0 ** -0.5
            KQ, KK = c.QL // 128, c.KVL // 128
            NH = c.B_HEADS
            NT = S // 128
            NQC = S // TB
            with ExitStack() as outer:
                def TO(name, shape, dt):
                    return SB(outer, name, shape, dt)
                cqn = TO("b_cqn", [128, KQ, S], BF16)
                ckvn = TO("b_ckvn", [128, KK, S], BF16)
                krT = TO("b_krT", [128, S], BF16)
                tab = TO("b_tab", [64, 2, S], F32)
                qb = [TO("b_qb%d" % i, [128, 256], BF16) for i in range(2)]
                qb2 = [TO("b_qc%d" % i, [128, 256], BF16) for i in range(2)]
                t1 = [TO("b_t1%d" % i, [128, 256], F32) for i in range(2)]
                t2 = [TO("b_t2%d" % i, [128, 256], F32) for i in range(2)]
                sch.dma("sp", "b_tab", lambda e: e.dma_start(out=tab[:], in_=tabB_d[:, :, :]), writes=["tab"])
                sch.op("dve", lambda e: e.memset(krT[64:128, :], 0.0), writes=["krT_hi"])
                rc = rr(2)
                with ExitStack() as ph:
                    def T(name, shape, dt):
                        return SB(ph, name, shape, dt)
                    xt = T("b1_xt", [128, NDC, TB // 2], F32)
                    sq = [T("b1_sq%d" % i, [128, TB], BF16) for i in range(2)]
                    rstd = T("b1_rstd", [128, TB], F32)
                    hT = T("b1_hT", [128, NDC, TB], BF16)
                    cf = T("b1_cf", [128, max(KQ, KK), TB], F32)
                    wp = WPool(ph, "b1_w", 3, NDC, 128)
                    brr = rr(6)
                    for blk in range(NB):
                        c0 = PAD + blk * TB
                        norm_block((xt, sq, rstd), bin_, seq, c0, TB, ("gmix", l), hT, "hT", [6, 7])
                        for grp, (nk, dst, gk) in enumerate(((KQ, cqn, ("bqn", jb)), (KK, ckvn, ("bkvn", jb)))):
                            for j in range(nk):
                                t = grp * KQ + j
                                wt, wres = wp.load(("b_in", jb), t)
                                k = brr()
                                for kc in range(NDC):
                                    sch.op("pe", lambda e, kc=kc, wt=wt, k=k: e.matmul(bank(k), lhsT=wt[:, kc, :], rhs=hT[:, kc, :], start=(kc == 0), stop=(kc == NDC - 1)),
                                           reads=[wres, "hT"], writes=[("ps", k)], signal=(kc == NDC - 1))
                                sch.op("act", lambda e, k=k, j=j: e.activation(out=cf[:, j, :], in_=bank(k), func=AF.Copy),
                                       reads=[("ps", k)], writes=[("cf", j)])
                                sqt = sq[j % 2]
                                sch.op("act", lambda e, k=k, sqt=sqt: e.activation(out=sqt[:], in_=bank(k), func=AF.Square),
                                       reads=[("ps", k)], writes=[("sq", j % 2)])
                                sch.op("pe", lambda e, j=j, sqt=sqt, nk=nk: e.matmul(bank(7), lhsT=ones[:], rhs=sqt[:], start=(j == 0), stop=(j == nk - 1)),
                                       reads=[("sq", j % 2), "ones"], writes=[("ps", 7)], signal=True)
                            sch.op("act", lambda e, nk=nk: e.activation(out=rstd[:], in_=bank(7), func=AF.Sqrt, bias=epsb[:, 0:1], scale=1.0 / (nk * 128)),
                                   reads=[("ps", 7), "epsb"], writes=["rstd"])
                            sch.op("dve", lambda e: e.reciprocal(out=rstd[:], in_=rstd[:]),
                                   reads=["rstd"], writes=["rstd"])
                            for j in range(nk):
                                sch.op("dve", lambda e, j=j, dst=dst, gk=gk, blk=blk: e.scalar_tensor_tensor(out=dst[:, j, blk * TB:(blk + 1) * TB], in0=cf[:, j, :], scalar=cc(gk, j), in1=rstd[:],
                                                                                                  op0=ALU.mult, op1=ALU.mult),
                                       reads=[("cf", j), "rstd", "cst"], writes=[("lat", grp)])
                        wt, wres = wp.load(("b_inr", jb), 0)
                        k, k2 = brr(), brr()
                        for kc in range(NDC):
                            sch.op("pe", lambda e, kc=kc, wt=wt, k=k: e.matmul(bank(k)[0:64, :], lhsT=wt[:, kc, 0:64], rhs=hT[:, kc, :], start=(kc == 0), stop=(kc == NDC - 1)),
                                   reads=[wres, "hT"], writes=[("ps", k)], signal=(kc == NDC - 1))
                        for hh in range(2):
                            rope_cols(k, k2, hh * 256, 256, RB, 64, tab, blk * TB + hh * 256, qb, t1, t2, krT[0:64, blk * TB + hh * 256:blk * TB + (hh + 1) * 256], "krT", rc, qb2)
                    sch.barrier()
                with ExitStack() as ph:
                    def T(name, shape, dt):
                        return SB(ph, name, shape, dt)
                    qn = T("b2_qn", [128, S], BF16)
                    qr = T("b2_qr", [128, S], BF16)
                    sch.op("dve", lambda e: e.memset(qr[64:128, :], 0.0), writes=["qr"])
                    kn = T("b2_kn", [128, S], BF16)
                    Vh = T("b2_V", [128, NT, 128], BF16)
                    PT = [T("b2_PT%d" % i, [128, TB], BF16) for i in range(6)]
                    rden = [T("b2_rd%d" % i, [128, TB], F32) for i in range(2)]
                    acc = [[T("b2_acc%d%d" % (i, j), [128, TB], F32) for j in range(2)] for i in range(2)]
                    hi = [T("b2_hi%d" % i, [128, TB], BF16) for i in range(2)]
                    lo = [T("b2_lo%d" % i, [128, TB], BF16) for i in range(2)]
                    oh = [T("b2_oh%d" % i, [128, S], BF16) for i in range(2)]
                    wq = WPool(ph, "b2_wqn", 2, KQ, 128)
                    wqr = WPool(ph, "b2_wqr", 2, KQ, 64)
                    wk = WPool(ph, "b2_wkn", 2, KK, 128)
                    wvp = WPool(ph, "b2_wv", 2, KK, 128)
                    srr = rr(4)
                    prr = rr(6)
                    arr = rr(2)
                    for h in range(NH):
                        wqt, wqres = wq.load(("b_qn", jb), h)
                        wqrt, wqrres = wqr.load(("b_qr", jb), h)
                        wkt, wkres = wk.load(("b_kn", jb), h)
                        wvt, wvres = wvp.load(("b_v", jb), h)
                        for qc in range(NQC):
                            cs = slice(qc * TB, (qc + 1) * TB)
                            k = srr()
                            for kc in range(KQ):
                                sch.op("pe", lambda e, kc=kc, k=k, cs=cs, wqt=wqt: e.matmul(bank(k), lhsT=wqt[:, kc, :], rhs=cqn[:, kc, cs], start=(kc == 0), stop=(kc == KQ - 1)),
                                       reads=[wqres, ("lat", 0)], writes=[("ps", k)], signal=(kc == KQ - 1))
                            sch.op("act", lambda e, k=k, cs=cs: e.activation(out=qn[:, cs], in_=bank(k), func=AF.Copy), reads=[("ps", k)], writes=["qn"])
                            k = srr()
                            for kc in range(KK):
                                sch.op("pe", lambda e, kc=kc, k=k, cs=cs, wkt=wkt: e.matmul(bank(k), lhsT=wkt[:, kc, :], rhs=ckvn[:, kc, cs], start=(kc == 0), stop=(kc == KK - 1)),
                                       reads=[wkres, ("lat", 1)], writes=[("ps", k)], signal=(kc == KK - 1))
                            sch.op("act", lambda e, k=k, cs=cs: e.activation(out=kn[:, cs], in_=bank(k), func=AF.Copy), reads=[("ps", k)], writes=["kn"])
                            k, k2 = srr(), srr()
                            for kc in range(KQ):
                                sch.op("pe", lambda e, kc=kc, k=k, cs=cs, wqrt=wqrt: e.matmul(bank(k)[0:64, :], lhsT=wqrt[:, kc, :], rhs=cqn[:, kc, cs], start=(kc == 0), stop=(kc == KQ - 1)),
                                       reads=[wqrres, ("lat", 0)], writes=[("ps", k)], signal=(kc == KQ - 1))
                            for hh in range(2):
                                rope_cols(k, k2, hh * 256, 256, RB, 64, tab, qc * TB + hh * 256, qb, t1, t2, qr[0:64, qc * TB + hh * 256:qc * TB + (hh + 1) * 256], "qr", rc, qb2)
                        for t4 in range(NT // 4):
                            k = srr()
                            for tt in range(4):
                                tk = t4 * 4 + tt
                                for kc in range(KK):
                                    sch.op("pe", lambda e, kc=kc, k=k, tt=tt, tk=tk, wvt=wvt: e.matmul(bank(k)[:, tt * 128:(tt + 1) * 128], lhsT=ckvn[:, kc, tk * 128:(tk + 1) * 128], rhs=wvt[:, kc, :],
                                                                                                 start=(kc == 0), stop=(kc == KK - 1)),
                                           reads=[wvres, ("lat", 1)], writes=[("ps", k)], signal=(kc == KK - 1 and tt == 3))
                            sch.op("act", lambda e, k=k, t4=t4: e.activation(out=Vh[:, t4 * 4:(t4 + 1) * 4, :], in_=bank(k).rearrange("p (t d) -> p t d", t=4), func=AF.Copy),
                                   reads=[("ps", k)], writes=["Vh"])
                        oi = h % 2
                        for qc in range(NQC):
                            cs = slice(qc * TB, (qc + 1) * TB)
                            a = arr()
                            ko, kd = 4 + 2 * a, 5 + 2 * a
                            pend = []

                            def scores(m, cs=cs):
                                k = srr()
                                sch.op("pe", lambda e, k=k, m=m: e.matmul(bank(k), lhsT=kn[:, m * 128:(m + 1) * 128], rhs=qn[:, cs], start=True, stop=False),
                                       reads=["kn", "qn"], writes=[("ps", k)], signal=False)
                                sch.op("pe", lambda e, k=k, m=m: e.matmul(bank(k), lhsT=krT[:, m * 128:(m + 1) * 128], rhs=qr[:, cs], start=False, stop=True),
                                       reads=["krT", "qr"], writes=[("ps", k)])
                                pi = prr()
                                sch.op("act", lambda e, k=k, pi=pi: e.activation(out=PT[pi][:], in_=bank(k), func=AF.Exp, scale=scale),
                                       reads=[("ps", k)], writes=[("PT", pi)])
                                return pi

                            di = qc % 2

                            def pv(m, pi, ko=ko, kd=kd, di=di):
                                sch.op("pe", lambda e: e.matmul(bank(ko), lhsT=Vh[:, m, :], rhs=PT[pi][:], start=(m == 0), stop=(m == NT - 1)),
                                       reads=["Vh", ("PT", pi)], writes=[("ps", ko)], signal=(m == NT - 1))
                                if m % 8 == 7:
                                    sch.op("pe", lambda e: e.matmul(bank(kd), lhsT=ones[:], rhs=PT[pi][:], start=(m == 7), stop=False),
                                           reads=["ones", ("PT", pi)], writes=[("ps", kd)], signal=True)
                                    return
                                eng = "pool" if m % 2 == 0 else "dve"
                                ac = acc[di][m % 2]
                                if m < 2:
                                    sch.op(eng, lambda e: e.tensor_copy(out=ac[:], in_=PT[pi][:]), reads=[("PT", pi)], writes=[("acc", di, m % 2)])
                                else:
                                    sch.op(eng, lambda e: e.tensor_tensor(out=ac[:], in0=ac[:], in1=PT[pi][:], op=ALU.add),
                                           reads=[("PT", pi), ("acc", di, m % 2)], writes=[("acc", di, m % 2)])
                            LOOK = 3
                            for m in range(NT + LOOK):
                                if m < NT:
                                    pend.append((m, scores(m)))
                                if m >= LOOK:
                                    mm, pi = pend.pop(0)
                                    pv(mm, pi)
                            a0, a1 = acc[di][0], acc[di][1]
                            sch.op("dve", lambda e, a0=a0, a1=a1: e.tensor_tensor(out=a0[:], in0=a0[:], in1=a1[:], op=ALU.add),
                                   reads=[("acc", di, 0), ("acc", di, 1)], writes=[("acc", di, 0)])
                            sch.op("dve", lambda e, a0=a0, di=di: e.tensor_copy(out=hi[di][:], in_=a0[:]), reads=[("acc", di, 0)], writes=[("hi", di)])
                            sch.op("dve", lambda e, a0=a0, di=di: e.tensor_tensor(out=lo[di][:], in0=a0[:], in1=hi[di][:], op=ALU.subtract),
                                   reads=[("acc", di, 0), ("hi", di)], writes=[("lo", di)])
                            sch.op("pe", lambda e, di=di, kd=kd: e.matmul(bank(kd), lhsT=ones[:], rhs=hi[di][:], start=(NT < 8), stop=False),
                                   reads=["ones", ("hi", di)], writes=[("ps", kd)], signal=False)
                            sch.op("pe", lambda e, di=di, kd=kd: e.matmul(bank(kd), lhsT=ones[:], rhs=lo[di][:], start=False, stop=True),
                                   reads=["ones", ("lo", di)], writes=[("ps", kd)])
                            sch.op("dve", lambda e, di=di, kd=kd: e.reciprocal(out=rden[di][:], in_=bank(kd)), reads=[("ps", kd)], writes=[("rden", di)])
                            sch.op("dve", lambda e, di=di, ko=ko, oi=oi, cs=cs: e.tensor_tensor(out=oh[oi][:, cs], in0=bank(ko), in1=rden[di][:], op=ALU.mult),
                                   reads=[("ps", ko), ("rden", di)], writes=[("oh", oi)])
                        dst = osc[seq][h * 128:(h + 1) * 128, :]
                        sch.dma("pool", "b2_oh%d" % oi, lambda e, oi=oi, dst=dst: e.dma_start(out=dst, in_=oh[oi][:]),
                                reads=[("oh", oi)], writes=[("osc", seq)])
                    sch.barrier()
            with ExitStack() as ph:
                def T(name, shape, dt):
                    return SB(ph, name, shape, dt)
                ot = [T("b3_ot%d" % i, [128, NH, TB], BF16) for i in range(2)]
                xres = [T("b3_xres%d" % i, [128, TB], F32) for i in range(2)]
                xn = [T("b3_xn%d" % i, [128, TB], F32) for i in range(2)]
                wp = WPool(ph, "b3_w", 3, NH, 128)
                brr = rr(8)
                for blk in range(NB):
                    oi = blk % 2
                    src = osc[seq].rearrange("(k p) c -> p k c", p=128)[:, :, blk * TB:(blk + 1) * TB]
                    sch.dma("sp", "b3_ot%d" % oi, lambda e, oi=oi, src=src: e.dma_start(out=ot[oi][:], in_=src),
                            reads=[("osc", seq)], writes=[("ot", oi)])
                    out_proj(wp, ("b_o", jb), NH, ot[oi], ("ot", oi), bin_, bout, seq, blk, xres, xn, brr)
                sch.barrier()

        def phase_Y(seq, bin_):
            with ExitStack() as ph:
                def T(name, shape, dt):
                    return SB(ph, name, shape, dt)
                xt = T("y_xt", [128, NDC, TB // 2], F32)
                sq = [T("y_sq%d" % i, [128, TB // 2], BF16) for i in range(2)]
                rstd = T("y_rstd", [128, TB // 2], F32)
                yT = T("y_yT", [128, NDC, TB], F32)
                ytok = [T("y_tok%d" % i, [128, D], F32) for i in range(2)]
                idt = T("y_idt", [128, 128], F32)
                sch.dma("sp", "y_idt", lambda e: e.dma_start(out=idt[:], in_=id_d[:, :]), writes=["idt"])
                brr = rr(6)
                for blk in range(NB):
                    norm_block((xt, sq, rstd), bin_, seq, PAD + blk * TB, TB, ("gfin", 0), yT, "yT", [6, 7])
                    for tt in range(4):
                        yi = tt % 2
                        for d4 in range(NDC // 4 if NDC >= 4 else 1):
                            nd = min(4, NDC)
                            k = brr()
                            for dd in range(nd):
                                dc = d4 * 4 + dd
                                sch.op("pe", lambda e, k=k, dd=dd, dc=dc, tt=tt: e.transpose(bank(k)[:, dd * 128:(dd + 1) * 128], yT[:, dc, tt * 128:(tt + 1) * 128], idt[:]),
                                       reads=["yT", "idt"], writes=[("ps", k)], signal=(dd == nd - 1))
                            eng = "act" if d4 % 2 == 0 else "dve"
                            if eng == "act":
                                sch.op("act", lambda e, k=k, d4=d4, yi=yi, nd=nd: e.activation(out=ytok[yi][:, d4 * 512:d4 * 512 + nd * 128], in_=bank(k)[:, 0:nd * 128], func=AF.Copy),
                                       reads=[("ps", k)], writes=[("ytok", yi, d4)])
                            else:
                                sch.op("dve", lambda e, k=k, d4=d4, yi=yi, nd=nd: e.tensor_copy(out=ytok[yi][:, d4 * 512:d4 * 512 + nd * 128], in_=bank(k)[:, 0:nd * 128]),
                                       reads=[("ps", k)], writes=[("ytok", yi, d4)])
                        r0 = seq * S + blk * TB + tt * 128
                        o = sch.dma("pool", "y_tok%d" % yi, lambda e, yi=yi, r0=r0: e.dma_start(out=y_out[r0:r0 + 128, :], in_=ytok[yi][:]),
                                    reads=[("ytok", yi, d4) for d4 in range(max(1, NDC // 4))], writes=[])
                        sch.out_tokens.append(o)
                sch.barrier()

        hl = 0
        for l in range(c.DEPTH):
            for seq in range(NSEQ):
                if l % 2 == 0:
                    if "noA" not in DBG:
                        phase_A(l, l // 2, seq, hl % 2, (hl + 1) % 2)
                else:
                    phase_B(l, l // 2, seq, hl % 2, (hl + 1) % 2)
            hl += 1
            for seq in range(NSEQ):
                if "noF" not in DBG:
                    phase_F(l, seq, hl % 2, (hl + 1) % 2)
            hl += 1
        for seq in range(NSEQ):
            phase_Y(seq, hl % 2)
        stuck = sch.simulate()
        assert not stuck, stuck
        print('streams', {e: len(o) for e, o in sch.streams.items()}, 'nsem', len(sch.dsem) + 4, flush=True)
        with nc.Block() as block:
            sch.emit(block)
    return nc


def _prep_common(cfg, inp):
    wl, cl = WLayout(cfg), CLayout(cfg)
    wall = pack_weights(cfg, wl, inp)
    consts = pack_consts(cfg, cl, inp)
    tabA, tabB, cbf, ident = make_tables(cfg)
    d = {"consts": consts, "tabA": tabA, "tabB": tabB, "cbf": cbf, "ident": ident}
    for g, w_ in enumerate(wall):
        d["wall%d" % g] = w_
    return d


def run(cfg, seqs, inp, trace=False):
    nc = build(cfg)
    common = _prep_common(cfg, inp)
    in_maps = []
    for s in seqs:
        m = dict(common)
        m["x_in"] = np.ascontiguousarray(s.reshape(cfg.NSEQ * cfg.S, cfg.D))
        in_maps.append(m)
    res = run_bass_kernel_spmd(nc, in_maps, core_ids=list(range(len(seqs))), trace=trace)
    outs = [np.asarray(r["y"]).reshape(cfg.NSEQ, cfg.S, cfg.D) for r in res.results]
    return outs, res


def kernel(x_prompt, x_sample, **w):
    cfg = Cfg()
    xp = np.asarray(x_prompt, np.float32)
    xsm = np.asarray(x_sample, np.float32)
    zero = np.zeros_like(xsm[0])
    seqs = [np.stack([xp[2 * i], xp[2 * i + 1]]) for i in range(4)] + [np.stack([xsm[i], zero]) for i in range(4)]
    outs, _ = run(cfg, seqs, w)
    y_prompt = np.concatenate([outs[i] for i in range(4)], axis=0).astype(np.float32)
    y_sample = np.stack([outs[4 + i][0] for i in range(4)], axis=0).astype(np.float32)
    return (y_prompt, y_sample)
```
